# Optimizing a Trainium2 kernel written in Bass

```python
import jax, jax.numpy as jnp
from jax import lax
import numpy as np

D_MODEL = 1024
BATCH = 4
SEQ = 4096
DEPTH = 1

D_RNN = D_MODEL
N_RNN_BLOCKS = 4
RNN_BLOCK = D_RNN // N_RNN_BLOCKS
CONV_WIDTH = 4
C_RG = 8.0
MIN_RAD, MAX_RAD = 0.9, 0.999
D_POOL = D_MODEL
POOL_WINDOWS = (2, 4, 8, 16)
N_POOL_GROUPS = len(POOL_WINDOWS)
POOL_GROUP = D_POOL // N_POOL_GROUPS
N_BRANCHES = 2
D_FF = 4 * D_MODEL
N_MOD = 6
D_IN = 2 * D_RNN + D_POOL + N_BRANCHES * D_MODEL
EPS = 1e-6

kernel_name = "hybrid_rglru_pool_gated_block"


def rmsnorm(x, g):
    xf = x.astype(jnp.float32)
    y = xf * lax.rsqrt(jnp.mean(xf * xf, axis=-1, keepdims=True) + EPS)
    return (y * g.astype(jnp.float32)).astype(x.dtype)


def block_diag(x, w):
    b, s, _ = x.shape
    g, dg, _ = w.shape
    y = jnp.einsum('bsgi,gij->bsgj', x.reshape(b, s, g, dg), w)
    return y.reshape(b, s, g * dg)


def causal_conv(x, w, bias):
    k = w.shape[0]
    s = x.shape[1]
    xp = jnp.pad(x, ((0, 0), (k - 1, 0), (0, 0)))
    y = bias
    for j in range(k):
        y = y + xp[:, j:j + s] * w[j]
    return y


def rg_lru(x, w_a, b_a, w_x, b_x, a_param):
    xf = x.astype(jnp.float32)
    r = jax.nn.sigmoid(block_diag(xf, w_a.astype(jnp.float32)) + b_a.astype(jnp.float32))
    i = jax.nn.sigmoid(block_diag(xf, w_x.astype(jnp.float32)) + b_x.astype(jnp.float32))
    log_a = -C_RG * r * jax.nn.softplus(a_param.astype(jnp.float32))
    a = jnp.exp(log_a)
    mult = jnp.sqrt(-jnp.expm1(2.0 * log_a))
    first = (jnp.arange(x.shape[1]) == 0)[None, :, None]
    mult = jnp.where(first, 1.0, mult)
    bval = xf * i * mult

    def combine(l, rr):
        a1, b1 = l
        a2, b2 = rr
        return a1 * a2, a2 * b1 + b2

    _, h = lax.associative_scan(combine, (a, bval), axis=1)
    return h.astype(x.dtype)


def multiscale_pool(u, w_pool, b_pool, pool_scale):
    uf = u.astype(jnp.float32)
    s = u.shape[1]
    pos = jnp.arange(s, dtype=jnp.float32)[None, :, None]
    outs = []
    for gi, win in enumerate(POOL_WINDOWS):
        seg = uf[..., gi * POOL_GROUP:(gi + 1) * POOL_GROUP]
        cs = jnp.cumsum(seg, axis=1)
        cs_prev = jnp.pad(cs, ((0, 0), (win, 0), (0, 0)))[:, :s]
        cnt = jnp.minimum(pos + 1.0, float(win))
        outs.append((cs - cs_prev) / cnt - seg)
    p = jnp.concatenate(outs, axis=-1)
    p = block_diag(p, w_pool.astype(jnp.float32)) + b_pool.astype(jnp.float32)
    return (p * pool_scale.astype(jnp.float32)).astype(u.dtype)


def setup_inputs(seed: int = 0) -> dict:
    key = jax.random.key(seed)
    ks = jax.random.split(key, 24)
    f32 = jnp.float32
    L = DEPTH

    def nrm(k, shape, fan_in):
        return jax.random.normal(k, shape, f32) * (fan_in ** -0.5)

    u = jax.random.uniform(ks[10], (L, D_RNN), f32)
    a_real = 0.5 * jnp.log(u * (MAX_RAD ** 2 - MIN_RAD ** 2) + MIN_RAD ** 2)
    a_param = jnp.log(jnp.expm1(-a_real))
    return {
        "x": jax.random.normal(ks[0], (BATCH, SEQ, D_MODEL), f32),
        "c": jax.random.normal(ks[1], (BATCH, D_MODEL), f32),
        "norm_mix_g": 1.0 + 0.05 * jax.random.normal(ks[2], (L, D_MODEL), f32),
        "norm_mlp_g": 1.0 + 0.05 * jax.random.normal(ks[3], (L, D_MODEL), f32),
        "w_ada": nrm(ks[4], (L, D_MODEL, N_MOD * D_MODEL), D_MODEL),
        "b_ada": 0.02 * jax.random.normal(ks[5], (L, N_MOD * D_MODEL), f32),
        "w_in": nrm(ks[6], (L, D_MODEL, D_IN), D_MODEL),
        "conv_w": nrm(ks[7], (L, CONV_WIDTH, D_RNN), CONV_WIDTH),
        "conv_b": 0.02 * jax.random.normal(ks[8], (L, D_RNN), f32),
        "w_rg_a": nrm(ks[9], (L, N_RNN_BLOCKS, RNN_BLOCK, RNN_BLOCK), RNN_BLOCK),
        "b_rg_a": 0.02 * jax.random.normal(ks[11], (L, D_RNN), f32),
        "w_rg_x": nrm(ks[12], (L, N_RNN_BLOCKS, RNN_BLOCK, RNN_BLOCK), RNN_BLOCK),
        "b_rg_x": 0.02 * jax.random.normal(ks[13], (L, D_RNN), f32),
        "a_param": a_param,
        "w_branch_a": nrm(ks[14], (L, D_RNN, D_MODEL), D_RNN),
        "w_pool": nrm(ks[15], (L, N_POOL_GROUPS, POOL_GROUP, POOL_GROUP), POOL_GROUP),
        "b_pool": 0.02 * jax.random.normal(ks[16], (L, D_POOL), f32),
        "pool_scale": 1.0 + 0.1 * jax.random.normal(ks[17], (L, D_POOL), f32),
        "w_branch_b": nrm(ks[18], (L, D_POOL, D_MODEL), D_POOL),
        "w_out": nrm(ks[19], (L, D_MODEL, D_MODEL), D_MODEL),
        "w_up": nrm(ks[20], (L, D_MODEL, D_FF), D_MODEL),
        "w_down": nrm(ks[21], (L, D_FF, D_MODEL), D_FF),
        "final_g": 1.0 + 0.05 * jax.random.normal(ks[22], (D_MODEL,), f32),
    }


def reference(x, c, norm_mix_g, norm_mlp_g, w_ada, b_ada, w_in, conv_w, conv_b,
              w_rg_a, b_rg_a, w_rg_x, b_rg_x, a_param, w_branch_a, w_pool, b_pool,
              pool_scale, w_branch_b, w_out, w_up, w_down, final_g):
    c_act = jax.nn.silu(c)
    for l in range(DEPTH):
        mod = c_act @ w_ada[l] + b_ada[l]
        sh1, sc1, gt1, sh2, sc2, gt2 = [m[:, None, :] for m in jnp.split(mod, N_MOD, axis=-1)]

        h = rmsnorm(x, norm_mix_g[l]) * (1.0 + sc1) + sh1
        proj = h @ w_in[l]
        x_rnn, y_rnn, u_pool, g_a, g_b = jnp.split(
            proj, np.cumsum([D_RNN, D_RNN, D_POOL, D_MODEL]).tolist(), axis=-1)

        xr = causal_conv(x_rnn, conv_w[l], conv_b[l])
        hr = rg_lru(xr, w_rg_a[l], b_rg_a[l], w_rg_x[l], b_rg_x[l], a_param[l])
        branch_a = (jax.nn.gelu(y_rnn) * hr) @ w_branch_a[l]

        pooled = multiscale_pool(u_pool, w_pool[l], b_pool[l], pool_scale[l])
        branch_b = pooled @ w_branch_b[l]

        merged = jax.nn.sigmoid(g_a) * branch_a + jax.nn.sigmoid(g_b) * branch_b
        x = x + gt1 * (merged @ w_out[l])

        h = rmsnorm(x, norm_mlp_g[l]) * (1.0 + sc2) + sh2
        ff = jnp.square(jax.nn.relu(h @ w_up[l]))
        x = x + gt2 * (ff @ w_down[l])
    return rmsnorm(x, final_g)
```

```python
import numpy as np
from contextlib import ExitStack
import concourse.bass as bass
import concourse.mybir as mybir
from concourse.bass_utils import run_bass_kernel_spmd

F32 = mybir.dt.float32
BF16 = mybir.dt.bfloat16
AF = mybir.ActivationFunctionType
ALU = mybir.AluOpType

ENGS = ("pe", "act", "dve", "pool", "sp")
P = 128
D = 1024
DFF = 4096
NTOK = 2048
TT = 512
EPS = 1e-6
GELU_NATIVE = True


class Sched:
    def __init__(self):
        self.q = {e: [] for e in ENGS}
        self.prog = {e: 0 for e in ENGS}
        self.waited = {e: {} for e in ENGS}
        self.bufs = {}
        self.dma_cnt = {}

    def _st(self, k):
        st = self.bufs.get(k)
        if st is None:
            st = {"w": None, "r": {}}
            self.bufs[k] = st
        return st

    def _collect(self, eng, reads, writes):
        deps = []
        for k in reads:
            st = self._st(k)
            d = st["w"]
            if d is not None and not (d[2] == eng and eng == "pe"):
                deps.append(d)
        for k in writes:
            st = self._st(k)
            if st["w"] is not None and not (st["w"][2] == eng and eng == "pe"):
                deps.append(st["w"])
            for d in st["r"].values():
                if not (d[2] == eng and eng == "pe"):
                    deps.append(d)
        w = self.waited[eng]
        best = {}
        for (sk, val, _src) in deps:
            if w.get(sk, 0) >= val:
                continue
            if best.get(sk, 0) < val:
                best[sk] = val
        waits = []
        for sk, val in best.items():
            w[sk] = val
            waits.append((sk, val))
        return waits

    def _commit(self, dep, reads, writes):
        for k in reads:
            self._st(k)["r"][dep[0]] = dep
        for k in writes:
            st = self._st(k)
            st["w"] = dep
            st["r"] = {}

    def op(self, eng, fn, reads=(), writes=()):
        waits = self._collect(eng, reads, writes)
        self.prog[eng] += 1
        dep = ("p_" + eng, self.prog[eng], eng)
        self.q[eng].append((waits, fn, ("p_" + eng, 1)))
        self._commit(dep, reads, writes)

    def dma(self, qeng, fn, sem, reads=(), writes=()):
        waits = self._collect(qeng, reads, writes)
        self.dma_cnt[sem] = self.dma_cnt.get(sem, 0) + 16
        dep = ("d_" + sem, self.dma_cnt[sem], None)
        self.q[qeng].append((waits, fn, ("d_" + sem, 16)))
        self._commit(dep, reads, writes)

    def barrier(self):
        targets = [("p_" + e, self.prog[e]) for e in ENGS if self.prog[e] > 0]
        targets += [("d_" + s, c) for s, c in self.dma_cnt.items()]
        for e in ENGS:
            w = self.waited[e]
            waits = []
            for sk, val in targets:
                if sk == "p_" + e:
                    continue
                if w.get(sk, 0) < val:
                    w[sk] = val
                    waits.append((sk, val))
            if waits:
                self.q[e].append((waits, None, None))

    def _plan(self):
        needed = {e: set() for e in ENGS}
        for e in ENGS:
            for waits, fn, inc in self.q[e]:
                for sk, val in waits:
                    if sk.startswith("p_"):
                        needed[sk[2:]].add(val)
        rank = {e: {k: i + 1 for i, k in enumerate(sorted(needed[e]))} for e in ENGS}
        plan = {}
        for e in ENGS:
            cnt = 0
            out = []
            for waits, fn, inc in self.q[e]:
                w2 = [(sk, rank[sk[2:]][val]) if sk.startswith("p_") else (sk, val) for sk, val in waits]
                inc2 = inc
                if inc is not None and inc[0].startswith("p_"):
                    cnt += 1
                    inc2 = inc if cnt in needed[e] else None
                out.append((w2, fn, inc2))
            plan[e] = out
        return plan

    @staticmethod
    def _check(plan):
        sem = {}
        ptr = {e: 0 for e in ENGS}
        total = sum(len(plan[e]) for e in ENGS)
        done = 0
        while done < total:
            progressed = False
            for e in ENGS:
                while ptr[e] < len(plan[e]):
                    waits, fn, inc = plan[e][ptr[e]]
                    if any(sem.get(sk, 0) < val for sk, val in waits):
                        break
                    if inc is not None:
                        sem[inc[0]] = sem.get(inc[0], 0) + inc[1]
                    ptr[e] += 1
                    done += 1
                    progressed = True
            if not progressed:
                raise RuntimeError(f"semaphore plan deadlocks at {ptr}")

    def emit(self, nc, stack):
        plan = self._plan()
        self._check(plan)
        sems = {}
        for e in ENGS:
            if self.prog[e] > 0:
                sems["p_" + e] = stack.enter_context(nc.semaphore("p_" + e))
        for s_ in self.dma_cnt:
            sems["d_" + s_] = stack.enter_context(nc.semaphore("d_" + s_))
        block = stack.enter_context(nc.Block())

        def mk(engname):
            def body(e):
                for waits, fn, inc in plan[engname]:
                    for sk, val in waits:
                        e.wait_ge(sems[sk], val)
                    if fn is not None:
                        ins = fn(e)
                        if inc is not None:
                            ins.then_inc(sems[inc[0]], inc[1])
            return body

        block.sync(mk("sp"))
        block.tensor(mk("pe"))
        block.scalar(mk("act"))
        block.vector(mk("dve"))
        block.gpsimd(mk("pool"))


(V_G1, V_G2, V_BSH1, V_BSC1, V_BSH2, V_BSC2, V_CW0, V_CW1, V_CW2, V_CW3, V_CB, V_BRA, V_BRX, V_AP, V_BP, V_PS, V_C, V_FLAG,
 V_BGT1, V_BGT2) = range(20)
NV = 20
(DV_SILU, DV_SH1, DV_SC1, DV_SH2, DV_SC2, DV_GM1, DV_GM2, DV_HBA, DV_HBX, DV_M4, DV_M8, DV_T0, DV_T1, DV_SP,
 DV_ACC) = range(15)
NDV = 20


def build_nc():
    nc = bass.Bass("TRN2", target_bir_lowering=False)

    def din(name, shape):
        return nc.dram_tensor(name, shape, F32, kind="ExternalInput").ap()

    xin = din("xin", [2 * NTOK, D])
    vecs_d = din("vecs", [P, NV * 8])
    rows_d = din("rows", [P, 2, D])
    invc_d = din("invc", [P, 4, 16])
    ident_d = din("ident", [P, P])
    w_in = din("w_in", [D, 5 * D])
    w_ra = din("w_ra", [D, 256])
    w_rx = din("w_rx", [D, 256])
    w_pl = din("w_pl", [D, 256])
    w_a = din("w_a", [D, D])
    w_b = din("w_b", [D, D])
    w_o = din("w_o", [D, D])
    w_up = din("w_up", [D, DFF])
    w_dn = din("w_dn", [DFF, D])
    w_adaT = din("w_adaT", [6 * D, D])
    yout = nc.dram_tensor("yout", [NTOK, D], F32, kind="ExternalOutput").ap()

    S = Sched()
    BASE = 16512
    RW, RHP, RX, RS = BASE, BASE + 65536, BASE + 131072, BASE + 196608
    cnt = [0]

    def at(name, shape, dt, off):
        cnt[0] += 1
        return nc.alloc_sbuf_tensor_at(f"{name}_{cnt[0]}", shape, dt, offset=off)

    with ExitStack() as st:
        ps = [st.enter_context(nc.psum_tensor(f"ps{i}", [P, 512], F32)) for i in range(8)]
        psb = [p.bitcast(BF16) for p in ps]
        bank_i = [0]

        def nb():
            b = bank_i[0]
            bank_i[0] = (b + 1) % 8
            return b

        vecs = at("vecs", [P, NV * 8], F32, RS + 0)
        dv = at("dv", [P, NDV * 8], F32, RS + 640)
        stat = at("stat", [P, 64], F32, RS + 1280)
        ident = at("ident", [P, P], BF16, RS + 1536)
        xh = at("xh", [P, 8, 4], F32, RS + 1792)
        uh = at("uh", [P, 8, 16], F32, RS + 1920)
        hst = at("hst", [P, 8], F32, RS + 2432)
        hTh = at("hTh", [P, 8, 16], BF16, RS + 2464)
        invc = at("invc", [P, 4, 16], F32, RS + 2720)
        misc = at("misc", [P, 16], F32, RS + 2976)
        gtb1 = at("gtb1", [P, D], F32, RS + 4096)
        gtb2 = at("gtb2", [P, D], F32, RS + 8192)
        identf = at("identf", [P, P], F32, RS + 12288)
        ones = at("ones", [P, P], F32, RS + 12800)
        dg = at("dg", [P, P], F32, RS + 13312)

        def V(i, c=None):
            return vecs[:, i * 8:(i + 1) * 8] if c is None else vecs[:, i * 8 + c:i * 8 + c + 1]

        def DV(i, c=None):
            return dv[:, i * 8:(i + 1) * 8] if c is None else dv[:, i * 8 + c:i * 8 + c + 1]

        FM = vecs[:, V_FLAG * 8:V_FLAG * 8 + 1]
        FN = vecs[:, V_FLAG * 8 + 1:V_FLAG * 8 + 2]

        slots = [at(f"slot{i}", [P, 8, D], BF16, RW + i * 16384) for i in range(4)]
        wa_g = at("wa_g", [P, 8, 256], BF16, RW + 2 * 16384)
        wi_g = at("wi_g", [P, 8, 256], BF16, RW + 2 * 16384 + 4096)
        wpl = at("wpl", [P, 8, 256], BF16, RW + 2 * 16384 + 8192)

        stA = [(gtb1, "gtb1"), (gtb2, "gtb2")]
        stA4 = [(at("stA4_0", [P, 4, 256], F32, RS + 4096), "gtb1"), (at("stA4_1", [P, 4, 256], F32, RS + 8192), "gtb2")]
        st_rr = [0]

        def load_cast(slot_t, key, dram2d, nrow_chunks=8, extra=(), stages=None, eng="act"):
            stages = stages or stA
            for kc in range(nrow_chunks):
                st_t, st_k = stages[st_rr[0] % len(stages)]
                st_rr[0] += 1
                S.dma("sp", lambda e, kc=kc, st_t=st_t: e.dma_start(out=st_t[:], in_=dram2d[kc * P:(kc + 1) * P, :]), "st_" + st_k, writes=[st_k])
                if eng == "act":
                    S.op("act", lambda e, kc=kc, st_t=st_t: e.activation(out=slot_t[:, kc, :], in_=st_t[:], func=AF.Identity),
                         reads=[st_k], writes=[key] + list(extra))
                else:
                    S.op("dve", lambda e, kc=kc, st_t=st_t: e.tensor_copy(out=slot_t[:, kc, :], in_=st_t[:]),
                         reads=[st_k], writes=[key] + list(extra))

        def load_small(w_t, key, dram2d):
            for h in range(2):
                st_t, st_k = stA4[st_rr[0] % 2]
                st_rr[0] += 1
                S.dma("sp", lambda e, h=h, st_t=st_t: e.dma_start(out=st_t[:], in_=dram2d[h * 512:(h + 1) * 512, :].rearrange("(s p) n -> p s n", p=P)),
                      "st_" + st_k, writes=[st_k])
                S.op("act", lambda e, h=h, st_t=st_t: e.activation(out=w_t[:, 4 * h:4 * h + 4, :], in_=st_t[:], func=AF.Identity),
                     reads=[st_k], writes=[key])

        def make_stream(slot_t, key, dram2d, stages, gt=None, gtk=None, extra=()):
            state = {"prev": None}

            def consume():
                kc, st_t, st_k = state["prev"]
                if gt is None:
                    S.op("act", lambda e: e.activation(out=slot_t[:, kc, :], in_=st_t[:], func=AF.Identity), reads=[st_k], writes=[key] + list(extra))
                else:
                    S.op("dve", lambda e: e.tensor_tensor(out=slot_t[:, kc, :], in0=st_t[:], in1=gt[:], op=ALU.mult), reads=[st_k, gtk], writes=[key] + list(extra))
                state["prev"] = None

            def mk(kc):
                def step():
                    st_t, st_k = stages[kc % len(stages)]
                    if state["prev"] is not None:
                        consume()
                    S.dma("sp", lambda e: e.dma_start(out=st_t[:], in_=dram2d[kc * P:(kc + 1) * P, :]), "st_" + st_k, writes=[st_k])
                    state["prev"] = (kc, st_t, st_k)
                return step
            return [mk(kc) for kc in range(8)] + [consume]

        pending = []

        def tick(n=1):
            for _ in range(n):
                if pending:
                    pending.pop(0)()

        def flush():
            while pending:
                pending.pop(0)()

        S.dma("sp", lambda e: e.dma_start(out=vecs[:], in_=vecs_d), "c_vecs", writes=["vecs"])
        S.dma("sp", lambda e: e.dma_start(out=identf[:], in_=ident_d), "c_ident", writes=["identf"])
        S.dma("sp", lambda e: e.dma_start(out=invc[:], in_=invc_d), "c_invc", writes=["invc"])

        wTs = [at("wTs0", [P, 4, D], F32, RHP), at("wTs1", [P, 4, D], F32, RHP + 16384),
               at("wTs2", [P, 4, D], F32, RHP + 49152), at("wTs3", [P, 4, D], F32, RW + 16384)]
        srow = at("srow", [P, D], F32, RHP + 32768)
        junkf = at("junkf", [P, D], F32, RHP + 36864)
        xt = at("xt", [P, 4, D], F32, RX + 0)

        def norm_load(tile_idx):
            r0 = tile_idx * TT
            S.dma("sp", lambda e: e.dma_start(out=xt[:], in_=xin[r0:r0 + TT, :].rearrange("(s p) d -> p s d", p=P)), "xt", writes=["xt"])

        norm_load(0)

        S.dma("sp", lambda e: e.dma_start(out=srow[:], in_=rows_d[:, 0, :]), "c_srow", writes=["srow"])

        def adaT_dma(blk, buf_t, bkey):
            S.dma("sp", lambda e: e.dma_start(
                out=buf_t[:], in_=w_adaT[blk * 512:(blk + 1) * 512, :].rearrange("(s p) k -> p s k", p=P)),
                bkey, writes=[bkey])

        def adaT_acc(blk, buf_t, bkey, srow_t, srow_k, junk_t, junk_k):
            for s in range(4):
                col = DV_ACC * 8 + blk * 4 + s
                S.op("dve", lambda e, s=s, col=col: e.scalar_tensor_tensor(
                    out=junk_t[:], in0=buf_t[:, s, :], scalar=1.0, in1=srow_t[:], op0=ALU.mult, op1=ALU.mult,
                    accum_out=dv[:, col:col + 1]), reads=[bkey, srow_k], writes=[junk_k, f"dvacc{blk}"])

        def silu_row(srow_t, srow_k, junk_t, junk_k):
            jv = junk_t if isinstance(junk_t, bass.AP) else junk_t[:]
            S.op("act", lambda e: e.activation(out=jv, in_=srow_t[:], func=AF.Tanh, scale=0.5), reads=[srow_k], writes=[junk_k])
            S.op("dve", lambda e: e.scalar_tensor_tensor(out=jv, in0=jv, scalar=1.0, in1=srow_t[:], op0=ALU.add, op1=ALU.mult),
                 reads=[junk_k, srow_k], writes=[junk_k])
            S.op("dve", lambda e: e.tensor_scalar(out=srow_t[:], in0=jv, scalar1=0.5, scalar2=None, op0=ALU.mult),
                 reads=[junk_k], writes=[srow_k])

        for blk in range(4):
            adaT_dma(blk, wTs[blk], f"wTs{blk}")
        gst = [at("gstA", [P, 8, 256], F32, RW + 3 * 16384), at("gstB", [P, 8, 256], F32, RW + 3 * 16384 + 8192)]
        for gi_, (gsrc_, gk_) in enumerate(((w_ra, "gstA"), (w_rx, "gstB"))):
            S.dma("sp", lambda e, gi_=gi_, gsrc_=gsrc_: e.dma_start(out=gst[gi_][:], in_=gsrc_.rearrange("(s p) n -> p s n", p=P)), gk_, writes=[gk_])

        S.op("dve", lambda e: e.tensor_copy(out=ident[:], in_=identf[:]), reads=["identf"], writes=["ident"])
        S.op("dve", lambda e: e.memset(ones[:], 1.0), writes=["ones"])
        S.op("dve", lambda e: e.memset(misc[:, 0:1], EPS), writes=["misc"])
        S.op("dve", lambda e: e.memset(xh[:], 0.0), writes=[f"xh{c}" for c in range(8)])
        S.op("dve", lambda e: e.memset(hst[:], 0.0), writes=[f"hst{c}" for c in range(8)])
        silu_row(srow, "srow", junkf, "junkf")
        for blk in range(4):
            adaT_acc(blk, wTs[blk], f"wTs{blk}", srow, "srow", junkf, "junkf")
        S.op("dve", lambda e: e.tensor_tensor(out=DV(DV_SH1), in0=dv[:, DV_ACC * 8:DV_ACC * 8 + 8], in1=V(V_BSH1), op=ALU.add),
             reads=["dvacc0", "dvacc1", "vecs"], writes=["dvmod0"])
        S.op("dve", lambda e: e.tensor_tensor(out=DV(DV_SC1), in0=dv[:, DV_ACC * 8 + 8:DV_ACC * 8 + 16], in1=V(V_BSC1), op=ALU.add),
             reads=["dvacc2", "dvacc3", "vecs"], writes=["dvmod1"])
        S.op("dve", lambda e: e.scalar_tensor_tensor(out=DV(DV_GM1), in0=DV(DV_SC1), scalar=1.0, in1=V(V_G1), op0=ALU.add, op1=ALU.mult),
             reads=["dvmod1", "vecs"], writes=["dvgm1"])
        S.op("dve", lambda e: e.tensor_scalar(out=DV(DV_HBA), in0=V(V_BRA), scalar1=0.5, scalar2=None, op0=ALU.mult), reads=["vecs"], writes=["dvhba"])
        S.op("dve", lambda e: e.tensor_scalar(out=DV(DV_HBX), in0=V(V_BRX), scalar1=0.5, scalar2=None, op0=ALU.mult), reads=["vecs"], writes=["dvhbx"])
        S.op("act", lambda e: e.activation(out=DV(DV_T0), in_=V(V_AP), func=AF.Abs), reads=["vecs"], writes=["dvt0"])
        S.op("act", lambda e: e.activation(out=DV(DV_T1), in_=DV(DV_T0), func=AF.Exp, scale=-1.0), reads=["dvt0"], writes=["dvt1"])
        S.op("act", lambda e: e.activation(out=DV(DV_T0), in_=DV(DV_T1), func=AF.Ln, bias=1.0, scale=1.0), reads=["dvt1"], writes=["dvt0"])
        S.op("dve", lambda e: e.scalar_tensor_tensor(out=DV(DV_SP), in0=V(V_AP), scalar=0.0, in1=DV(DV_T0), op0=ALU.max, op1=ALU.add),
             reads=["vecs", "dvt0"], writes=["dvsp"])
        S.op("dve", lambda e: e.tensor_scalar(out=DV(DV_M4), in0=DV(DV_SP), scalar1=-4.0, scalar2=None, op0=ALU.mult), reads=["dvsp"], writes=["dvm4"])
        S.op("dve", lambda e: e.tensor_scalar(out=DV(DV_M8), in0=DV(DV_SP), scalar1=-8.0, scalar2=None, op0=ALU.mult), reads=["dvsp"], writes=["dvm8"])

        hT_all = at("hT_all", [P, 8, NTOK], BF16, RHP)
        prodA = at("prodA", [P, 8, NTOK], BF16, RHP + 32768)
        hT_st = [at(f"hT_st{i}", [P, 8, TT], BF16, RHP + 32768 + i * 8192) for i in range(2)]
        xn = at("xn", [P, 4, D], BF16, RX + 16384)
        xr = at("xr", [P, 4, TT], F32, RX + 24576)
        thr = at("thr", [P, 4, TT], F32, RX + 32768)
        thi = at("thi", [P, 4, TT], F32, RX + 40960)
        a2 = at("a2", [P, 4, TT], F32, RX + 49152)
        t1 = at("t1", [P, 4, TT], F32, RX + 57344)
        xrb = at("xrb", [P, 4, TT], BF16, RW + 2 * 16384 + 12288)

        def norm_part1():
            for s in range(4):
                S.op("act", lambda e, s=s: e.activation(out=xn[:, s, :], in_=xt[:, s, :], func=AF.Square, accum_out=stat[:, s:s + 1]),
                     reads=["xt"], writes=[f"xn{s}", f"ss{s}"])
            S.op("act", lambda e: e.activation(out=stat[:, 4:8], in_=stat[:, 0:4], func=AF.Sqrt, scale=1.0 / D, bias=misc[:, 0:1]),
                 reads=["ss0", "ss1", "ss2", "ss3", "misc"], writes=["std"])
            S.op("dve", lambda e: e.reciprocal(out=stat[:, 8:12], in_=stat[:, 4:8]), reads=["std"], writes=["rstd"])
            for s in range(4):
                S.op("act", lambda e, s=s: e.activation(out=xn[:, s, :], in_=xt[:, s, :], func=AF.Identity, scale=stat[:, 8 + s:9 + s]),
                     reads=["xt", "rstd"], writes=[f"xn{s}"])

        def norm_part2(dst_fn, dst_keys):
            for c in range(8):
                b = nb()

                def tr(e, c=c, b=b):
                    ins = None
                    for s in range(4):
                        ins = e.transpose(out=psb[b][:, s * P:(s + 1) * P], in_=xn[:, s, c * P:(c + 1) * P], identity=ident[:])
                    return ins
                S.op("pe", tr, reads=["xn0", "xn1", "xn2", "xn3", "ident"], writes=[f"ps{b}"])
                S.op("act", lambda e, c=c, b=b: e.activation(out=dst_fn(c), in_=psb[b][:, 0:TT], func=AF.Identity, scale=DV(DV_GM1, c), bias=DV(DV_SH1, c)),
                     reads=[f"ps{b}", "dvgm1", "dvmod0"], writes=[dst_keys[c]])

        def hT_view(T):
            if T < 4:
                t_ = hT_st[T % 2]
                return (lambda c: t_[:, c, :]), [f"hTst{T % 2}_{c}" for c in range(8)]
            m0 = (T - 4) * TT
            return (lambda c: hT_all[:, c, m0:m0 + TT]), [f"hT{T - 4}_{c}" for c in range(8)]

        FN2 = vecs[:, V_FLAG * 8 + 2:V_FLAG * 8 + 3]

        def s12(T, blk, hview, hkeys, last=False):
            for jj_ in range(2):
                i = 2 * (blk % 2) + jj_
                c = 2 * blk + jj_
                b = nb()

                def mmx(e, c=c, b=b):
                    ins = None
                    for k in range(8):
                        ins = e.matmul(ps[b][:, :], lhsT=slots[0][:, k, c * P:(c + 1) * P], rhs=hview(k), start=(k == 0), stop=(k == 7))
                    return ins
                S.op("pe", mmx, reads=["S0"] + hkeys, writes=[f"ps{b}"])
                S.op("act", lambda e, i=i, b=b, c=c: e.activation(out=xr[:, i, :], in_=ps[b][:, :], func=AF.Identity, scale=V(V_CW3, c), bias=V(V_CB, c)),
                     reads=[f"ps{b}", "vecs"], writes=[f"xr{i}"])
                for k in (2, 1, 0):
                    sh = 3 - k
                    S.op("dve", lambda e, i=i, c=c, k=k, sh=sh, b=b: e.scalar_tensor_tensor(
                        out=xr[:, i, sh:TT], in0=ps[b][:, 0:TT - sh], scalar=V(V_CW0 + k, c), in1=xr[:, i, sh:TT], op0=ALU.mult, op1=ALU.add),
                        reads=[f"ps{b}", f"xr{i}", "vecs"], writes=[f"xr{i}"])
                    S.op("dve", lambda e, i=i, c=c, k=k, sh=sh: e.scalar_tensor_tensor(
                        out=xr[:, i, 0:sh], in0=xh[:, c, 3 - sh:3], scalar=V(V_CW0 + k, c), in1=xr[:, i, 0:sh], op0=ALU.mult, op1=ALU.add),
                        reads=[f"xh{c}", f"xr{i}", "vecs"], writes=[f"xr{i}"])
                S.op("dve", lambda e, c=c, b=b: e.tensor_copy(out=xh[:, c, 0:3], in_=ps[b][:, TT - 3:TT]), reads=[f"ps{b}", f"xh{c}"], writes=[f"xh{c}"])
            i0 = 2 * (blk % 2)
            S.op("dve", lambda e, i0=i0: e.tensor_copy(out=xrb[:, i0:i0 + 2, :], in_=xr[:, i0:i0 + 2, :]),
                 reads=[f"xr{i0}", f"xr{i0 + 1}"], writes=[f"xrb{i0}", f"xrb{i0 + 1}"])
            if last:
                pending.extend(make_stream(slots[0], "S0", w_in[:, 4 * D:5 * D], stA))

        def s34(T, blk):
            for jj_ in range(2):
                i = 2 * (blk % 2) + jj_
                c = 2 * blk + jj_
                bl, j = blk % 2, jj_
                for (wg, dst, hb, dk) in ((wa_g, thr, DV_HBA, "thr"), (wi_g, thi, DV_HBX, "thi")):
                    b = nb()

                    def mmgate(e, j=j, b=b, wg=wg, blk=blk, bl=bl):
                        ins = None
                        for jj in range(2):
                            ins = e.matmul(ps[b][:, :], lhsT=wg[:, 2 * blk + jj, j * P:(j + 1) * P], rhs=xrb[:, 2 * bl + jj, :], start=(jj == 0), stop=(jj == 1))
                        return ins
                    S.op("pe", mmgate, reads=["S2a", "S2b", f"xrb{2 * bl}", f"xrb{2 * bl + 1}"], writes=[f"ps{b}"])
                    S.op("act", lambda e, i=i, b=b, dst=dst, hb=hb, c=c: e.activation(out=dst[:, i, :], in_=ps[b][:, :], func=AF.Tanh,
                                                                                      scale=0.5, bias=DV(hb, c)),
                         reads=[f"ps{b}", "dvhba", "dvhbx"], writes=[f"{dk}{i}"])
            for jj_ in range(2):
                i = 2 * (blk % 2) + jj_
                c = 2 * blk + jj_
                S.op("act", lambda e, i=i, c=c: e.activation(out=thr[:, i, :], in_=thr[:, i, :], func=AF.Exp, scale=DV(DV_M4, c), bias=DV(DV_M4, c)),
                     reads=[f"thr{i}", "dvm4"], writes=[f"thr{i}"])
            i0 = 2 * (blk % 2)
            S.op("dve", lambda e, i0=i0: e.scalar_tensor_tensor(out=t1[:, i0:i0 + 2, :], in0=thi[:, i0:i0 + 2, :], scalar=1.0, in1=xr[:, i0:i0 + 2, :],
                                                                op0=ALU.add, op1=ALU.mult),
                 reads=[f"thi{i0}", f"thi{i0 + 1}", f"xr{i0}", f"xr{i0 + 1}"], writes=[f"t1{i0}", f"t1{i0 + 1}"])
            S.op("pool", lambda e, i0=i0: e.tensor_tensor(out=a2[:, i0:i0 + 2, :], in0=thr[:, i0:i0 + 2, :], in1=thr[:, i0:i0 + 2, :], op=ALU.mult),
                 reads=[f"thr{i0}", f"thr{i0 + 1}"], writes=[f"a2{i0}", f"a2{i0 + 1}"])

        def s56(T, blk, hview, hkeys):
            main = T >= 4
            mt = T - 4
            m0 = mt * TT
            i0 = 2 * (blk % 2)
            c0 = 2 * blk
            ak = [f"a2{i0}", f"a2{i0 + 1}"]
            tk = [f"t1{i0}", f"t1{i0 + 1}"]
            S.op("act", lambda e, i0=i0: e.activation(out=a2[:, i0:i0 + 2, :], in_=a2[:, i0:i0 + 2, :], func=AF.Sqrt, scale=-0.25, bias=0.25),
                 reads=ak, writes=ak)
            if T == 0:
                S.op("dve", lambda e, i0=i0: e.memset(a2[:, i0:i0 + 2, 0:1], 0.5), reads=ak, writes=ak)
            elif T == 4:
                S.op("dve", lambda e, i0=i0: e.tensor_scalar(out=a2[:, i0:i0 + 2, 0:1], in0=a2[:, i0:i0 + 2, 0:1], scalar1=FM, scalar2=FN2, op0=ALU.mult, op1=ALU.add),
                     reads=ak + ["vecs"], writes=ak)
            S.op("pool", lambda e, i0=i0: e.tensor_tensor(out=t1[:, i0:i0 + 2, :], in0=t1[:, i0:i0 + 2, :], in1=a2[:, i0:i0 + 2, :], op=ALU.mult),
                 reads=tk + ak, writes=tk)
            for jj_ in range(2):
                i = i0 + jj_
                c = c0 + jj_
                S.op("dve", lambda e, i=i, c=c: e.tensor_tensor_scan(out=a2[:, i, :], data0=thr[:, i, :], data1=t1[:, i, :], initial=hst[:, c:c + 1],
                                                                     op0=ALU.mult, op1=ALU.add),
                     reads=[f"thr{i}", f"t1{i}", f"hst{c}", f"a2{i}"], writes=[f"a2{i}"])
            S.op("pool", lambda e, i0=i0, c0=c0: e.tensor_copy(out=hst[:, c0:c0 + 2], in_=a2[:, i0:i0 + 2, TT - 1]),
                 reads=ak, writes=[f"hst{c0}", f"hst{c0 + 1}"])
            if main:
                for jj_ in range(2):
                    i = 2 * (blk % 2) + jj_
                    c = 2 * blk + jj_
                    b = nb()

                    def mmy(e, c=c, b=b):
                        ins = None
                        for k in range(8):
                            ins = e.matmul(ps[b][:, :], lhsT=slots[1][:, k, c * P:(c + 1) * P], rhs=hview(k), start=(k == 0), stop=(k == 7))
                        return ins
                    S.op("pe", mmy, reads=["S1"] + hkeys, writes=[f"ps{b}"])
                    S.op("act", lambda e, i=i, b=b: e.activation(out=thi[:, i, :], in_=ps[b][:, :], func=AF.Gelu_apprx_tanh),
                         reads=[f"ps{b}"], writes=[f"thi{i}"])
                S.op("pool", lambda e, i0=i0, c0=c0: e.tensor_tensor(out=prodA[:, c0:c0 + 2, m0:m0 + TT], in0=thi[:, i0:i0 + 2, :], in1=a2[:, i0:i0 + 2, :], op=ALU.mult),
                     reads=[f"thi{i0}", f"thi{i0 + 1}"] + ak, writes=[f"prodA{mt}_{c0}", f"prodA{mt}_{c0 + 1}"])

        norm_part1()
        st8 = [(at(f"stS{i}", [P, D], F32, RX + 24576 + i * 4096), f"stS{i}") for i in range(8)]
        load_cast(slots[0], "S0", w_in[:, 0:D], stages=st8, eng="dve")
        S.op("dve", lambda e: e.tensor_copy(out=wa_g[:], in_=gst[0][:]), reads=["gstA"], writes=["S2a"])
        S.op("dve", lambda e: e.tensor_copy(out=wi_g[:], in_=gst[1][:]), reads=["gstB"], writes=["S2b"])
        norm_load(1)
        pending.extend(make_stream(slots[1], "S1", w_in[:, D:2 * D], stA, extra=["wTs3"]))
        pending.extend(make_stream(slots[3], "S3", w_in[:, 2 * D:3 * D], stA, extra=["gstA", "gstB"]))
        d0, k0 = hT_view(0)
        norm_part2(d0, k0)
        seq = [(T, blk) for T in range(8) for blk in range(4)]
        s12(0, 0, d0, k0)
        s12(0, 1, d0, k0)
        for n, (T, blk) in enumerate(seq):
            dst_fn, dkeys = hT_view(T)
            if (T, blk) == (3, 0):
                S.op("pool", lambda e: e.tensor_copy(out=hTh[:], in_=hT_st[1][:, :, TT - 16:TT]), reads=dkeys, writes=["hTh"])
            if blk == 1 and T + 1 < 8:
                dn_, kn_ = hT_view(T + 1)
                norm_part2(dn_, kn_)
            s34(T, blk)
            if n < 20 or n >= 29:
                tick(1)
            if n == 20:
                flush()
                load_small(wpl, "S2c", w_pl)
            if blk == 0 and T + 1 < 8:
                norm_part1()
                if T + 2 < 8:
                    norm_load(T + 2)
            if n + 2 < len(seq):
                T2, b2 = seq[n + 2]
                dn, kn = hT_view(T2)
                if b2 == 0 and T2 == 4:
                    S.op("dve", lambda e: e.tensor_scalar(out=xh[:], in0=xh[:], scalar1=FM, scalar2=None, op0=ALU.mult),
                         reads=[f"xh{c}" for c in range(8)] + ["vecs"], writes=[f"xh{c}" for c in range(8)])
                s12(T2, b2, dn, kn, last=(n + 2 == len(seq) - 1))
            if n >= 29:
                tick(1)
            s56(T, blk, dst_fn, dkeys)
            if n >= 29:
                tick(1)
            if (T, blk) == (3, 3):
                S.op("dve", lambda e: e.tensor_scalar(out=hst[:], in0=hst[:], scalar1=FM, scalar2=None, op0=ALU.mult),
                     reads=[f"hst{c}" for c in range(8)] + ["vecs"], writes=[f"hst{c}" for c in range(8)])
        pending.extend(make_stream(slots[1], "S1", w_b, stA))

        S.barrier()

        mB = at("mB", [P, 8, NTOK], BF16, RX + 0)
        usb = [at("usA", [P, 2, 528], F32, RX + 32768), at("usB", [P, 2, 528], F32, RX + 36992)]
        sa = at("sa", [P, 2, 528], F32, RX + 41216)
        sb_ = at("sb", [P, 2, 528], F32, RX + 45440)
        tmp16 = at("tmp16", [P, 2, 16], F32, RX + 49664)
        pmb = at("pmb", [P, 2, TT], BF16, RX + 49792)
        poolB = at("poolB", [P, 8, TT], BF16, RX + 51840)
        thbs = [at("thb0", [P, TT], F32, RX + 60032), at("thb1", [P, TT], F32, RX + 62080)]

        wTb2 = at("wTb2", [P, 2, D], F32, RW + 2 * 16384)
        srow2 = at("srow2", [P, D], F32, RW + 2 * 16384 + 12288)
        junkb2 = at("junkb2", [P, D], BF16, RS + 13824)
        S.dma("sp", lambda e: e.dma_start(out=srow2[:], in_=rows_d[:, 0, :]), "c_srow2", writes=["srow2", "S2a", "S2b"] + [f"xrb{i}" for i in range(4)])

        def ob_init():
            silu_row(srow2, "srow2", wTb2[:, 0, :], "wTb2")

        def ob_dma(b2):
            S.dma("sp", lambda e: e.dma_start(out=wTb2[:], in_=w_adaT[2 * D + b2 * 256:2 * D + (b2 + 1) * 256, :].rearrange("(s p) k -> p s k", p=P)),
                  "wTb2", writes=["wTb2"])

        def ob_acc(b2):
            for s2 in range(2):
                col = DV_ACC * 8 + 16 + b2 * 2 + s2
                S.op("dve", lambda e, s2=s2, col=col: e.scalar_tensor_tensor(
                    out=junkb2[:], in0=wTb2[:, s2, :], scalar=1.0, in1=srow2[:], op0=ALU.mult, op1=ALU.mult,
                    accum_out=dv[:, col:col + 1]), reads=["wTb2", "srow2"], writes=["junkb2", "dvaccB"])

        ob_state = {"next": 0}

        def ob_step():
            b2 = ob_state["next"]
            if b2 > 16:
                return
            if b2 >= 1:
                ob_acc(b2 - 1)
            if b2 < 16:
                ob_dma(b2)
            ob_state["next"] = b2 + 1
        for c in range(8):
            b = nb()

            def mmh(e, c=c, b=b):
                ins = None
                for k in range(8):
                    ins = e.matmul(ps[b][:, 0:16], lhsT=slots[3][:, k, c * P:(c + 1) * P], rhs=hTh[:, k, :], start=(k == 0), stop=(k == 7))
                return ins
            S.op("pe", mmh, reads=["S3", "hTh"], writes=[f"ps{b}"])
            S.op("dve", lambda e, c=c, b=b: e.tensor_scalar(out=uh[:, c, :], in0=ps[b][:, 0:16], scalar1=FM, scalar2=None, op0=ALU.mult),
                 reads=[f"ps{b}", "vecs"], writes=["uh"])

        def uproj(mt_, blk_):
            m0_ = mt_ * TT
            ust = usb[blk_ % 2]
            usk = f"us{blk_ % 2}"
            for j in range(2):
                c = 2 * blk_ + j
                b = nb()

                def mmu(e, c=c, b=b, m0_=m0_):
                    ins = None
                    for k in range(8):
                        ins = e.matmul(ps[b][:, :], lhsT=slots[3][:, k, c * P:(c + 1) * P], rhs=hT_all[:, k, m0_:m0_ + TT], start=(k == 0), stop=(k == 7))
                    return ins
                S.op("pe", mmu, reads=["S3"] + [f"hT{mt_}_{k}" for k in range(8)], writes=[f"ps{b}"])
                S.op("pool", lambda e, j=j, c=c: e.tensor_copy(out=ust[:, j, 0:16], in_=uh[:, c, :]), reads=["uh"], writes=[usk])
                S.op("act", lambda e, j=j, b=b: e.activation(out=ust[:, j, 16:16 + TT], in_=ps[b][:, :], func=AF.Identity),
                     reads=[f"ps{b}"], writes=[usk])
                S.op("pool", lambda e, j=j, c=c: e.tensor_copy(out=uh[:, c, :], in_=ust[:, j, TT:TT + 16]), reads=[usk], writes=["uh"])

        S.op("dve", lambda e: e.tensor_tensor(out=dv[:, DV_ACC * 8:DV_ACC * 8 + 8], in0=V(V_BP), in1=V(V_PS), op=ALU.mult),
             reads=["vecs", "dvacc0", "dvacc1"], writes=["dvbps", "dvacc0", "dvacc1"])
        uproj(0, 0)
        for mt in range(4):
            m0 = mt * TT
            hkeys = [f"hT{mt}_{c}" for c in range(8)]
            for blk in range(4):
                win = 2 ** (blk + 1)
                us = usb[blk % 2]
                usk = f"us{blk % 2}"
                if blk < 3:
                    uproj(mt, blk + 1)
                elif mt < 3:
                    uproj(mt + 1, 0)
                tick(1)
                src, srck = us, usk
                bufs = [(sa, "sa"), (sb_, "sb")]
                for lvl in range(blk + 1):
                    sh = 2 ** lvl
                    lo = 2 ** (lvl + 1) - 1
                    dstt, dstk = bufs[lvl % 2]
                    S.op("dve", lambda e, src=src, dstt=dstt, sh=sh, lo=lo: e.tensor_tensor(
                        out=dstt[:, :, lo:528], in0=src[:, :, lo:528], in1=src[:, :, lo - sh:528 - sh], op=ALU.add),
                        reads=[srck], writes=[dstk])
                    src, srck = dstt, dstk
                for j in range(2):
                    c = 2 * blk + j
                    S.op("dve", lambda e, j=j, src=src, win=win, us=us: e.scalar_tensor_tensor(
                        out=pmb[:, j, :], in0=src[:, j, 16:16 + TT], scalar=1.0 / win, in1=us[:, j, 16:16 + TT], op0=ALU.mult, op1=ALU.subtract),
                        reads=[srck, usk], writes=[f"pmb{j}"])
                    if mt == 0:
                        S.op("dve", lambda e, j=j, src=src, blk=blk: e.tensor_tensor(out=tmp16[:, j, :], in0=src[:, j, 16:32], in1=invc[:, blk, :], op=ALU.mult),
                             reads=[srck, "invc"], writes=["tmp16"])
                        S.op("dve", lambda e, j=j, us=us: e.tensor_tensor(out=pmb[:, j, 0:16], in0=tmp16[:, j, :], in1=us[:, j, 16:32], op=ALU.subtract),
                             reads=["tmp16", usk], writes=[f"pmb{j}"])
                tick(1)
                for j in range(2):
                    c = 2 * blk + j
                    b = nb()

                    def mmp(e, j=j, b=b, blk=blk):
                        ins = None
                        for jj in range(2):
                            ins = e.matmul(ps[b][:, :], lhsT=wpl[:, 2 * blk + jj, j * P:(j + 1) * P], rhs=pmb[:, jj, :], start=(jj == 0), stop=(jj == 1))
                        return ins
                    S.op("pe", mmp, reads=["S2c", "pmb0", "pmb1"], writes=[f"ps{b}"])
                    S.op("act", lambda e, c=c, b=b: e.activation(out=poolB[:, c, :], in_=ps[b][:, :], func=AF.Identity,
                                                                 scale=V(V_PS, c), bias=dv[:, DV_ACC * 8 + c:DV_ACC * 8 + c + 1]),
                         reads=[f"ps{b}", "vecs", "dvbps"], writes=[f"poolB{c}"])
                tick(1)
                if mt == 0 and blk == 1:
                    ob_init()
                if mt < 3 and (mt, blk) >= (0, 1):
                    ob_step()
                tick(1)
            flush()
            if mt == 3:
                while ob_state["next"] <= 16:
                    ob_step()
                pending.extend(make_stream(slots[3], "S3", w_in[:, 3 * D:4 * D], stA))
                pending.extend(make_stream(slots[2], "S2", w_a, stA, extra=["S2a", "S2b", "S2c", "wTb2", "srow2"]))
            for c in range(8):
                tick(2)
                if mt < 3 and c % 2 == 1:
                    ob_step()
                b = nb()

                def mmgb(e, c=c, b=b, m0=m0):
                    ins = None
                    for k in range(8):
                        ins = e.matmul(ps[b][:, :], lhsT=slots[0][:, k, c * P:(c + 1) * P], rhs=hT_all[:, k, m0:m0 + TT], start=(k == 0), stop=(k == 7))
                    return ins
                S.op("pe", mmgb, reads=["S0"] + hkeys, writes=[f"ps{b}"])
                thb = thbs[c % 2]
                thk = f"thb{c % 2}"
                S.op("act", lambda e, b=b, thb=thb: e.activation(out=thb[:], in_=ps[b][:, :], func=AF.Tanh, scale=0.5), reads=[f"ps{b}"], writes=[thk])
                b2 = nb()

                def mmB(e, c=c, b2=b2):
                    ins = None
                    for k in range(8):
                        ins = e.matmul(ps[b2][:, :], lhsT=slots[1][:, k, c * P:(c + 1) * P], rhs=poolB[:, k, :], start=(k == 0), stop=(k == 7))
                    return ins
                S.op("pe", mmB, reads=["S1"] + [f"poolB{k}" for k in range(8)], writes=[f"ps{b2}"])
                S.op("dve", lambda e, c=c, b2=b2, m0=m0, thb=thb: e.scalar_tensor_tensor(out=mB[:, c, m0:m0 + TT], in0=thb[:], scalar=1.0, in1=ps[b2][:, :],
                                                                                       op0=ALU.add, op1=ALU.mult),
                     reads=[thk, f"ps{b2}"], writes=[f"mB{mt}_{c}"])

        flush()
        A0 = DV_ACC * 8

        def bcast_gate(gcol, gck, gdst, gkey):
            for half in range(2):
                b = nb()
                for cc in range(4):
                    c = half * 4 + cc
                    S.op("dve", lambda e, c=c, cc=cc: e.tensor_scalar(out=dgs[cc][:], in0=identf[:], scalar1=DV(gcol, c), scalar2=None, op0=ALU.mult),
                         reads=["identf", gck], writes=[f"dg{cc}", "junkb2"])
                for cc in range(4):
                    S.op("pe", lambda e, b=b, cc=cc: e.matmul(ps[b][:, cc * P:(cc + 1) * P], lhsT=ones[:], rhs=dgs[cc][:], start=True, stop=True),
                         reads=["ones", f"dg{cc}"], writes=[f"ps{b}"])
                S.op("dve", lambda e, b=b, half=half: e.tensor_copy(out=gdst[:, half * 512:(half + 1) * 512], in_=ps[b][:, :]),
                     reads=[f"ps{b}"], writes=[gkey])

        dgs = [dg] + [at(f"dg{i}", [P, P], F32, RS + 13824 + 2048 - 512 * (4 - i)) for i in range(1, 4)]
        S.op("dve", lambda e: e.tensor_tensor(out=DV(DV_T0), in0=dv[:, A0 + 16:A0 + 24], in1=V(V_BGT1), op=ALU.add),
             reads=["dvaccB", "vecs", "dvt0"], writes=["dvt0"])
        S.op("dve", lambda e: e.tensor_scalar(out=DV(DV_T0), in0=DV(DV_T0), scalar1=0.5, scalar2=None, op0=ALU.mult), reads=["dvt0"], writes=["dvt0"])
        S.op("dve", lambda e: e.tensor_tensor(out=DV(DV_SH2), in0=dv[:, A0 + 24:A0 + 32], in1=V(V_BSH2), op=ALU.add),
             reads=["dvaccB", "vecs"], writes=["dvsh2"])
        S.op("dve", lambda e: e.tensor_tensor(out=DV(DV_SC2), in0=dv[:, A0 + 32:A0 + 40], in1=V(V_BSC2), op=ALU.add),
             reads=["dvaccB", "vecs"], writes=["dvsc2"])
        S.op("dve", lambda e: e.scalar_tensor_tensor(out=DV(DV_GM2), in0=DV(DV_SC2), scalar=1.0, in1=V(V_G2), op0=ALU.add, op1=ALU.mult),
             reads=["dvsc2", "vecs"], writes=["dvgm2"])
        S.op("dve", lambda e: e.tensor_tensor(out=DV(DV_T1), in0=dv[:, A0 + 40:A0 + 48], in1=V(V_BGT2), op=ALU.add),
             reads=["dvaccB", "vecs", "dvt1"], writes=["dvt1"])
        bcast_gate(DV_T0, "dvt0", gtb1, "gtb1")
        bcast_gate(DV_T1, "dvt1", gtb2, "gtb2")
        S.barrier()


        tha = [at(f"tha{i}", [P, TT], F32, RX + 32768 + i * 2048) for i in range(2)]
        tt_ = [at(f"tt{i}", [P, TT], F32, RX + 36864 + i * 2048) for i in range(2)]
        stC = [(at("stC0", [P, D], F32, RX + 40960), "stC0"), (at("stC1", [P, D], F32, RX + 45056), "stC1")]
        pending.extend(make_stream(slots[0], "S0", w_o, stC, gtb1, "gtb1"))
        pending.extend(make_stream(slots[1], "S1", w_up[:, 0:D], stC))
        xt2 = at("xt2", [P, 4, D], F32, RX + 49152)

        def load_xt2(mt_):
            r0_ = NTOK + mt_ * TT
            S.dma("sp", lambda e: e.dma_start(out=xt2[:], in_=xin[r0_:r0_ + TT, :].rearrange("(s p) d -> p s d", p=P)), "xt2", writes=["xt2"])

        for mt in range(4):
            m0 = mt * TT
            hkeys = [f"hT{mt}_{c}" for c in range(8)]
            if mt == 3:
                load_xt2(0)
            for c in range(8):
                i2 = c % 2
                b = nb()

                def mmga(e, c=c, b=b, m0=m0):
                    ins = None
                    for k in range(8):
                        ins = e.matmul(ps[b][:, :], lhsT=slots[3][:, k, c * P:(c + 1) * P], rhs=hT_all[:, k, m0:m0 + TT], start=(k == 0), stop=(k == 7))
                    return ins
                S.op("pe", mmga, reads=["S3"] + hkeys, writes=[f"ps{b}"])
                S.op("act", lambda e, b=b, i2=i2: e.activation(out=tha[i2][:], in_=ps[b][:, :], func=AF.Tanh, scale=0.5), reads=[f"ps{b}"], writes=[f"tha{i2}"])
                b2 = nb()

                def mmA(e, c=c, b2=b2, m0=m0):
                    ins = None
                    for k in range(8):
                        ins = e.matmul(ps[b2][:, :], lhsT=slots[2][:, k, c * P:(c + 1) * P], rhs=prodA[:, k, m0:m0 + TT], start=(k == 0), stop=(k == 7))
                    return ins
                S.op("pe", mmA, reads=["S2"] + [f"prodA{mt}_{k}" for k in range(8)], writes=[f"ps{b2}"])
                S.op("dve", lambda e, i2=i2, b2=b2: e.scalar_tensor_tensor(out=tt_[i2][:], in0=tha[i2][:], scalar=1.0, in1=ps[b2][:, :], op0=ALU.add, op1=ALU.mult),
                     reads=[f"tha{i2}", f"ps{b2}"], writes=[f"tt{i2}"])
                S.op("pool", lambda e, c=c, i2=i2, m0=m0: e.tensor_tensor(out=mB[:, c, m0:m0 + TT], in0=tt_[i2][:], in1=mB[:, c, m0:m0 + TT], op=ALU.add),
                     reads=[f"tt{i2}", f"mB{mt}_{c}"], writes=[f"mB{mt}_{c}"])
                tick()
        flush()

        S.barrier()

        x1 = at("x1", [P, 16, D], F32, RHP)
        h2T = at("h2T", [P, 8, NTOK], BF16, RX + 0)
        stg = [at(f"stg{i}", [P, D], F32, RX + 32768 + i * 4096) for i in range(2)]
        xn2 = at("xn2", [P, 4, D], BF16, RX + 40960)
        STG = [(stg[0], "stg0"), (stg[1], "stg1")]

        pending += make_stream(slots[3], "S3", w_dn[0:D, :], STG, gtb2, "gtb2")
        pending += make_stream(slots[2], "S2", w_up[:, D:2 * D], STG)

        def norm2_part1(mt):
            for s in range(4):
                S.op("act", lambda e, s=s: e.activation(out=xn2[:, s, :], in_=x1[:, mt * 4 + s, :], func=AF.Square, accum_out=stat[:, 16 + s:17 + s]),
                     reads=[f"x1_{mt * 4 + s}"], writes=[f"xn2_{s}", f"ss2_{s}"])
            S.op("act", lambda e: e.activation(out=stat[:, 20:24], in_=stat[:, 16:20], func=AF.Sqrt, scale=1.0 / D, bias=misc[:, 0:1]),
                 reads=["ss2_0", "ss2_1", "ss2_2", "ss2_3", "misc"], writes=["std2"])
            S.op("dve", lambda e: e.reciprocal(out=stat[:, 24:28], in_=stat[:, 20:24]), reads=["std2"], writes=["rstd2"])
            for s in range(4):
                S.op("act", lambda e, s=s: e.activation(out=xn2[:, s, :], in_=x1[:, mt * 4 + s, :], func=AF.Identity, scale=stat[:, 24 + s:25 + s]),
                     reads=[f"x1_{mt * 4 + s}", "rstd2"], writes=[f"xn2_{s}"])

        def norm2_part2(mt):
            m0 = mt * TT
            for c in range(8):
                b = nb()

                def tr2(e, c=c, b=b):
                    ins = None
                    for s in range(4):
                        ins = e.transpose(out=psb[b][:, s * P:(s + 1) * P], in_=xn2[:, s, c * P:(c + 1) * P], identity=ident[:])
                    return ins
                S.op("pe", tr2, reads=["xn2_0", "xn2_1", "xn2_2", "xn2_3", "ident"], writes=[f"ps{b}"])
                S.op("act", lambda e, c=c, b=b: e.activation(out=h2T[:, c, m0:m0 + TT], in_=psb[b][:, 0:TT], func=AF.Identity, scale=DV(DV_GM2, c), bias=DV(DV_SH2, c)),
                     reads=[f"ps{b}", "dvgm2", "dvsh2"], writes=[f"h2T{mt}_{c}", f"mB{mt}_{c}"])

        for mt in range(4):
            m0 = mt * TT
            r0 = NTOK + m0
            if mt > 0:
                load_xt2(mt)
            for s in range(4):
                for half in range(2):
                    b = nb()

                    def mmo(e, s=s, half=half, b=b, m0=m0):
                        ins = None
                        for k in range(8):
                            ins = e.matmul(ps[b][:, :], lhsT=mB[:, k, m0 + s * P:m0 + (s + 1) * P], rhs=slots[0][:, k, half * 512:(half + 1) * 512],
                                           start=(k == 0), stop=(k == 7))
                        return ins
                    S.op("pe", mmo, reads=["S0"] + [f"mB{mt}_{k}" for k in range(8)], writes=[f"ps{b}"])
                    S.op("dve", lambda e, s=s, half=half, b=b, mt=mt: e.tensor_tensor(
                        out=x1[:, mt * 4 + s, half * 512:(half + 1) * 512], in0=ps[b][:, :], in1=xt2[:, s, half * 512:(half + 1) * 512], op=ALU.add),
                        reads=[f"ps{b}", "xt2"], writes=[f"x1_{mt * 4 + s}"])
                    tick()
            if mt > 0:
                norm2_part2(mt - 1)
            norm2_part1(mt)
        norm2_part2(3)
        flush()

        S.barrier()

        ff = [at(f"ff{i}", [P, 8, TT], BF16, RX + 32768 + i * 8192) for i in range(2)]
        rr = [at(f"rr{i}", [P, TT], F32, RX + 49152 + i * 2048) for i in range(2)]
        stgE = [at("stgE0", [P, D], F32, RX + 53248), at("stgE1", [P, D], F32, RX + 57344)]
        junk2 = at("junk2", [P, D], BF16, RX + 61440)
        STGE = [(stgE[0], "stgE0"), (stgE[1], "stgE1")]
        up_slot = {0: 1, 1: 2, 2: 1, 3: 2}
        dn_slot = {0: 3, 1: 0, 2: 3, 3: 0}

        def queue_quarter(q, up=True):
            if up:
                pending.extend(make_stream(slots[up_slot[q]], f"S{up_slot[q]}", w_up[:, q * D:(q + 1) * D], [(gtb1, "gtb1")]))
            pending.extend(make_stream(slots[dn_slot[q]], f"S{dn_slot[q]}", w_dn[q * D:(q + 1) * D, :], STGE, gtb2, "gtb2"))

        fgb = stgE[0]

        def final_norm_tile(mt):
            for s4 in range(4):
                s16 = mt * 4 + s4
                S.op("act", lambda e, s16=s16, s4=s4: e.activation(out=junk2[:], in_=x1[:, s16, :], func=AF.Square, accum_out=stat[:, 32 + s4:33 + s4]),
                     reads=[f"x1_{s16}"], writes=["junk2", f"ss3_{s4}"])
            S.op("act", lambda e: e.activation(out=stat[:, 40:44], in_=stat[:, 32:36], func=AF.Sqrt, scale=1.0 / D, bias=misc[:, 0:1]),
                 reads=["ss3_0", "ss3_1", "ss3_2", "ss3_3", "misc"], writes=["std3"])
            S.op("dve", lambda e: e.reciprocal(out=stat[:, 44:48], in_=stat[:, 40:44]), reads=["std3"], writes=["rstd3"])
            for s4 in range(4):
                s16 = mt * 4 + s4
                S.op("dve", lambda e, s16=s16, s4=s4: e.scalar_tensor_tensor(out=x1[:, s16, :], in0=x1[:, s16, :], scalar=stat[:, 44 + s4:45 + s4], in1=fgb[:],
                                                                             op0=ALU.mult, op1=ALU.mult),
                     reads=[f"x1_{s16}", "rstd3", "fgb"], writes=[f"x1_{s16}"])
                S.dma("sp", lambda e, s16=s16: e.dma_start(out=yout[s16 * P:(s16 + 1) * P, :], in_=x1[:, s16, :]), "yo", reads=[f"x1_{s16}"], writes=[f"y_{s16}"])

        def final_norm_sub(mt, s4):
            s16 = mt * 4 + s4
            S.op("act", lambda e: e.activation(out=junk2[:], in_=x1[:, s16, :], func=AF.Square, accum_out=stat[:, 32 + s4:33 + s4]),
                 reads=[f"x1_{s16}"], writes=["junk2", f"ss3_{s4}"])
            S.op("act", lambda e: e.activation(out=stat[:, 40 + s4:41 + s4], in_=stat[:, 32 + s4:33 + s4], func=AF.Sqrt, scale=1.0 / D, bias=misc[:, 0:1]),
                 reads=[f"ss3_{s4}", "misc"], writes=[f"std3_{s4}", "std3"])
            S.op("dve", lambda e: e.reciprocal(out=stat[:, 44 + s4:45 + s4], in_=stat[:, 40 + s4:41 + s4]), reads=[f"std3_{s4}"], writes=[f"rstd3_{s4}", "rstd3"])
            S.op("dve", lambda e: e.scalar_tensor_tensor(out=x1[:, s16, :], in0=x1[:, s16, :], scalar=stat[:, 44 + s4:45 + s4], in1=fgb[:],
                                                         op0=ALU.mult, op1=ALU.mult),
                 reads=[f"x1_{s16}", f"rstd3_{s4}", "fgb"], writes=[f"x1_{s16}"])
            S.dma("sp", lambda e: e.dma_start(out=yout[s16 * P:(s16 + 1) * P, :], in_=x1[:, s16, :]), "yo", reads=[f"x1_{s16}"], writes=[f"y_{s16}"])

        queue_quarter(1, up=False)
        for q in range(4):
            if 1 <= q and q + 1 < 4:
                queue_quarter(q + 1)
            if q == 3:
                S.dma("sp", lambda e: e.dma_start(out=fgb[:], in_=rows_d[:, 1, :]), "st_stgE0", reads=[], writes=["stgE0", "fgb"])
            us_, ds_ = up_slot[q], dn_slot[q]

            def ffn_up(mt, us_=us_):
                m0 = mt * TT
                fb = ff[mt % 2]
                fk = f"ff{mt % 2}"
                for j in range(8):
                    b = nb()
                    i2 = j % 2

                    def mmup(e, j=j, b=b, m0=m0, us_=us_):
                        ins = None
                        for k in range(8):
                            ins = e.matmul(ps[b][:, :], lhsT=slots[us_][:, k, j * P:(j + 1) * P], rhs=h2T[:, k, m0:m0 + TT], start=(k == 0), stop=(k == 7))
                        return ins
                    S.op("pe", mmup, reads=[f"S{us_}"] + [f"h2T{mt}_{k}" for k in range(8)], writes=[f"ps{b}"])
                    S.op("act", lambda e, b=b, i2=i2: e.activation(out=rr[i2][:], in_=ps[b][:, :], func=AF.Relu), reads=[f"ps{b}"], writes=[f"rr{i2}"])
                    S.op("act", lambda e, j=j, i2=i2, fb=fb: e.activation(out=fb[:, j, :], in_=rr[i2][:], func=AF.Square), reads=[f"rr{i2}"], writes=[f"{fk}_{j}"])
                    if j % 2 == 1:
                        tick()

            def ffn_dn(mt, ds_=ds_, q=q):
                fb = ff[mt % 2]
                fk = f"ff{mt % 2}"
                for s in range(4):
                    for half in range(2):
                        b = nb()

                        def mmdn(e, s=s, half=half, b=b, fb=fb, ds_=ds_):
                            ins = None
                            for j in range(8):
                                ins = e.matmul(ps[b][:, :], lhsT=fb[:, j, s * P:(s + 1) * P], rhs=slots[ds_][:, j, half * 512:(half + 1) * 512],
                                               start=(j == 0), stop=(j == 7))
                            return ins
                        S.op("pe", mmdn, reads=[f"S{ds_}"] + [f"{fk}_{j}" for j in range(8)], writes=[f"ps{b}"])
                        S.op("dve", lambda e, s=s, half=half, b=b, mt=mt: e.tensor_tensor(
                            out=x1[:, mt * 4 + s, half * 512:(half + 1) * 512], in0=ps[b][:, :], in1=x1[:, mt * 4 + s, half * 512:(half + 1) * 512], op=ALU.add),
                            reads=[f"ps{b}", f"x1_{mt * 4 + s}"], writes=[f"x1_{mt * 4 + s}"])
                        if half == 1 and s % 2 == 1:
                            tick()
                        if q == 3 and mt == 3 and half == 1:
                            final_norm_sub(mt, s)
                if q == 3 and mt < 3:
                    final_norm_tile(mt)

            for kind, mt in (("up", 0), ("up", 1), ("dn", 0), ("up", 2), ("dn", 1), ("up", 3), ("dn", 2), ("dn", 3)):
                if kind == "up":
                    ffn_up(mt)
                else:
                    ffn_dn(mt)
            flush()

        S.barrier()
        S.emit(nc, st)
    return nc


def _cols(v):
    return np.ascontiguousarray(np.asarray(v, dtype=np.float32).reshape(8, P).T)


_NC_CACHE = {}


def kernel(x, c, norm_mix_g, norm_mlp_g, w_ada, b_ada, w_in, conv_w, conv_b, w_rg_a, b_rg_a, w_rg_x, b_rg_x,
           a_param, w_branch_a, w_pool, b_pool, pool_scale, w_branch_b, w_out, w_up, w_down, final_g):
    f32 = np.float32
    x = np.asarray(x, f32)
    c = np.asarray(c, f32)
    w_ada0 = np.asarray(w_ada, f32)[0]
    b_ada0 = np.asarray(b_ada, f32)[0]
    B, SEQ, _ = x.shape
    half = SEQ // 2
    w_adaT = np.ascontiguousarray(w_ada0.T)
    shared = {
        "ident": np.eye(P, dtype=f32),
        "w_in": np.ascontiguousarray(np.asarray(w_in, f32)[0]),
        "w_ra": np.ascontiguousarray(np.asarray(w_rg_a, f32)[0].reshape(D, 256)),
        "w_rx": np.ascontiguousarray(np.asarray(w_rg_x, f32)[0].reshape(D, 256)),
        "w_pl": np.ascontiguousarray(np.asarray(w_pool, f32)[0].reshape(D, 256)),
        "w_a": np.ascontiguousarray(np.asarray(w_branch_a, f32)[0]),
        "w_b": np.ascontiguousarray(np.asarray(w_branch_b, f32)[0]),
        "w_o": np.ascontiguousarray(np.asarray(w_out, f32)[0]),
        "w_up": np.ascontiguousarray(np.asarray(w_up, f32)[0]),
        "w_dn": np.ascontiguousarray(np.asarray(w_down, f32)[0]),
        "w_adaT": w_adaT,
    }
    cw = np.asarray(conv_w, f32)[0]
    in_maps = []
    for r in range(8):
        b, hf = r // 2, r % 2
        if hf == 0:
            xin = np.concatenate([np.zeros((half, D), f32), x[b, 0:half]], axis=0)
        else:
            xin = x[b]
        flag = np.zeros((P, 8), f32)
        flag[:, 0] = float(hf)
        flag[:, 1] = 1.0 - float(hf)
        flag[:, 2] = 0.5 * (1.0 - float(hf))
        vec_list = [_cols(np.asarray(norm_mix_g, f32)[0]), _cols(np.asarray(norm_mlp_g, f32)[0]),
                    _cols(b_ada0[0:D]), _cols(b_ada0[D:2 * D]), _cols(b_ada0[3 * D:4 * D]), _cols(b_ada0[4 * D:5 * D]),
                    _cols(cw[0]), _cols(cw[1]), _cols(cw[2]), _cols(cw[3]), _cols(np.asarray(conv_b, f32)[0]),
                    _cols(np.asarray(b_rg_a, f32)[0]), _cols(np.asarray(b_rg_x, f32)[0]), _cols(np.asarray(a_param, f32)[0]),
                    _cols(np.asarray(b_pool, f32)[0]), _cols(np.asarray(pool_scale, f32)[0]), _cols(c[b]), flag,
                    _cols(b_ada0[2 * D:3 * D]), _cols(b_ada0[5 * D:6 * D])]
        vecs = np.ascontiguousarray(np.concatenate(vec_list, axis=1))
        rows = np.empty((P, 2, D), f32)
        rows[:, 0, :] = c[b][None, :]
        rows[:, 1, :] = np.asarray(final_g, f32)[None, :]
        invc = np.empty((P, 4, 16), f32)
        for gi, win in enumerate((2, 4, 8, 16)):
            if hf == 0:
                cntv = np.minimum(np.arange(16) + 1, win).astype(f32)
            else:
                cntv = np.full(16, win, f32)
            invc[:, gi, :] = (1.0 / cntv)[None, :]
        m = {"xin": np.ascontiguousarray(xin), "vecs": vecs, "rows": rows, "invc": invc}
        m.update(shared)
        in_maps.append(m)
    if "nc" not in _NC_CACHE:
        _NC_CACHE["nc"] = build_nc()
    nc = _NC_CACHE["nc"]
    res = run_bass_kernel_spmd(nc, in_maps, core_ids=list(range(8)))
    out = np.empty((B, SEQ, D), f32)
    for r in range(8):
        b, hf = r // 2, r % 2
        out[b, hf * half:(hf + 1) * half] = res.results[r]["yout"]
    return out
```

```python
import numpy as np
from contextlib import ExitStack
import concourse.bass as bass
import concourse.mybir as mybir
from concourse.bass_utils import run_bass_kernel_spmd

F32 = mybir.dt.float32
BF16 = mybir.dt.bfloat16
AF = mybir.ActivationFunctionType
ALU = mybir.AluOpType

ENGS = ("pe", "act", "dve", "pool", "sp")
P = 128
D = 1024
DFF = 4096
NTOK = 2048
TT = 512
EPS = 1e-6
GELU_NATIVE = True


class Sched:
    def __init__(self):
        self.q = {e: [] for e in ENGS}
        self.prog = {e: 0 for e in ENGS}
        self.waited = {e: {} for e in ENGS}
        self.bufs = {}
        self.dma_cnt = {}

    def _st(self, k):
        st = self.bufs.get(k)
        if st is None:
            st = {"w": None, "r": {}}
            self.bufs[k] = st
        return st

    def _collect(self, eng, reads, writes):
        deps = []
        for k in reads:
            st = self._st(k)
            d = st["w"]
            if d is not None and not (d[2] == eng and eng == "pe"):
                deps.append(d)
        for k in writes:
            st = self._st(k)
            if st["w"] is not None and not (st["w"][2] == eng and eng == "pe"):
                deps.append(st["w"])
            for d in st["r"].values():
                if not (d[2] == eng and eng == "pe"):
                    deps.append(d)
        w = self.waited[eng]
        best = {}
        for (sk, val, _src) in deps:
            if w.get(sk, 0) >= val:
                continue
            if best.get(sk, 0) < val:
                best[sk] = val
        waits = []
        for sk, val in best.items():
            w[sk] = val
            waits.append((sk, val))
        return waits

    def _commit(self, dep, reads, writes):
        for k in reads:
            self._st(k)["r"][dep[0]] = dep
        for k in writes:
            st = self._st(k)
            st["w"] = dep
            st["r"] = {}

    def op(self, eng, fn, reads=(), writes=()):
        waits = self._collect(eng, reads, writes)
        self.prog[eng] += 1
        dep = ("p_" + eng, self.prog[eng], eng)
        self.q[eng].append((waits, fn, ("p_" + eng, 1)))
        self._commit(dep, reads, writes)

    def dma(self, qeng, fn, sem, reads=(), writes=()):
        waits = self._collect(qeng, reads, writes)
        self.dma_cnt[sem] = self.dma_cnt.get(sem, 0) + 16
        dep = ("d_" + sem, self.dma_cnt[sem], None)
        self.q[qeng].append((waits, fn, ("d_" + sem, 16)))
        self._commit(dep, reads, writes)

    def barrier(self):
        targets = [("p_" + e, self.prog[e]) for e in ENGS if self.prog[e] > 0]
        targets += [("d_" + s, c) for s, c in self.dma_cnt.items()]
        for e in ENGS:
            w = self.waited[e]
            waits = []
            for sk, val in targets:
                if sk == "p_" + e:
                    continue
                if w.get(sk, 0) < val:
                    w[sk] = val
                    waits.append((sk, val))
            if waits:
                self.q[e].append((waits, None, None))

    def _plan(self):
        needed = {e: set() for e in ENGS}
        for e in ENGS:
            for waits, fn, inc in self.q[e]:
                for sk, val in waits:
                    if sk.startswith("p_"):
                        needed[sk[2:]].add(val)
        rank = {e: {k: i + 1 for i, k in enumerate(sorted(needed[e]))} for e in ENGS}
        plan = {}
        for e in ENGS:
            cnt = 0
            out = []
            for waits, fn, inc in self.q[e]:
                w2 = [(sk, rank[sk[2:]][val]) if sk.startswith("p_") else (sk, val) for sk, val in waits]
                inc2 = inc
                if inc is not None and inc[0].startswith("p_"):
                    cnt += 1
                    inc2 = inc if cnt in needed[e] else None
                out.append((w2, fn, inc2))
            plan[e] = out
        return plan

    @staticmethod
    def _check(plan):
        sem = {}
        ptr = {e: 0 for e in ENGS}
        total = sum(len(plan[e]) for e in ENGS)
        done = 0
        while done < total:
            progressed = False
            for e in ENGS:
                while ptr[e] < len(plan[e]):
                    waits, fn, inc = plan[e][ptr[e]]
                    if any(sem.get(sk, 0) < val for sk, val in waits):
                        break
                    if inc is not None:
                        sem[inc[0]] = sem.get(inc[0], 0) + inc[1]
                    ptr[e] += 1
                    done += 1
                    progressed = True
            if not progressed:
                raise RuntimeError(f"semaphore plan deadlocks at {ptr}")

    def emit(self, nc, stack):
        plan = self._plan()
        self._check(plan)
        sems = {}
        for e in ENGS:
            if self.prog[e] > 0:
                sems["p_" + e] = stack.enter_context(nc.semaphore("p_" + e))
        for s_ in self.dma_cnt:
            sems["d_" + s_] = stack.enter_context(nc.semaphore("d_" + s_))
        block = stack.enter_context(nc.Block())

        def mk(engname):
            def body(e):
                for waits, fn, inc in plan[engname]:
                    for sk, val in waits:
                        e.wait_ge(sems[sk], val)
                    if fn is not None:
                        ins = fn(e)
                        if inc is not None:
                            ins.then_inc(sems[inc[0]], inc[1])
            return body

        block.sync(mk("sp"))
        block.tensor(mk("pe"))
        block.scalar(mk("act"))
        block.vector(mk("dve"))
        block.gpsimd(mk("pool"))


(V_G1, V_G2, V_BSH1, V_BSC1, V_BSH2, V_BSC2, V_CW0, V_CW1, V_CW2, V_CW3, V_CB, V_BRA, V_BRX, V_AP, V_BP, V_PS, V_C, V_FLAG,
 V_BGT1, V_BGT2) = range(20)
NV = 20
(DV_SILU, DV_SH1, DV_SC1, DV_SH2, DV_SC2, DV_GM1, DV_GM2, DV_HBA, DV_HBX, DV_M4, DV_M8, DV_T0, DV_T1, DV_SP,
 DV_ACC) = range(15)
NDV = 20


def build_nc():
    nc = bass.Bass("TRN2", target_bir_lowering=False)

    def din(name, shape):
        return nc.dram_tensor(name, shape, F32, kind="ExternalInput").ap()

    xin = din("xin", [2 * NTOK, D])
    vecs_d = din("vecs", [P, NV * 8])
    rows_d = din("rows", [P, 2, D])
    invc_d = din("invc", [P, 4, 16])
    ident_d = din("ident", [P, P])
    w_in = din("w_in", [D, 5 * D])
    w_ra = din("w_ra", [D, 256])
    w_rx = din("w_rx", [D, 256])
    w_pl = din("w_pl", [D, 256])
    w_a = din("w_a", [D, D])
    w_b = din("w_b", [D, D])
    w_o = din("w_o", [D, D])
    w_up = din("w_up", [D, DFF])
    w_dn = din("w_dn", [DFF, D])
    w_adaT = din("w_adaT", [6 * D, D])
    yout = nc.dram_tensor("yout", [NTOK, D], F32, kind="ExternalOutput").ap()

    S = Sched()
    BASE = 16512
    RW, RHP, RX, RS = BASE, BASE + 65536, BASE + 131072, BASE + 196608
    cnt = [0]

    def at(name, shape, dt, off):
        cnt[0] += 1
        return nc.alloc_sbuf_tensor_at(f"{name}_{cnt[0]}", shape, dt, offset=off)

    with ExitStack() as st:
        ps = [st.enter_context(nc.psum_tensor(f"ps{i}", [P, 512], F32)) for i in range(8)]
        psb = [p.bitcast(BF16) for p in ps]
        bank_i = [0]

        def nb():
            b = bank_i[0]
            bank_i[0] = (b + 1) % 8
            return b

        vecs = at("vecs", [P, NV * 8], F32, RS + 0)
        dv = at("dv", [P, NDV * 8], F32, RS + 640)
        stat = at("stat", [P, 64], F32, RS + 1280)
        ident = at("ident", [P, P], BF16, RS + 1536)
        xh = at("xh", [P, 8, 4], F32, RS + 1792)
        uh = at("uh", [P, 8, 16], F32, RS + 1920)
        hst = at("hst", [P, 8], F32, RS + 2432)
        hTh = at("hTh", [P, 8, 16], BF16, RS + 2464)
        invc = at("invc", [P, 4, 16], F32, RS + 2720)
        misc = at("misc", [P, 16], F32, RS + 2976)
        gtb1 = at("gtb1", [P, D], F32, RS + 4096)
        gtb2 = at("gtb2", [P, D], F32, RS + 8192)
        identf = at("identf", [P, P], F32, RS + 12288)
        ones = at("ones", [P, P], F32, RS + 12800)
        dg = at("dg", [P, P], F32, RS + 13312)

        def V(i, c=None):
            return vecs[:, i * 8:(i + 1) * 8] if c is None else vecs[:, i * 8 + c:i * 8 + c + 1]

        def DV(i, c=None):
            return dv[:, i * 8:(i + 1) * 8] if c is None else dv[:, i * 8 + c:i * 8 + c + 1]

        FM = vecs[:, V_FLAG * 8:V_FLAG * 8 + 1]
        FN = vecs[:, V_FLAG * 8 + 1:V_FLAG * 8 + 2]

        slots = [at(f"slot{i}", [P, 8, D], BF16, RW + i * 16384) for i in range(4)]
        wa_g = at("wa_g", [P, 8, 256], BF16, RW + 2 * 16384)
        wi_g = at("wi_g", [P, 8, 256], BF16, RW + 2 * 16384 + 4096)
        wpl = at("wpl", [P, 8, 256], BF16, RW + 2 * 16384 + 8192)

        stA = [(gtb1, "gtb1"), (gtb2, "gtb2")]
        stA4 = [(at("stA4_0", [P, 4, 256], F32, RS + 4096), "gtb1"), (at("stA4_1", [P, 4, 256], F32, RS + 8192), "gtb2")]
        st_rr = [0]

        def load_cast(slot_t, key, dram2d, nrow_chunks=8, extra=(), stages=None, eng="act"):
            stages = stages or stA
            for kc in range(nrow_chunks):
                st_t, st_k = stages[st_rr[0] % len(stages)]
                st_rr[0] += 1
                S.dma("sp", lambda e, kc=kc, st_t=st_t: e.dma_start(out=st_t[:], in_=dram2d[kc * P:(kc + 1) * P, :]), "st_" + st_k, writes=[st_k])
                if eng == "act":
                    S.op("act", lambda e, kc=kc, st_t=st_t: e.activation(out=slot_t[:, kc, :], in_=st_t[:], func=AF.Identity),
                         reads=[st_k], writes=[key] + list(extra))
                else:
                    S.op("dve", lambda e, kc=kc, st_t=st_t: e.tensor_copy(out=slot_t[:, kc, :], in_=st_t[:]),
                         reads=[st_k], writes=[key] + list(extra))

        def load_small(w_t, key, dram2d):
            for h in range(2):
                st_t, st_k = stA4[st_rr[0] % 2]
                st_rr[0] += 1
                S.dma("sp", lambda e, h=h, st_t=st_t: e.dma_start(out=st_t[:], in_=dram2d[h * 512:(h + 1) * 512, :].rearrange("(s p) n -> p s n", p=P)),
                      "st_" + st_k, writes=[st_k])
                S.op("act", lambda e, h=h, st_t=st_t: e.activation(out=w_t[:, 4 * h:4 * h + 4, :], in_=st_t[:], func=AF.Identity),
                     reads=[st_k], writes=[key])

        def make_stream(slot_t, key, dram2d, stages, gt=None, gtk=None, extra=()):
            state = {"prev": None}

            def consume():
                kc, st_t, st_k = state["prev"]
                if gt is None:
                    S.op("act", lambda e: e.activation(out=slot_t[:, kc, :], in_=st_t[:], func=AF.Identity), reads=[st_k], writes=[key] + list(extra))
                else:
                    S.op("dve", lambda e: e.tensor_tensor(out=slot_t[:, kc, :], in0=st_t[:], in1=gt[:], op=ALU.mult), reads=[st_k, gtk], writes=[key] + list(extra))
                state["prev"] = None

            def mk(kc):
                def step():
                    st_t, st_k = stages[kc % len(stages)]
                    if state["prev"] is not None:
                        consume()
                    S.dma("sp", lambda e: e.dma_start(out=st_t[:], in_=dram2d[kc * P:(kc + 1) * P, :]), "st_" + st_k, writes=[st_k])
                    state["prev"] = (kc, st_t, st_k)
                return step
            return [mk(kc) for kc in range(8)] + [consume]

        pending = []

        def tick(n=1):
            for _ in range(n):
                if pending:
                    pending.pop(0)()

        def flush():
            while pending:
                pending.pop(0)()

        S.dma("sp", lambda e: e.dma_start(out=vecs[:], in_=vecs_d), "c_vecs", writes=["vecs"])
        S.dma("sp", lambda e: e.dma_start(out=identf[:], in_=ident_d), "c_ident", writes=["identf"])
        S.dma("sp", lambda e: e.dma_start(out=invc[:], in_=invc_d), "c_invc", writes=["invc"])

        wTs = [at("wTs0", [P, 4, D], F32, RHP), at("wTs1", [P, 4, D], F32, RHP + 16384),
               at("wTs2", [P, 4, D], F32, RHP + 49152), at("wTs3", [P, 4, D], F32, RW + 16384)]
        srow = at("srow", [P, D], F32, RHP + 32768)
        junkf = at("junkf", [P, D], F32, RHP + 36864)
        xt = at("xt", [P, 4, D], F32, RX + 0)

        def norm_load(tile_idx):
            r0 = tile_idx * TT
            S.dma("sp", lambda e: e.dma_start(out=xt[:], in_=xin[r0:r0 + TT, :].rearrange("(s p) d -> p s d", p=P)), "xt", writes=["xt"])

        norm_load(0)

        S.dma("sp", lambda e: e.dma_start(out=srow[:], in_=rows_d[:, 0, :]), "c_srow", writes=["srow"])

        def adaT_dma(blk, buf_t, bkey):
            S.dma("sp", lambda e: e.dma_start(
                out=buf_t[:], in_=w_adaT[blk * 512:(blk + 1) * 512, :].rearrange("(s p) k -> p s k", p=P)),
                bkey, writes=[bkey])

        def adaT_acc(blk, buf_t, bkey, srow_t, srow_k, junk_t, junk_k):
            for s in range(4):
                col = DV_ACC * 8 + blk * 4 + s
                S.op("dve", lambda e, s=s, col=col: e.scalar_tensor_tensor(
                    out=junk_t[:], in0=buf_t[:, s, :], scalar=1.0, in1=srow_t[:], op0=ALU.mult, op1=ALU.mult,
                    accum_out=dv[:, col:col + 1]), reads=[bkey, srow_k], writes=[junk_k, f"dvacc{blk}"])

        def silu_row(srow_t, srow_k, junk_t, junk_k):
            jv = junk_t if isinstance(junk_t, bass.AP) else junk_t[:]
            S.op("act", lambda e: e.activation(out=jv, in_=srow_t[:], func=AF.Tanh, scale=0.5), reads=[srow_k], writes=[junk_k])
            S.op("dve", lambda e: e.scalar_tensor_tensor(out=jv, in0=jv, scalar=1.0, in1=srow_t[:], op0=ALU.add, op1=ALU.mult),
                 reads=[junk_k, srow_k], writes=[junk_k])
            S.op("dve", lambda e: e.tensor_scalar(out=srow_t[:], in0=jv, scalar1=0.5, scalar2=None, op0=ALU.mult),
                 reads=[junk_k], writes=[srow_k])

        for blk in range(4):
            adaT_dma(blk, wTs[blk], f"wTs{blk}")
        gst = [at("gstA", [P, 8, 256], F32, RW + 3 * 16384), at("gstB", [P, 8, 256], F32, RW + 3 * 16384 + 8192)]

        S.op("dve", lambda e: e.tensor_copy(out=ident[:], in_=identf[:]), reads=["identf"], writes=["ident"])
        S.op("dve", lambda e: e.memset(ones[:], 1.0), writes=["ones"])
        S.op("dve", lambda e: e.memset(misc[:, 0:1], EPS), writes=["misc"])
        S.op("dve", lambda e: e.memset(xh[:], 0.0), writes=[f"xh{c}" for c in range(8)])
        S.op("dve", lambda e: e.memset(hst[:], 0.0), writes=[f"hst{c}" for c in range(8)])
        silu_row(srow, "srow", junkf, "junkf")
        for blk in range(4):
            adaT_acc(blk, wTs[blk], f"wTs{blk}", srow, "srow", junkf, "junkf")
        S.op("dve", lambda e: e.tensor_tensor(out=DV(DV_SH1), in0=dv[:, DV_ACC * 8:DV_ACC * 8 + 8], in1=V(V_BSH1), op=ALU.add),
             reads=["dvacc0", "dvacc1", "vecs"], writes=["dvmod0"])
        S.op("dve", lambda e: e.tensor_tensor(out=DV(DV_SC1), in0=dv[:, DV_ACC * 8 + 8:DV_ACC * 8 + 16], in1=V(V_BSC1), op=ALU.add),
             reads=["dvacc2", "dvacc3", "vecs"], writes=["dvmod1"])
        S.op("dve", lambda e: e.scalar_tensor_tensor(out=DV(DV_GM1), in0=DV(DV_SC1), scalar=1.0, in1=V(V_G1), op0=ALU.add, op1=ALU.mult),
             reads=["dvmod1", "vecs"], writes=["dvgm1"])
        S.op("dve", lambda e: e.tensor_scalar(out=DV(DV_HBA), in0=V(V_BRA), scalar1=0.5, scalar2=None, op0=ALU.mult), reads=["vecs"], writes=["dvhba"])
        S.op("dve", lambda e: e.tensor_scalar(out=DV(DV_HBX), in0=V(V_BRX), scalar1=0.5, scalar2=None, op0=ALU.mult), reads=["vecs"], writes=["dvhbx"])
        S.op("act", lambda e: e.activation(out=DV(DV_T0), in_=V(V_AP), func=AF.Abs), reads=["vecs"], writes=["dvt0"])
        S.op("act", lambda e: e.activation(out=DV(DV_T1), in_=DV(DV_T0), func=AF.Exp, scale=-1.0), reads=["dvt0"], writes=["dvt1"])
        S.op("act", lambda e: e.activation(out=DV(DV_T0), in_=DV(DV_T1), func=AF.Ln, bias=1.0, scale=1.0), reads=["dvt1"], writes=["dvt0"])
        S.op("dve", lambda e: e.scalar_tensor_tensor(out=DV(DV_SP), in0=V(V_AP), scalar=0.0, in1=DV(DV_T0), op0=ALU.max, op1=ALU.add),
             reads=["vecs", "dvt0"], writes=["dvsp"])
        S.op("dve", lambda e: e.tensor_scalar(out=DV(DV_M4), in0=DV(DV_SP), scalar1=-4.0, scalar2=None, op0=ALU.mult), reads=["dvsp"], writes=["dvm4"])
        S.op("dve", lambda e: e.tensor_scalar(out=DV(DV_M8), in0=DV(DV_SP), scalar1=-8.0, scalar2=None, op0=ALU.mult), reads=["dvsp"], writes=["dvm8"])

        hT_all = at("hT_all", [P, 8, NTOK], BF16, RHP)
        prodA = at("prodA", [P, 8, NTOK], BF16, RHP + 32768)
        hT_st = [at(f"hT_st{i}", [P, 8, TT], BF16, RHP + 32768 + i * 8192) for i in range(2)]
        xn = at("xn", [P, 4, D], BF16, RX + 16384)
        xr = at("xr", [P, 4, TT], F32, RX + 24576)
        thr = at("thr", [P, 4, TT], F32, RX + 32768)
        thi = at("thi", [P, 4, TT], F32, RX + 40960)
        a2 = at("a2", [P, 4, TT], F32, RX + 49152)
        t1 = at("t1", [P, 4, TT], F32, RX + 57344)
        xrb = at("xrb", [P, 4, TT], BF16, RW + 2 * 16384 + 12288)

        def norm_part1():
            for s in range(4):
                S.op("act", lambda e, s=s: e.activation(out=xn[:, s, :], in_=xt[:, s, :], func=AF.Square, accum_out=stat[:, s:s + 1]),
                     reads=["xt"], writes=[f"xn{s}", f"ss{s}"])
            S.op("act", lambda e: e.activation(out=stat[:, 4:8], in_=stat[:, 0:4], func=AF.Sqrt, scale=1.0 / D, bias=misc[:, 0:1]),
                 reads=["ss0", "ss1", "ss2", "ss3", "misc"], writes=["std"])
            S.op("dve", lambda e: e.reciprocal(out=stat[:, 8:12], in_=stat[:, 4:8]), reads=["std"], writes=["rstd"])
            for s in range(4):
                S.op("act", lambda e, s=s: e.activation(out=xn[:, s, :], in_=xt[:, s, :], func=AF.Identity, scale=stat[:, 8 + s:9 + s]),
                     reads=["xt", "rstd"], writes=[f"xn{s}"])

        def norm_part2(dst_fn, dst_keys):
            for c in range(8):
                b = nb()

                def tr(e, c=c, b=b):
                    ins = None
                    for s in range(4):
                        ins = e.transpose(out=psb[b][:, s * P:(s + 1) * P], in_=xn[:, s, c * P:(c + 1) * P], identity=ident[:])
                    return ins
                S.op("pe", tr, reads=["xn0", "xn1", "xn2", "xn3", "ident"], writes=[f"ps{b}"])
                S.op("act", lambda e, c=c, b=b: e.activation(out=dst_fn(c), in_=psb[b][:, 0:TT], func=AF.Identity, scale=DV(DV_GM1, c), bias=DV(DV_SH1, c)),
                     reads=[f"ps{b}", "dvgm1", "dvmod0"], writes=[dst_keys[c]])

        def hT_view(T):
            if T < 4:
                t_ = hT_st[T % 2]
                return (lambda c: t_[:, c, :]), [f"hTst{T % 2}_{c}" for c in range(8)]
            m0 = (T - 4) * TT
            return (lambda c: hT_all[:, c, m0:m0 + TT]), [f"hT{T - 4}_{c}" for c in range(8)]

        FN2 = vecs[:, V_FLAG * 8 + 2:V_FLAG * 8 + 3]

        def s12(T, blk, hview, hkeys, last=False):
            for jj_ in range(2):
                i = 2 * (blk % 2) + jj_
                c = 2 * blk + jj_
                b = nb()

                def mmx(e, c=c, b=b):
                    ins = None
                    for k in range(8):
                        ins = e.matmul(ps[b][:, :], lhsT=slots[0][:, k, c * P:(c + 1) * P], rhs=hview(k), start=(k == 0), stop=(k == 7))
                    return ins
                S.op("pe", mmx, reads=["S0"] + hkeys, writes=[f"ps{b}"])
                S.op("act", lambda e, i=i, b=b, c=c: e.activation(out=xr[:, i, :], in_=ps[b][:, :], func=AF.Identity, scale=V(V_CW3, c), bias=V(V_CB, c)),
                     reads=[f"ps{b}", "vecs"], writes=[f"xr{i}"])
                for k in (2, 1, 0):
                    sh = 3 - k
                    S.op("dve", lambda e, i=i, c=c, k=k, sh=sh, b=b: e.scalar_tensor_tensor(
                        out=xr[:, i, sh:TT], in0=ps[b][:, 0:TT - sh], scalar=V(V_CW0 + k, c), in1=xr[:, i, sh:TT], op0=ALU.mult, op1=ALU.add),
                        reads=[f"ps{b}", f"xr{i}", "vecs"], writes=[f"xr{i}"])
                    S.op("dve", lambda e, i=i, c=c, k=k, sh=sh: e.scalar_tensor_tensor(
                        out=xr[:, i, 0:sh], in0=xh[:, c, 3 - sh:3], scalar=V(V_CW0 + k, c), in1=xr[:, i, 0:sh], op0=ALU.mult, op1=ALU.add),
                        reads=[f"xh{c}", f"xr{i}", "vecs"], writes=[f"xr{i}"])
                S.op("dve", lambda e, c=c, b=b: e.tensor_copy(out=xh[:, c, 0:3], in_=ps[b][:, TT - 3:TT]), reads=[f"ps{b}", f"xh{c}"], writes=[f"xh{c}"])
            i0 = 2 * (blk % 2)
            S.op("dve", lambda e, i0=i0: e.tensor_copy(out=xrb[:, i0:i0 + 2, :], in_=xr[:, i0:i0 + 2, :]),
                 reads=[f"xr{i0}", f"xr{i0 + 1}"], writes=[f"xrb{i0}", f"xrb{i0 + 1}"])
            if last:
                pending.extend(make_stream(slots[0], "S0", w_in[:, 4 * D:5 * D], stA))

        def s34(T, blk):
            for jj_ in range(2):
                i = 2 * (blk % 2) + jj_
                c = 2 * blk + jj_
                bl, j = blk % 2, jj_
                for (wg, dst, hb, dk) in ((wa_g, thr, DV_HBA, "thr"), (wi_g, thi, DV_HBX, "thi")):
                    b = nb()

                    def mmgate(e, j=j, b=b, wg=wg, blk=blk, bl=bl):
                        ins = None
                        for jj in range(2):
                            ins = e.matmul(ps[b][:, :], lhsT=wg[:, 2 * blk + jj, j * P:(j + 1) * P], rhs=xrb[:, 2 * bl + jj, :], start=(jj == 0), stop=(jj == 1))
                        return ins
                    S.op("pe", mmgate, reads=["S2a", "S2b", f"xrb{2 * bl}", f"xrb{2 * bl + 1}"], writes=[f"ps{b}"])
                    S.op("act", lambda e, i=i, b=b, dst=dst, hb=hb, c=c: e.activation(out=dst[:, i, :], in_=ps[b][:, :], func=AF.Tanh,
                                                                                      scale=0.5, bias=DV(hb, c)),
                         reads=[f"ps{b}", "dvhba", "dvhbx"], writes=[f"{dk}{i}"])
            for jj_ in range(2):
                i = 2 * (blk % 2) + jj_
                c = 2 * blk + jj_
                S.op("act", lambda e, i=i, c=c: e.activation(out=thr[:, i, :], in_=thr[:, i, :], func=AF.Exp, scale=DV(DV_M4, c), bias=DV(DV_M4, c)),
                     reads=[f"thr{i}", "dvm4"], writes=[f"thr{i}"])
            i0 = 2 * (blk % 2)
            S.op("dve", lambda e, i0=i0: e.scalar_tensor_tensor(out=t1[:, i0:i0 + 2, :], in0=thi[:, i0:i0 + 2, :], scalar=1.0, in1=xr[:, i0:i0 + 2, :],
                                                                op0=ALU.add, op1=ALU.mult),
                 reads=[f"thi{i0}", f"thi{i0 + 1}", f"xr{i0}", f"xr{i0 + 1}"], writes=[f"t1{i0}", f"t1{i0 + 1}"])
            S.op("pool", lambda e, i0=i0: e.tensor_tensor(out=a2[:, i0:i0 + 2, :], in0=thr[:, i0:i0 + 2, :], in1=thr[:, i0:i0 + 2, :], op=ALU.mult),
                 reads=[f"thr{i0}", f"thr{i0 + 1}"], writes=[f"a2{i0}", f"a2{i0 + 1}"])

        def s56(T, blk, hview, hkeys):
            main = T >= 4
            mt = T - 4
            m0 = mt * TT
            i0 = 2 * (blk % 2)
            c0 = 2 * blk
            ak = [f"a2{i0}", f"a2{i0 + 1}"]
            tk = [f"t1{i0}", f"t1{i0 + 1}"]
            S.op("act", lambda e, i0=i0: e.activation(out=a2[:, i0:i0 + 2, :], in_=a2[:, i0:i0 + 2, :], func=AF.Sqrt, scale=-0.25, bias=0.25),
                 reads=ak, writes=ak)
            if T == 0:
                S.op("dve", lambda e, i0=i0: e.memset(a2[:, i0:i0 + 2, 0:1], 0.5), reads=ak, writes=ak)
            elif T == 4:
                S.op("dve", lambda e, i0=i0: e.tensor_scalar(out=a2[:, i0:i0 + 2, 0:1], in0=a2[:, i0:i0 + 2, 0:1], scalar1=FM, scalar2=FN2, op0=ALU.mult, op1=ALU.add),
                     reads=ak + ["vecs"], writes=ak)
            S.op("pool", lambda e, i0=i0: e.tensor_tensor(out=t1[:, i0:i0 + 2, :], in0=t1[:, i0:i0 + 2, :], in1=a2[:, i0:i0 + 2, :], op=ALU.mult),
                 reads=tk + ak, writes=tk)
            for jj_ in range(2):
                i = i0 + jj_
                c = c0 + jj_
                S.op("dve", lambda e, i=i, c=c: e.tensor_tensor_scan(out=a2[:, i, :], data0=thr[:, i, :], data1=t1[:, i, :], initial=hst[:, c:c + 1],
                                                                     op0=ALU.mult, op1=ALU.add),
                     reads=[f"thr{i}", f"t1{i}", f"hst{c}", f"a2{i}"], writes=[f"a2{i}"])
            S.op("pool", lambda e, i0=i0, c0=c0: e.tensor_copy(out=hst[:, c0:c0 + 2], in_=a2[:, i0:i0 + 2, TT - 1]),
                 reads=ak, writes=[f"hst{c0}", f"hst{c0 + 1}"])
            if main:
                for jj_ in range(2):
                    i = 2 * (blk % 2) + jj_
                    c = 2 * blk + jj_
                    b = nb()

                    def mmy(e, c=c, b=b):
                        ins = None
                        for k in range(8):
                            ins = e.matmul(ps[b][:, :], lhsT=slots[1][:, k, c * P:(c + 1) * P], rhs=hview(k), start=(k == 0), stop=(k == 7))
                        return ins
                    S.op("pe", mmy, reads=["S1"] + hkeys, writes=[f"ps{b}"])
                    S.op("act", lambda e, i=i, b=b: e.activation(out=thi[:, i, :], in_=ps[b][:, :], func=AF.Gelu_apprx_tanh),
                         reads=[f"ps{b}"], writes=[f"thi{i}"])
                S.op("pool", lambda e, i0=i0, c0=c0: e.tensor_tensor(out=prodA[:, c0:c0 + 2, m0:m0 + TT], in0=thi[:, i0:i0 + 2, :], in1=a2[:, i0:i0 + 2, :], op=ALU.mult),
                     reads=[f"thi{i0}", f"thi{i0 + 1}"] + ak, writes=[f"prodA{mt}_{c0}", f"prodA{mt}_{c0 + 1}"])

        norm_part1()
        st8 = [(at(f"stS{i}", [P, D], F32, RX + 24576 + i * 4096), f"stS{i}") for i in range(8)]
        load_cast(slots[0], "S0", w_in[:, 0:D], stages=st8, eng="dve")
        for gi_, (gsrc_, gk_) in enumerate(((w_ra, "gstA"), (w_rx, "gstB"))):
            S.dma("sp", lambda e, gi_=gi_, gsrc_=gsrc_: e.dma_start(out=gst[gi_][:], in_=gsrc_.rearrange("(s p) n -> p s n", p=P)), gk_, writes=[gk_])
        S.op("dve", lambda e: e.tensor_copy(out=wa_g[:], in_=gst[0][:]), reads=["gstA"], writes=["S2a"])
        S.op("dve", lambda e: e.tensor_copy(out=wi_g[:], in_=gst[1][:]), reads=["gstB"], writes=["S2b"])
        norm_load(1)
        pending.extend(make_stream(slots[1], "S1", w_in[:, D:2 * D], stA, extra=["wTs3"]))
        pending.extend(make_stream(slots[3], "S3", w_in[:, 2 * D:3 * D], stA, extra=["gstA", "gstB"]))
        d0, k0 = hT_view(0)
        norm_part2(d0, k0)
        seq = [(T, blk) for T in range(8) for blk in range(4)]
        s12(0, 0, d0, k0)
        s12(0, 1, d0, k0)
        for n, (T, blk) in enumerate(seq):
            dst_fn, dkeys = hT_view(T)
            if (T, blk) == (3, 0):
                S.op("pool", lambda e: e.tensor_copy(out=hTh[:], in_=hT_st[1][:, :, TT - 16:TT]), reads=dkeys, writes=["hTh"])
            if blk == 1 and T + 1 < 8:
                dn_, kn_ = hT_view(T + 1)
                norm_part2(dn_, kn_)
            s34(T, blk)
            if n < 20 or n >= 29:
                tick(1)
            if n == 20:
                flush()
                load_small(wpl, "S2c", w_pl)
            if blk == 0 and T + 1 < 8:
                norm_part1()
                if T + 2 < 8:
                    norm_load(T + 2)
            if n + 2 < len(seq):
                T2, b2 = seq[n + 2]
                dn, kn = hT_view(T2)
                if b2 == 0 and T2 == 4:
                    S.op("dve", lambda e: e.tensor_scalar(out=xh[:], in0=xh[:], scalar1=FM, scalar2=None, op0=ALU.mult),
                         reads=[f"xh{c}" for c in range(8)] + ["vecs"], writes=[f"xh{c}" for c in range(8)])
                s12(T2, b2, dn, kn, last=(n + 2 == len(seq) - 1))
            if n >= 29:
                tick(1)
            s56(T, blk, dst_fn, dkeys)
            if n >= 29:
                tick(1)
            if (T, blk) == (3, 3):
                S.op("dve", lambda e: e.tensor_scalar(out=hst[:], in0=hst[:], scalar1=FM, scalar2=None, op0=ALU.mult),
                     reads=[f"hst{c}" for c in range(8)] + ["vecs"], writes=[f"hst{c}" for c in range(8)])
        pending.extend(make_stream(slots[1], "S1", w_b, stA))

        S.barrier()

        mB = at("mB", [P, 8, NTOK], BF16, RX + 0)
        usb = [at("usA", [P, 2, 528], F32, RX + 32768), at("usB", [P, 2, 528], F32, RX + 36992)]
        sa = at("sa", [P, 2, 528], F32, RX + 41216)
        sb_ = at("sb", [P, 2, 528], F32, RX + 45440)
        tmp16 = at("tmp16", [P, 2, 16], F32, RX + 49664)
        pmb = at("pmb", [P, 2, TT], BF16, RX + 49792)
        poolB = at("poolB", [P, 8, TT], BF16, RX + 51840)
        thbs = [at("thb0", [P, TT], F32, RX + 60032), at("thb1", [P, TT], F32, RX + 62080)]

        wTb2 = at("wTb2", [P, 2, D], F32, RW + 2 * 16384)
        srow2 = at("srow2", [P, D], F32, RW + 2 * 16384 + 12288)
        junkb2 = at("junkb2", [P, D], BF16, RS + 13824)
        S.dma("sp", lambda e: e.dma_start(out=srow2[:], in_=rows_d[:, 0, :]), "c_srow2", writes=["srow2", "S2a", "S2b"] + [f"xrb{i}" for i in range(4)])

        def ob_init():
            silu_row(srow2, "srow2", wTb2[:, 0, :], "wTb2")

        def ob_dma(b2):
            S.dma("sp", lambda e: e.dma_start(out=wTb2[:], in_=w_adaT[2 * D + b2 * 256:2 * D + (b2 + 1) * 256, :].rearrange("(s p) k -> p s k", p=P)),
                  "wTb2", writes=["wTb2"])

        def ob_acc(b2):
            for s2 in range(2):
                col = DV_ACC * 8 + 16 + b2 * 2 + s2
                S.op("dve", lambda e, s2=s2, col=col: e.scalar_tensor_tensor(
                    out=junkb2[:], in0=wTb2[:, s2, :], scalar=1.0, in1=srow2[:], op0=ALU.mult, op1=ALU.mult,
                    accum_out=dv[:, col:col + 1]), reads=["wTb2", "srow2"], writes=["junkb2", "dvaccB"])

        ob_state = {"next": 0}

        def ob_step():
            b2 = ob_state["next"]
            if b2 > 16:
                return
            if b2 >= 1:
                ob_acc(b2 - 1)
            if b2 < 16:
                ob_dma(b2)
            ob_state["next"] = b2 + 1
        for c in range(8):
            b = nb()

            def mmh(e, c=c, b=b):
                ins = None
                for k in range(8):
                    ins = e.matmul(ps[b][:, 0:16], lhsT=slots[3][:, k, c * P:(c + 1) * P], rhs=hTh[:, k, :], start=(k == 0), stop=(k == 7))
                return ins
            S.op("pe", mmh, reads=["S3", "hTh"], writes=[f"ps{b}"])
            S.op("dve", lambda e, c=c, b=b: e.tensor_scalar(out=uh[:, c, :], in0=ps[b][:, 0:16], scalar1=FM, scalar2=None, op0=ALU.mult),
                 reads=[f"ps{b}", "vecs"], writes=["uh"])

        def uproj(mt_, blk_):
            m0_ = mt_ * TT
            ust = usb[blk_ % 2]
            usk = f"us{blk_ % 2}"
            for j in range(2):
                c = 2 * blk_ + j
                b = nb()

                def mmu(e, c=c, b=b, m0_=m0_):
                    ins = None
                    for k in range(8):
                        ins = e.matmul(ps[b][:, :], lhsT=slots[3][:, k, c * P:(c + 1) * P], rhs=hT_all[:, k, m0_:m0_ + TT], start=(k == 0), stop=(k == 7))
                    return ins
                S.op("pe", mmu, reads=["S3"] + [f"hT{mt_}_{k}" for k in range(8)], writes=[f"ps{b}"])
                S.op("pool", lambda e, j=j, c=c: e.tensor_copy(out=ust[:, j, 0:16], in_=uh[:, c, :]), reads=["uh"], writes=[usk])
                S.op("act", lambda e, j=j, b=b: e.activation(out=ust[:, j, 16:16 + TT], in_=ps[b][:, :], func=AF.Identity),
                     reads=[f"ps{b}"], writes=[usk])
                S.op("pool", lambda e, j=j, c=c: e.tensor_copy(out=uh[:, c, :], in_=ust[:, j, TT:TT + 16]), reads=[usk], writes=["uh"])

        S.op("dve", lambda e: e.tensor_tensor(out=dv[:, DV_ACC * 8:DV_ACC * 8 + 8], in0=V(V_BP), in1=V(V_PS), op=ALU.mult),
             reads=["vecs", "dvacc0", "dvacc1"], writes=["dvbps", "dvacc0", "dvacc1"])
        uproj(0, 0)
        for mt in range(4):
            m0 = mt * TT
            hkeys = [f"hT{mt}_{c}" for c in range(8)]
            for blk in range(4):
                win = 2 ** (blk + 1)
                us = usb[blk % 2]
                usk = f"us{blk % 2}"
                if blk < 3:
                    uproj(mt, blk + 1)
                elif mt < 3:
                    uproj(mt + 1, 0)
                tick(1)
                src, srck = us, usk
                bufs = [(sa, "sa"), (sb_, "sb")]
                for lvl in range(blk + 1):
                    sh = 2 ** lvl
                    lo = 2 ** (lvl + 1) - 1
                    dstt, dstk = bufs[lvl % 2]
                    S.op("dve", lambda e, src=src, dstt=dstt, sh=sh, lo=lo: e.tensor_tensor(
                        out=dstt[:, :, lo:528], in0=src[:, :, lo:528], in1=src[:, :, lo - sh:528 - sh], op=ALU.add),
                        reads=[srck], writes=[dstk])
                    src, srck = dstt, dstk
                for j in range(2):
                    c = 2 * blk + j
                    S.op("dve", lambda e, j=j, src=src, win=win, us=us: e.scalar_tensor_tensor(
                        out=pmb[:, j, :], in0=src[:, j, 16:16 + TT], scalar=1.0 / win, in1=us[:, j, 16:16 + TT], op0=ALU.mult, op1=ALU.subtract),
                        reads=[srck, usk], writes=[f"pmb{j}"])
                    if mt == 0:
                        S.op("dve", lambda e, j=j, src=src, blk=blk: e.tensor_tensor(out=tmp16[:, j, :], in0=src[:, j, 16:32], in1=invc[:, blk, :], op=ALU.mult),
                             reads=[srck, "invc"], writes=["tmp16"])
                        S.op("dve", lambda e, j=j, us=us: e.tensor_tensor(out=pmb[:, j, 0:16], in0=tmp16[:, j, :], in1=us[:, j, 16:32], op=ALU.subtract),
                             reads=["tmp16", usk], writes=[f"pmb{j}"])
                tick(1)
                for j in range(2):
                    c = 2 * blk + j
                    b = nb()

                    def mmp(e, j=j, b=b, blk=blk):
                        ins = None
                        for jj in range(2):
                            ins = e.matmul(ps[b][:, :], lhsT=wpl[:, 2 * blk + jj, j * P:(j + 1) * P], rhs=pmb[:, jj, :], start=(jj == 0), stop=(jj == 1))
                        return ins
                    S.op("pe", mmp, reads=["S2c", "pmb0", "pmb1"], writes=[f"ps{b}"])
                    S.op("act", lambda e, c=c, b=b: e.activation(out=poolB[:, c, :], in_=ps[b][:, :], func=AF.Identity,
                                                                 scale=V(V_PS, c), bias=dv[:, DV_ACC * 8 + c:DV_ACC * 8 + c + 1]),
                         reads=[f"ps{b}", "vecs", "dvbps"], writes=[f"poolB{c}"])
                tick(1)
                if mt == 0 and blk == 1:
                    ob_init()
                if mt < 3 and (mt, blk) >= (0, 1):
                    ob_step()
                tick(1)
            flush()
            if mt == 3:
                while ob_state["next"] <= 16:
                    ob_step()
                pending.extend(make_stream(slots[3], "S3", w_in[:, 3 * D:4 * D], stA))
                pending.extend(make_stream(slots[2], "S2", w_a, stA, extra=["S2a", "S2b", "S2c", "wTb2", "srow2"]))
            for c in range(8):
                tick(2)
                if mt < 3 and c % 2 == 1:
                    ob_step()
                b = nb()

                def mmgb(e, c=c, b=b, m0=m0):
                    ins = None
                    for k in range(8):
                        ins = e.matmul(ps[b][:, :], lhsT=slots[0][:, k, c * P:(c + 1) * P], rhs=hT_all[:, k, m0:m0 + TT], start=(k == 0), stop=(k == 7))
                    return ins
                S.op("pe", mmgb, reads=["S0"] + hkeys, writes=[f"ps{b}"])
                thb = thbs[c % 2]
                thk = f"thb{c % 2}"
                S.op("act", lambda e, b=b, thb=thb: e.activation(out=thb[:], in_=ps[b][:, :], func=AF.Tanh, scale=0.5), reads=[f"ps{b}"], writes=[thk])
                b2 = nb()

                def mmB(e, c=c, b2=b2):
                    ins = None
                    for k in range(8):
                        ins = e.matmul(ps[b2][:, :], lhsT=slots[1][:, k, c * P:(c + 1) * P], rhs=poolB[:, k, :], start=(k == 0), stop=(k == 7))
                    return ins
                S.op("pe", mmB, reads=["S1"] + [f"poolB{k}" for k in range(8)], writes=[f"ps{b2}"])
                S.op("dve", lambda e, c=c, b2=b2, m0=m0, thb=thb: e.scalar_tensor_tensor(out=mB[:, c, m0:m0 + TT], in0=thb[:], scalar=1.0, in1=ps[b2][:, :],
                                                                                       op0=ALU.add, op1=ALU.mult),
                     reads=[thk, f"ps{b2}"], writes=[f"mB{mt}_{c}"])

        flush()
        A0 = DV_ACC * 8

        def bcast_gate(gcol, gck, gdst, gkey):
            for half in range(2):
                b = nb()
                for cc in range(4):
                    c = half * 4 + cc
                    S.op("dve", lambda e, c=c, cc=cc: e.tensor_scalar(out=dgs[cc][:], in0=identf[:], scalar1=DV(gcol, c), scalar2=None, op0=ALU.mult),
                         reads=["identf", gck], writes=[f"dg{cc}", "junkb2"])
                for cc in range(4):
                    S.op("pe", lambda e, b=b, cc=cc: e.matmul(ps[b][:, cc * P:(cc + 1) * P], lhsT=ones[:], rhs=dgs[cc][:], start=True, stop=True),
                         reads=["ones", f"dg{cc}"], writes=[f"ps{b}"])
                S.op("dve", lambda e, b=b, half=half: e.tensor_copy(out=gdst[:, half * 512:(half + 1) * 512], in_=ps[b][:, :]),
                     reads=[f"ps{b}"], writes=[gkey])

        dgs = [dg] + [at(f"dg{i}", [P, P], F32, RS + 13824 + 2048 - 512 * (4 - i)) for i in range(1, 4)]
        S.op("dve", lambda e: e.tensor_tensor(out=DV(DV_T0), in0=dv[:, A0 + 16:A0 + 24], in1=V(V_BGT1), op=ALU.add),
             reads=["dvaccB", "vecs", "dvt0"], writes=["dvt0"])
        S.op("dve", lambda e: e.tensor_scalar(out=DV(DV_T0), in0=DV(DV_T0), scalar1=0.5, scalar2=None, op0=ALU.mult), reads=["dvt0"], writes=["dvt0"])
        S.op("dve", lambda e: e.tensor_tensor(out=DV(DV_SH2), in0=dv[:, A0 + 24:A0 + 32], in1=V(V_BSH2), op=ALU.add),
             reads=["dvaccB", "vecs"], writes=["dvsh2"])
        S.op("dve", lambda e: e.tensor_tensor(out=DV(DV_SC2), in0=dv[:, A0 + 32:A0 + 40], in1=V(V_BSC2), op=ALU.add),
             reads=["dvaccB", "vecs"], writes=["dvsc2"])
        S.op("dve", lambda e: e.scalar_tensor_tensor(out=DV(DV_GM2), in0=DV(DV_SC2), scalar=1.0, in1=V(V_G2), op0=ALU.add, op1=ALU.mult),
             reads=["dvsc2", "vecs"], writes=["dvgm2"])
        S.op("dve", lambda e: e.tensor_tensor(out=DV(DV_T1), in0=dv[:, A0 + 40:A0 + 48], in1=V(V_BGT2), op=ALU.add),
             reads=["dvaccB", "vecs", "dvt1"], writes=["dvt1"])
        bcast_gate(DV_T0, "dvt0", gtb1, "gtb1")
        bcast_gate(DV_T1, "dvt1", gtb2, "gtb2")
        S.barrier()


        tha = [at(f"tha{i}", [P, TT], F32, RX + 32768 + i * 2048) for i in range(2)]
        tt_ = [at(f"tt{i}", [P, TT], F32, RX + 36864 + i * 2048) for i in range(2)]
        stC = [(at("stC0", [P, D], F32, RX + 40960), "stC0"), (at("stC1", [P, D], F32, RX + 45056), "stC1")]
        pending.extend(make_stream(slots[0], "S0", w_o, stC, gtb1, "gtb1"))
        pending.extend(make_stream(slots[1], "S1", w_up[:, 0:D], stC))
        xt2 = at("xt2", [P, 4, D], F32, RX + 49152)

        def load_xt2(mt_):
            r0_ = NTOK + mt_ * TT
            S.dma("sp", lambda e: e.dma_start(out=xt2[:], in_=xin[r0_:r0_ + TT, :].rearrange("(s p) d -> p s d", p=P)), "xt2", writes=["xt2"])

        for mt in range(4):
            m0 = mt * TT
            hkeys = [f"hT{mt}_{c}" for c in range(8)]
            if mt == 3:
                load_xt2(0)
            for c in range(8):
                i2 = c % 2
                b = nb()

                def mmga(e, c=c, b=b, m0=m0):
                    ins = None
                    for k in range(8):
                        ins = e.matmul(ps[b][:, :], lhsT=slots[3][:, k, c * P:(c + 1) * P], rhs=hT_all[:, k, m0:m0 + TT], start=(k == 0), stop=(k == 7))
                    return ins
                S.op("pe", mmga, reads=["S3"] + hkeys, writes=[f"ps{b}"])
                S.op("act", lambda e, b=b, i2=i2: e.activation(out=tha[i2][:], in_=ps[b][:, :], func=AF.Tanh, scale=0.5), reads=[f"ps{b}"], writes=[f"tha{i2}"])
                b2 = nb()

                def mmA(e, c=c, b2=b2, m0=m0):
                    ins = None
                    for k in range(8):
                        ins = e.matmul(ps[b2][:, :], lhsT=slots[2][:, k, c * P:(c + 1) * P], rhs=prodA[:, k, m0:m0 + TT], start=(k == 0), stop=(k == 7))
                    return ins
                S.op("pe", mmA, reads=["S2"] + [f"prodA{mt}_{k}" for k in range(8)], writes=[f"ps{b2}"])
                S.op("dve", lambda e, i2=i2, b2=b2: e.scalar_tensor_tensor(out=tt_[i2][:], in0=tha[i2][:], scalar=1.0, in1=ps[b2][:, :], op0=ALU.add, op1=ALU.mult),
                     reads=[f"tha{i2}", f"ps{b2}"], writes=[f"tt{i2}"])
                S.op("pool", lambda e, c=c, i2=i2, m0=m0: e.tensor_tensor(out=mB[:, c, m0:m0 + TT], in0=tt_[i2][:], in1=mB[:, c, m0:m0 + TT], op=ALU.add),
                     reads=[f"tt{i2}", f"mB{mt}_{c}"], writes=[f"mB{mt}_{c}"])
                tick()
        flush()

        S.barrier()

        x1 = at("x1", [P, 16, D], F32, RHP)
        h2T = at("h2T", [P, 8, NTOK], BF16, RX + 0)
        stg = [at(f"stg{i}", [P, D], F32, RX + 32768 + i * 4096) for i in range(2)]
        xn2 = at("xn2", [P, 4, D], BF16, RX + 40960)
        STG = [(stg[0], "stg0"), (stg[1], "stg1")]

        pending += make_stream(slots[3], "S3", w_dn[0:D, :], STG, gtb2, "gtb2")
        pending += make_stream(slots[2], "S2", w_up[:, D:2 * D], STG)

        def norm2_part1(mt):
            for s in range(4):
                S.op("act", lambda e, s=s: e.activation(out=xn2[:, s, :], in_=x1[:, mt * 4 + s, :], func=AF.Square, accum_out=stat[:, 16 + s:17 + s]),
                     reads=[f"x1_{mt * 4 + s}"], writes=[f"xn2_{s}", f"ss2_{s}"])
            S.op("act", lambda e: e.activation(out=stat[:, 20:24], in_=stat[:, 16:20], func=AF.Sqrt, scale=1.0 / D, bias=misc[:, 0:1]),
                 reads=["ss2_0", "ss2_1", "ss2_2", "ss2_3", "misc"], writes=["std2"])
            S.op("dve", lambda e: e.reciprocal(out=stat[:, 24:28], in_=stat[:, 20:24]), reads=["std2"], writes=["rstd2"])
            for s in range(4):
                S.op("act", lambda e, s=s: e.activation(out=xn2[:, s, :], in_=x1[:, mt * 4 + s, :], func=AF.Identity, scale=stat[:, 24 + s:25 + s]),
                     reads=[f"x1_{mt * 4 + s}", "rstd2"], writes=[f"xn2_{s}"])

        def norm2_part2(mt):
            m0 = mt * TT
            for c in range(8):
                b = nb()

                def tr2(e, c=c, b=b):
                    ins = None
                    for s in range(4):
                        ins = e.transpose(out=psb[b][:, s * P:(s + 1) * P], in_=xn2[:, s, c * P:(c + 1) * P], identity=ident[:])
                    return ins
                S.op("pe", tr2, reads=["xn2_0", "xn2_1", "xn2_2", "xn2_3", "ident"], writes=[f"ps{b}"])
                S.op("act", lambda e, c=c, b=b: e.activation(out=h2T[:, c, m0:m0 + TT], in_=psb[b][:, 0:TT], func=AF.Identity, scale=DV(DV_GM2, c), bias=DV(DV_SH2, c)),
                     reads=[f"ps{b}", "dvgm2", "dvsh2"], writes=[f"h2T{mt}_{c}", f"mB{mt}_{c}"])

        for mt in range(4):
            m0 = mt * TT
            r0 = NTOK + m0
            if mt > 0:
                load_xt2(mt)
            for s in range(4):
                for half in range(2):
                    b = nb()

                    def mmo(e, s=s, half=half, b=b, m0=m0):
                        ins = None
                        for k in range(8):
                            ins = e.matmul(ps[b][:, :], lhsT=mB[:, k, m0 + s * P:m0 + (s + 1) * P], rhs=slots[0][:, k, half * 512:(half + 1) * 512],
                                           start=(k == 0), stop=(k == 7))
                        return ins
                    S.op("pe", mmo, reads=["S0"] + [f"mB{mt}_{k}" for k in range(8)], writes=[f"ps{b}"])
                    S.op("dve", lambda e, s=s, half=half, b=b, mt=mt: e.tensor_tensor(
                        out=x1[:, mt * 4 + s, half * 512:(half + 1) * 512], in0=ps[b][:, :], in1=xt2[:, s, half * 512:(half + 1) * 512], op=ALU.add),
                        reads=[f"ps{b}", "xt2"], writes=[f"x1_{mt * 4 + s}"])
                    tick()
            if mt > 0:
                norm2_part2(mt - 1)
            norm2_part1(mt)
        norm2_part2(3)
        flush()

        S.barrier()

        ff = [at(f"ff{i}", [P, 8, TT], BF16, RX + 32768 + i * 8192) for i in range(2)]
        rr = [at(f"rr{i}", [P, TT], F32, RX + 49152 + i * 2048) for i in range(2)]
        stgE = [at("stgE0", [P, D], F32, RX + 53248), at("stgE1", [P, D], F32, RX + 57344)]
        junk2 = at("junk2", [P, D], BF16, RX + 61440)
        STGE = [(stgE[0], "stgE0"), (stgE[1], "stgE1")]
        up_slot = {0: 1, 1: 2, 2: 1, 3: 2}
        dn_slot = {0: 3, 1: 0, 2: 3, 3: 0}

        def queue_quarter(q, up=True):
            if up:
                pending.extend(make_stream(slots[up_slot[q]], f"S{up_slot[q]}", w_up[:, q * D:(q + 1) * D], [(gtb1, "gtb1")]))
            pending.extend(make_stream(slots[dn_slot[q]], f"S{dn_slot[q]}", w_dn[q * D:(q + 1) * D, :], STGE, gtb2, "gtb2"))

        fgb = stgE[0]

        def final_norm_tile(mt):
            for s4 in range(4):
                s16 = mt * 4 + s4
                S.op("act", lambda e, s16=s16, s4=s4: e.activation(out=junk2[:], in_=x1[:, s16, :], func=AF.Square, accum_out=stat[:, 32 + s4:33 + s4]),
                     reads=[f"x1_{s16}"], writes=["junk2", f"ss3_{s4}"])
            S.op("act", lambda e: e.activation(out=stat[:, 40:44], in_=stat[:, 32:36], func=AF.Sqrt, scale=1.0 / D, bias=misc[:, 0:1]),
                 reads=["ss3_0", "ss3_1", "ss3_2", "ss3_3", "misc"], writes=["std3"])
            S.op("dve", lambda e: e.reciprocal(out=stat[:, 44:48], in_=stat[:, 40:44]), reads=["std3"], writes=["rstd3"])
            for s4 in range(4):
                s16 = mt * 4 + s4
                S.op("dve", lambda e, s16=s16, s4=s4: e.scalar_tensor_tensor(out=x1[:, s16, :], in0=x1[:, s16, :], scalar=stat[:, 44 + s4:45 + s4], in1=fgb[:],
                                                                             op0=ALU.mult, op1=ALU.mult),
                     reads=[f"x1_{s16}", "rstd3", "fgb"], writes=[f"x1_{s16}"])
                S.dma("sp", lambda e, s16=s16: e.dma_start(out=yout[s16 * P:(s16 + 1) * P, :], in_=x1[:, s16, :]), "yo", reads=[f"x1_{s16}"], writes=[f"y_{s16}"])

        def final_norm_sub(mt, s4):
            s16 = mt * 4 + s4
            S.op("act", lambda e: e.activation(out=junk2[:], in_=x1[:, s16, :], func=AF.Square, accum_out=stat[:, 32 + s4:33 + s4]),
                 reads=[f"x1_{s16}"], writes=["junk2", f"ss3_{s4}"])
            S.op("act", lambda e: e.activation(out=stat[:, 40 + s4:41 + s4], in_=stat[:, 32 + s4:33 + s4], func=AF.Sqrt, scale=1.0 / D, bias=misc[:, 0:1]),
                 reads=[f"ss3_{s4}", "misc"], writes=[f"std3_{s4}", "std3"])
            S.op("dve", lambda e: e.reciprocal(out=stat[:, 44 + s4:45 + s4], in_=stat[:, 40 + s4:41 + s4]), reads=[f"std3_{s4}"], writes=[f"rstd3_{s4}", "rstd3"])
            S.op("dve", lambda e: e.scalar_tensor_tensor(out=x1[:, s16, :], in0=x1[:, s16, :], scalar=stat[:, 44 + s4:45 + s4], in1=fgb[:],
                                                         op0=ALU.mult, op1=ALU.mult),
                 reads=[f"x1_{s16}", f"rstd3_{s4}", "fgb"], writes=[f"x1_{s16}"])
            S.dma("sp", lambda e: e.dma_start(out=yout[s16 * P:(s16 + 1) * P, :], in_=x1[:, s16, :]), "yo", reads=[f"x1_{s16}"], writes=[f"y_{s16}"])

        queue_quarter(1, up=False)
        for q in range(4):
            if 1 <= q and q + 1 < 4:
                queue_quarter(q + 1)
            if q == 3:
                S.dma("sp", lambda e: e.dma_start(out=fgb[:], in_=rows_d[:, 1, :]), "st_stgE0", reads=[], writes=["stgE0", "fgb"])
            us_, ds_ = up_slot[q], dn_slot[q]

            def ffn_up(mt, us_=us_):
                m0 = mt * TT
                fb = ff[mt % 2]
                fk = f"ff{mt % 2}"
                for j in range(8):
                    b = nb()
                    i2 = j % 2

                    def mmup(e, j=j, b=b, m0=m0, us_=us_):
                        ins = None
                        for k in range(8):
                            ins = e.matmul(ps[b][:, :], lhsT=slots[us_][:, k, j * P:(j + 1) * P], rhs=h2T[:, k, m0:m0 + TT], start=(k == 0), stop=(k == 7))
                        return ins
                    S.op("pe", mmup, reads=[f"S{us_}"] + [f"h2T{mt}_{k}" for k in range(8)], writes=[f"ps{b}"])
                    S.op("act", lambda e, b=b, i2=i2: e.activation(out=rr[i2][:], in_=ps[b][:, :], func=AF.Relu), reads=[f"ps{b}"], writes=[f"rr{i2}"])
                    S.op("act", lambda e, j=j, i2=i2, fb=fb: e.activation(out=fb[:, j, :], in_=rr[i2][:], func=AF.Square), reads=[f"rr{i2}"], writes=[f"{fk}_{j}"])
                    if j % 2 == 1:
                        tick()

            def ffn_dn(mt, ds_=ds_, q=q):
                fb = ff[mt % 2]
                fk = f"ff{mt % 2}"
                for s in range(4):
                    for half in range(2):
                        b = nb()

                        def mmdn(e, s=s, half=half, b=b, fb=fb, ds_=ds_):
                            ins = None
                            for j in range(8):
                                ins = e.matmul(ps[b][:, :], lhsT=fb[:, j, s * P:(s + 1) * P], rhs=slots[ds_][:, j, half * 512:(half + 1) * 512],
                                               start=(j == 0), stop=(j == 7))
                            return ins
                        S.op("pe", mmdn, reads=[f"S{ds_}"] + [f"{fk}_{j}" for j in range(8)], writes=[f"ps{b}"])
                        S.op("dve", lambda e, s=s, half=half, b=b, mt=mt: e.tensor_tensor(
                            out=x1[:, mt * 4 + s, half * 512:(half + 1) * 512], in0=ps[b][:, :], in1=x1[:, mt * 4 + s, half * 512:(half + 1) * 512], op=ALU.add),
                            reads=[f"ps{b}", f"x1_{mt * 4 + s}"], writes=[f"x1_{mt * 4 + s}"])
                        if half == 1 and s % 2 == 1:
                            tick()
                        if q == 3 and mt == 3 and half == 1:
                            final_norm_sub(mt, s)
                if q == 3 and mt < 3:
                    final_norm_tile(mt)

            for kind, mt in (("up", 0), ("up", 1), ("dn", 0), ("up", 2), ("dn", 1), ("up", 3), ("dn", 2), ("dn", 3)):
                if kind == "up":
                    ffn_up(mt)
                else:
                    ffn_dn(mt)
            flush()

        S.barrier()
        S.emit(nc, st)
    return nc


def _cols(v):
    return np.ascontiguousarray(np.asarray(v, dtype=np.float32).reshape(8, P).T)


_NC_CACHE = {}


def kernel(x, c, norm_mix_g, norm_mlp_g, w_ada, b_ada, w_in, conv_w, conv_b, w_rg_a, b_rg_a, w_rg_x, b_rg_x,
           a_param, w_branch_a, w_pool, b_pool, pool_scale, w_branch_b, w_out, w_up, w_down, final_g):
    f32 = np.float32
    x = np.asarray(x, f32)
    c = np.asarray(c, f32)
    w_ada0 = np.asarray(w_ada, f32)[0]
    b_ada0 = np.asarray(b_ada, f32)[0]
    B, SEQ, _ = x.shape
    half = SEQ // 2
    w_adaT = np.ascontiguousarray(w_ada0.T)
    shared = {
        "ident": np.eye(P, dtype=f32),
        "w_in": np.ascontiguousarray(np.asarray(w_in, f32)[0]),
        "w_ra": np.ascontiguousarray(np.asarray(w_rg_a, f32)[0].reshape(D, 256)),
        "w_rx": np.ascontiguousarray(np.asarray(w_rg_x, f32)[0].reshape(D, 256)),
        "w_pl": np.ascontiguousarray(np.asarray(w_pool, f32)[0].reshape(D, 256)),
        "w_a": np.ascontiguousarray(np.asarray(w_branch_a, f32)[0]),
        "w_b": np.ascontiguousarray(np.asarray(w_branch_b, f32)[0]),
        "w_o": np.ascontiguousarray(np.asarray(w_out, f32)[0]),
        "w_up": np.ascontiguousarray(np.asarray(w_up, f32)[0]),
        "w_dn": np.ascontiguousarray(np.asarray(w_down, f32)[0]),
        "w_adaT": w_adaT,
    }
    cw = np.asarray(conv_w, f32)[0]
    in_maps = []
    for r in range(8):
        b, hf = r // 2, r % 2
        if hf == 0:
            xin = np.concatenate([np.zeros((half, D), f32), x[b, 0:half]], axis=0)
        else:
            xin = x[b]
        flag = np.zeros((P, 8), f32)
        flag[:, 0] = float(hf)
        flag[:, 1] = 1.0 - float(hf)
        flag[:, 2] = 0.5 * (1.0 - float(hf))
        vec_list = [_cols(np.asarray(norm_mix_g, f32)[0]), _cols(np.asarray(norm_mlp_g, f32)[0]),
                    _cols(b_ada0[0:D]), _cols(b_ada0[D:2 * D]), _cols(b_ada0[3 * D:4 * D]), _cols(b_ada0[4 * D:5 * D]),
                    _cols(cw[0]), _cols(cw[1]), _cols(cw[2]), _cols(cw[3]), _cols(np.asarray(conv_b, f32)[0]),
                    _cols(np.asarray(b_rg_a, f32)[0]), _cols(np.asarray(b_rg_x, f32)[0]), _cols(np.asarray(a_param, f32)[0]),
                    _cols(np.asarray(b_pool, f32)[0]), _cols(np.asarray(pool_scale, f32)[0]), _cols(c[b]), flag,
                    _cols(b_ada0[2 * D:3 * D]), _cols(b_ada0[5 * D:6 * D])]
        vecs = np.ascontiguousarray(np.concatenate(vec_list, axis=1))
        rows = np.empty((P, 2, D), f32)
        rows[:, 0, :] = c[b][None, :]
        rows[:, 1, :] = np.asarray(final_g, f32)[None, :]
        invc = np.empty((P, 4, 16), f32)
        for gi, win in enumerate((2, 4, 8, 16)):
            if hf == 0:
                cntv = np.minimum(np.arange(16) + 1, win).astype(f32)
            else:
                cntv = np.full(16, win, f32)
            invc[:, gi, :] = (1.0 / cntv)[None, :]
        m = {"xin": np.ascontiguousarray(xin), "vecs": vecs, "rows": rows, "invc": invc}
        m.update(shared)
        in_maps.append(m)
    if "nc" not in _NC_CACHE:
        _NC_CACHE["nc"] = build_nc()
    nc = _NC_CACHE["nc"]
    res = run_bass_kernel_spmd(nc, in_maps, core_ids=list(range(8)))
    out = np.empty((B, SEQ, D), f32)
    for r in range(8):
        b, hf = r // 2, r % 2
        out[b, hf * half:(hf + 1) * half] = res.results[r]["yout"]
    return out
```

```python
import numpy as np
from contextlib import ExitStack
import concourse.bass as bass
import concourse.mybir as mybir
from concourse.bass_utils import run_bass_kernel_spmd

F32 = mybir.dt.float32
BF16 = mybir.dt.bfloat16
AF = mybir.ActivationFunctionType
ALU = mybir.AluOpType

ENGS = ("pe", "act", "dve", "pool", "sp")
P = 128
D = 1024
DFF = 4096
NTOK = 2048
TT = 512
EPS = 1e-6
GELU_NATIVE = True


class Sched:
    def __init__(self):
        self.q = {e: [] for e in ENGS}
        self.prog = {e: 0 for e in ENGS}
        self.waited = {e: {} for e in ENGS}
        self.bufs = {}
        self.dma_cnt = {}

    def _st(self, k):
        st = self.bufs.get(k)
        if st is None:
            st = {"w": None, "r": {}}
            self.bufs[k] = st
        return st

    def _collect(self, eng, reads, writes):
        deps = []
        for k in reads:
            st = self._st(k)
            d = st["w"]
            if d is not None and not (d[2] == eng and eng == "pe"):
                deps.append(d)
        for k in writes:
            st = self._st(k)
            if st["w"] is not None and not (st["w"][2] == eng and eng == "pe"):
                deps.append(st["w"])
            for d in st["r"].values():
                if not (d[2] == eng and eng == "pe"):
                    deps.append(d)
        w = self.waited[eng]
        best = {}
        for (sk, val, _src) in deps:
            if w.get(sk, 0) >= val:
                continue
            if best.get(sk, 0) < val:
                best[sk] = val
        waits = []
        for sk, val in best.items():
            w[sk] = val
            waits.append((sk, val))
        return waits

    def _commit(self, dep, reads, writes):
        for k in reads:
            self._st(k)["r"][dep[0]] = dep
        for k in writes:
            st = self._st(k)
            st["w"] = dep
            st["r"] = {}

    def op(self, eng, fn, reads=(), writes=()):
        waits = self._collect(eng, reads, writes)
        self.prog[eng] += 1
        dep = ("p_" + eng, self.prog[eng], eng)
        self.q[eng].append((waits, fn, ("p_" + eng, 1)))
        self._commit(dep, reads, writes)

    def dma(self, qeng, fn, sem, reads=(), writes=()):
        waits = self._collect(qeng, reads, writes)
        self.dma_cnt[sem] = self.dma_cnt.get(sem, 0) + 16
        dep = ("d_" + sem, self.dma_cnt[sem], None)
        self.q[qeng].append((waits, fn, ("d_" + sem, 16)))
        self._commit(dep, reads, writes)

    def barrier(self):
        targets = [("p_" + e, self.prog[e]) for e in ENGS if self.prog[e] > 0]
        targets += [("d_" + s, c) for s, c in self.dma_cnt.items()]
        for e in ENGS:
            w = self.waited[e]
            waits = []
            for sk, val in targets:
                if sk == "p_" + e:
                    continue
                if w.get(sk, 0) < val:
                    w[sk] = val
                    waits.append((sk, val))
            if waits:
                self.q[e].append((waits, None, None))

    def _plan(self):
        needed = {e: set() for e in ENGS}
        for e in ENGS:
            for waits, fn, inc in self.q[e]:
                for sk, val in waits:
                    if sk.startswith("p_"):
                        needed[sk[2:]].add(val)
        rank = {e: {k: i + 1 for i, k in enumerate(sorted(needed[e]))} for e in ENGS}
        plan = {}
        for e in ENGS:
            cnt = 0
            out = []
            for waits, fn, inc in self.q[e]:
                w2 = [(sk, rank[sk[2:]][val]) if sk.startswith("p_") else (sk, val) for sk, val in waits]
                inc2 = inc
                if inc is not None and inc[0].startswith("p_"):
                    cnt += 1
                    inc2 = inc if cnt in needed[e] else None
                out.append((w2, fn, inc2))
            plan[e] = out
        return plan

    @staticmethod
    def _check(plan):
        sem = {}
        ptr = {e: 0 for e in ENGS}
        total = sum(len(plan[e]) for e in ENGS)
        done = 0
        while done < total:
            progressed = False
            for e in ENGS:
                while ptr[e] < len(plan[e]):
                    waits, fn, inc = plan[e][ptr[e]]
                    if any(sem.get(sk, 0) < val for sk, val in waits):
                        break
                    if inc is not None:
                        sem[inc[0]] = sem.get(inc[0], 0) + inc[1]
                    ptr[e] += 1
                    done += 1
                    progressed = True
            if not progressed:
                raise RuntimeError(f"semaphore plan deadlocks at {ptr}")

    def emit(self, nc, stack):
        plan = self._plan()
        self._check(plan)
        sems = {}
        for e in ENGS:
            if self.prog[e] > 0:
                sems["p_" + e] = stack.enter_context(nc.semaphore("p_" + e))
        for s_ in self.dma_cnt:
            sems["d_" + s_] = stack.enter_context(nc.semaphore("d_" + s_))
        block = stack.enter_context(nc.Block())

        def mk(engname):
            def body(e):
                for waits, fn, inc in plan[engname]:
                    for sk, val in waits:
                        e.wait_ge(sems[sk], val)
                    if fn is not None:
                        ins = fn(e)
                        if inc is not None:
                            ins.then_inc(sems[inc[0]], inc[1])
            return body

        block.sync(mk("sp"))
        block.tensor(mk("pe"))
        block.scalar(mk("act"))
        block.vector(mk("dve"))
        block.gpsimd(mk("pool"))


(V_G1, V_G2, V_BSH1, V_BSC1, V_BSH2, V_BSC2, V_CW0, V_CW1, V_CW2, V_CW3, V_CB, V_BRA, V_BRX, V_AP, V_BP, V_PS, V_C, V_FLAG,
 V_BGT1, V_BGT2) = range(20)
NV = 20
(DV_SILU, DV_SH1, DV_SC1, DV_SH2, DV_SC2, DV_GM1, DV_GM2, DV_HBA, DV_HBX, DV_M4, DV_M8, DV_T0, DV_T1, DV_SP,
 DV_ACC) = range(15)
NDV = 20


def build_nc():
    nc = bass.Bass("TRN2", target_bir_lowering=False)

    def din(name, shape):
        return nc.dram_tensor(name, shape, F32, kind="ExternalInput").ap()

    xin = din("xin", [2 * NTOK, D])
    vecs_d = din("vecs", [P, NV * 8])
    rows_d = din("rows", [P, 2, D])
    invc_d = din("invc", [P, 4, 16])
    ident_d = din("ident", [P, P])
    w_in = din("w_in", [D, 5 * D])
    w_ra = din("w_ra", [D, 256])
    w_rx = din("w_rx", [D, 256])
    w_pl = din("w_pl", [D, 256])
    w_a = din("w_a", [D, D])
    w_b = din("w_b", [D, D])
    w_o = din("w_o", [D, D])
    w_up = din("w_up", [D, DFF])
    w_dn = din("w_dn", [DFF, D])
    w_adaT = din("w_adaT", [6 * D, D])
    yout = nc.dram_tensor("yout", [NTOK, D], F32, kind="ExternalOutput").ap()

    S = Sched()
    BASE = 16512
    RW, RHP, RX, RS = BASE, BASE + 65536, BASE + 131072, BASE + 196608
    cnt = [0]

    def at(name, shape, dt, off):
        cnt[0] += 1
        return nc.alloc_sbuf_tensor_at(f"{name}_{cnt[0]}", shape, dt, offset=off)

    with ExitStack() as st:
        ps = [st.enter_context(nc.psum_tensor(f"ps{i}", [P, 512], F32)) for i in range(8)]
        psb = [p.bitcast(BF16) for p in ps]
        bank_i = [0]

        def nb():
            b = bank_i[0]
            bank_i[0] = (b + 1) % 8
            return b

        vecs = at("vecs", [P, NV * 8], F32, RS + 0)
        dv = at("dv", [P, NDV * 8], F32, RS + 640)
        stat = at("stat", [P, 64], F32, RS + 1280)
        ident = at("ident", [P, P], BF16, RS + 1536)
        xh = at("xh", [P, 8, 4], F32, RS + 1792)
        uh = at("uh", [P, 8, 16], F32, RS + 1920)
        hst = at("hst", [P, 8], F32, RS + 2432)
        hTh = at("hTh", [P, 8, 16], BF16, RS + 2464)
        invc = at("invc", [P, 4, 16], F32, RS + 2720)
        misc = at("misc", [P, 16], F32, RS + 2976)
        gtb1 = at("gtb1", [P, D], F32, RS + 4096)
        gtb2 = at("gtb2", [P, D], F32, RS + 8192)
        identf = at("identf", [P, P], F32, RS + 12288)
        ones = at("ones", [P, P], F32, RS + 12800)
        dg = at("dg", [P, P], F32, RS + 13312)

        def V(i, c=None):
            return vecs[:, i * 8:(i + 1) * 8] if c is None else vecs[:, i * 8 + c:i * 8 + c + 1]

        def DV(i, c=None):
            return dv[:, i * 8:(i + 1) * 8] if c is None else dv[:, i * 8 + c:i * 8 + c + 1]

        FM = vecs[:, V_FLAG * 8:V_FLAG * 8 + 1]
        FN = vecs[:, V_FLAG * 8 + 1:V_FLAG * 8 + 2]

        slots = [at(f"slot{i}", [P, 8, D], BF16, RW + i * 16384) for i in range(4)]
        wa_g = at("wa_g", [P, 8, 256], BF16, RW + 2 * 16384)
        wi_g = at("wi_g", [P, 8, 256], BF16, RW + 2 * 16384 + 4096)
        wpl = at("wpl", [P, 8, 256], BF16, RW + 2 * 16384 + 8192)

        stA = [(gtb1, "gtb1"), (gtb2, "gtb2")]
        stA4 = [(at("stA4_0", [P, 4, 256], F32, RS + 4096), "gtb1"), (at("stA4_1", [P, 4, 256], F32, RS + 8192), "gtb2")]
        st_rr = [0]

        def load_cast(slot_t, key, dram2d, nrow_chunks=8, extra=(), stages=None, eng="act"):
            stages = stages or stA
            for kc in range(nrow_chunks):
                st_t, st_k = stages[st_rr[0] % len(stages)]
                st_rr[0] += 1
                S.dma("sp", lambda e, kc=kc, st_t=st_t: e.dma_start(out=st_t[:], in_=dram2d[kc * P:(kc + 1) * P, :]), "st_" + st_k, writes=[st_k])
                if eng == "act":
                    S.op("act", lambda e, kc=kc, st_t=st_t: e.activation(out=slot_t[:, kc, :], in_=st_t[:], func=AF.Identity),
                         reads=[st_k], writes=[key] + list(extra))
                else:
                    S.op("dve", lambda e, kc=kc, st_t=st_t: e.tensor_copy(out=slot_t[:, kc, :], in_=st_t[:]),
                         reads=[st_k], writes=[key] + list(extra))

        def load_small(w_t, key, dram2d):
            for h in range(2):
                st_t, st_k = stA4[st_rr[0] % 2]
                st_rr[0] += 1
                S.dma("sp", lambda e, h=h, st_t=st_t: e.dma_start(out=st_t[:], in_=dram2d[h * 512:(h + 1) * 512, :].rearrange("(s p) n -> p s n", p=P)),
                      "st_" + st_k, writes=[st_k])
                S.op("act", lambda e, h=h, st_t=st_t: e.activation(out=w_t[:, 4 * h:4 * h + 4, :], in_=st_t[:], func=AF.Identity),
                     reads=[st_k], writes=[key])

        def make_stream(slot_t, key, dram2d, stages, gt=None, gtk=None, extra=()):
            state = {"prev": None}

            def consume():
                kc, st_t, st_k = state["prev"]
                if gt is None:
                    S.op("act", lambda e: e.activation(out=slot_t[:, kc, :], in_=st_t[:], func=AF.Identity), reads=[st_k], writes=[key] + list(extra))
                else:
                    S.op("dve", lambda e: e.tensor_tensor(out=slot_t[:, kc, :], in0=st_t[:], in1=gt[:], op=ALU.mult), reads=[st_k, gtk], writes=[key] + list(extra))
                state["prev"] = None

            def mk(kc):
                def step():
                    st_t, st_k = stages[kc % len(stages)]
                    if state["prev"] is not None:
                        consume()
                    S.dma("sp", lambda e: e.dma_start(out=st_t[:], in_=dram2d[kc * P:(kc + 1) * P, :]), "st_" + st_k, writes=[st_k])
                    state["prev"] = (kc, st_t, st_k)
                return step
            return [mk(kc) for kc in range(8)] + [consume]

        pending = []

        def tick(n=1):
            for _ in range(n):
                if pending:
                    pending.pop(0)()

        def flush():
            while pending:
                pending.pop(0)()

        S.dma("sp", lambda e: e.dma_start(out=vecs[:], in_=vecs_d), "c_vecs", writes=["vecs"])
        S.dma("sp", lambda e: e.dma_start(out=identf[:], in_=ident_d), "c_ident", writes=["identf"])
        S.dma("sp", lambda e: e.dma_start(out=invc[:], in_=invc_d), "c_invc", writes=["invc"])

        wTs = [at("wTs0", [P, 4, D], F32, RHP), at("wTs1", [P, 4, D], F32, RHP + 16384),
               at("wTs2", [P, 4, D], F32, RHP + 49152), at("wTs3", [P, 4, D], F32, RW + 16384)]
        srow = at("srow", [P, D], F32, RHP + 32768)
        junkf = at("junkf", [P, D], F32, RHP + 36864)
        xt = at("xt", [P, 4, D], F32, RX + 0)

        def norm_load(tile_idx):
            r0 = tile_idx * TT
            S.dma("sp", lambda e: e.dma_start(out=xt[:], in_=xin[r0:r0 + TT, :].rearrange("(s p) d -> p s d", p=P)), "xt", writes=["xt"])

        norm_load(0)

        S.dma("sp", lambda e: e.dma_start(out=srow[:], in_=rows_d[:, 0, :]), "c_srow", writes=["srow"])

        def adaT_dma(blk, buf_t, bkey):
            S.dma("sp", lambda e: e.dma_start(
                out=buf_t[:], in_=w_adaT[blk * 512:(blk + 1) * 512, :].rearrange("(s p) k -> p s k", p=P)),
                bkey, writes=[bkey])

        def adaT_acc(blk, buf_t, bkey, srow_t, srow_k, junk_t, junk_k):
            for s in range(4):
                col = DV_ACC * 8 + blk * 4 + s
                S.op("dve", lambda e, s=s, col=col: e.scalar_tensor_tensor(
                    out=junk_t[:], in0=buf_t[:, s, :], scalar=1.0, in1=srow_t[:], op0=ALU.mult, op1=ALU.mult,
                    accum_out=dv[:, col:col + 1]), reads=[bkey, srow_k], writes=[junk_k, f"dvacc{blk}"])

        def silu_row(srow_t, srow_k, junk_t, junk_k):
            jv = junk_t if isinstance(junk_t, bass.AP) else junk_t[:]
            S.op("act", lambda e: e.activation(out=jv, in_=srow_t[:], func=AF.Tanh, scale=0.5), reads=[srow_k], writes=[junk_k])
            S.op("dve", lambda e: e.scalar_tensor_tensor(out=jv, in0=jv, scalar=1.0, in1=srow_t[:], op0=ALU.add, op1=ALU.mult),
                 reads=[junk_k, srow_k], writes=[junk_k])
            S.op("dve", lambda e: e.tensor_scalar(out=srow_t[:], in0=jv, scalar1=0.5, scalar2=None, op0=ALU.mult),
                 reads=[junk_k], writes=[srow_k])

        for blk in range(4):
            adaT_dma(blk, wTs[blk], f"wTs{blk}")
        gst = [at("gstA", [P, 8, 256], F32, RW + 3 * 16384), at("gstB", [P, 8, 256], F32, RW + 3 * 16384 + 8192)]
        for gi_, (gsrc_, gk_) in enumerate(((w_ra, "gstA"), (w_rx, "gstB"))):
            S.dma("sp", lambda e, gi_=gi_, gsrc_=gsrc_: e.dma_start(out=gst[gi_][:], in_=gsrc_.rearrange("(s p) n -> p s n", p=P)), gk_, writes=[gk_])

        S.op("dve", lambda e: e.tensor_copy(out=ident[:], in_=identf[:]), reads=["identf"], writes=["ident"])
        S.op("dve", lambda e: e.memset(ones[:], 1.0), writes=["ones"])
        S.op("dve", lambda e: e.memset(misc[:, 0:1], EPS), writes=["misc"])
        S.op("dve", lambda e: e.memset(xh[:], 0.0), writes=[f"xh{c}" for c in range(8)])
        S.op("dve", lambda e: e.memset(hst[:], 0.0), writes=[f"hst{c}" for c in range(8)])
        silu_row(srow, "srow", junkf, "junkf")
        for blk in range(4):
            adaT_acc(blk, wTs[blk], f"wTs{blk}", srow, "srow", junkf, "junkf")
        S.op("dve", lambda e: e.tensor_tensor(out=DV(DV_SH1), in0=dv[:, DV_ACC * 8:DV_ACC * 8 + 8], in1=V(V_BSH1), op=ALU.add),
             reads=["dvacc0", "dvacc1", "vecs"], writes=["dvmod0"])
        S.op("dve", lambda e: e.tensor_tensor(out=DV(DV_SC1), in0=dv[:, DV_ACC * 8 + 8:DV_ACC * 8 + 16], in1=V(V_BSC1), op=ALU.add),
             reads=["dvacc2", "dvacc3", "vecs"], writes=["dvmod1"])
        S.op("dve", lambda e: e.scalar_tensor_tensor(out=DV(DV_GM1), in0=DV(DV_SC1), scalar=1.0, in1=V(V_G1), op0=ALU.add, op1=ALU.mult),
             reads=["dvmod1", "vecs"], writes=["dvgm1"])
        S.op("dve", lambda e: e.tensor_scalar(out=DV(DV_HBA), in0=V(V_BRA), scalar1=0.5, scalar2=None, op0=ALU.mult), reads=["vecs"], writes=["dvhba"])
        S.op("dve", lambda e: e.tensor_scalar(out=DV(DV_HBX), in0=V(V_BRX), scalar1=0.5, scalar2=None, op0=ALU.mult), reads=["vecs"], writes=["dvhbx"])
        S.op("act", lambda e: e.activation(out=DV(DV_T0), in_=V(V_AP), func=AF.Abs), reads=["vecs"], writes=["dvt0"])
        S.op("act", lambda e: e.activation(out=DV(DV_T1), in_=DV(DV_T0), func=AF.Exp, scale=-1.0), reads=["dvt0"], writes=["dvt1"])
        S.op("act", lambda e: e.activation(out=DV(DV_T0), in_=DV(DV_T1), func=AF.Ln, bias=1.0, scale=1.0), reads=["dvt1"], writes=["dvt0"])
        S.op("dve", lambda e: e.scalar_tensor_tensor(out=DV(DV_SP), in0=V(V_AP), scalar=0.0, in1=DV(DV_T0), op0=ALU.max, op1=ALU.add),
             reads=["vecs", "dvt0"], writes=["dvsp"])
        S.op("dve", lambda e: e.tensor_scalar(out=DV(DV_M4), in0=DV(DV_SP), scalar1=-4.0, scalar2=None, op0=ALU.mult), reads=["dvsp"], writes=["dvm4"])
        S.op("dve", lambda e: e.tensor_scalar(out=DV(DV_M8), in0=DV(DV_SP), scalar1=-8.0, scalar2=None, op0=ALU.mult), reads=["dvsp"], writes=["dvm8"])

        hT_all = at("hT_all", [P, 8, NTOK], BF16, RHP)
        prodA = at("prodA", [P, 8, NTOK], BF16, RHP + 32768)
        hT_st = [at(f"hT_st{i}", [P, 8, TT], BF16, RHP + 32768 + i * 8192) for i in range(2)]
        xn = at("xn", [P, 4, D], BF16, RX + 16384)
        xr = at("xr", [P, 4, TT], F32, RX + 24576)
        thr = at("thr", [P, 4, TT], F32, RX + 32768)
        thi = at("thi", [P, 4, TT], F32, RX + 40960)
        a2 = at("a2", [P, 4, TT], F32, RX + 49152)
        t1 = at("t1", [P, 4, TT], F32, RX + 57344)
        xrb = at("xrb", [P, 4, TT], BF16, RW + 2 * 16384 + 12288)

        def norm_part1():
            for s in range(4):
                S.op("act", lambda e, s=s: e.activation(out=xn[:, s, :], in_=xt[:, s, :], func=AF.Square, accum_out=stat[:, s:s + 1]),
                     reads=["xt"], writes=[f"xn{s}", f"ss{s}"])
            S.op("act", lambda e: e.activation(out=stat[:, 4:8], in_=stat[:, 0:4], func=AF.Sqrt, scale=1.0 / D, bias=misc[:, 0:1]),
                 reads=["ss0", "ss1", "ss2", "ss3", "misc"], writes=["std"])
            S.op("dve", lambda e: e.reciprocal(out=stat[:, 8:12], in_=stat[:, 4:8]), reads=["std"], writes=["rstd"])
            for s in range(4):
                S.op("act", lambda e, s=s: e.activation(out=xn[:, s, :], in_=xt[:, s, :], func=AF.Identity, scale=stat[:, 8 + s:9 + s]),
                     reads=["xt", "rstd"], writes=[f"xn{s}"])

        def norm_part2(dst_fn, dst_keys):
            for c in range(8):
                b = nb()

                def tr(e, c=c, b=b):
                    ins = None
                    for s in range(4):
                        ins = e.transpose(out=psb[b][:, s * P:(s + 1) * P], in_=xn[:, s, c * P:(c + 1) * P], identity=ident[:])
                    return ins
                S.op("pe", tr, reads=["xn0", "xn1", "xn2", "xn3", "ident"], writes=[f"ps{b}"])
                S.op("act", lambda e, c=c, b=b: e.activation(out=dst_fn(c), in_=psb[b][:, 0:TT], func=AF.Identity, scale=DV(DV_GM1, c), bias=DV(DV_SH1, c)),
                     reads=[f"ps{b}", "dvgm1", "dvmod0"], writes=[dst_keys[c]])

        def hT_view(T):
            if T < 4:
                t_ = hT_st[T % 2]
                return (lambda c: t_[:, c, :]), [f"hTst{T % 2}_{c}" for c in range(8)]
            m0 = (T - 4) * TT
            return (lambda c: hT_all[:, c, m0:m0 + TT]), [f"hT{T - 4}_{c}" for c in range(8)]

        FN2 = vecs[:, V_FLAG * 8 + 2:V_FLAG * 8 + 3]

        def s12(T, blk, hview, hkeys, last=False):
            for jj_ in range(2):
                i = 2 * (blk % 2) + jj_
                c = 2 * blk + jj_
                b = nb()

                def mmx(e, c=c, b=b):
                    ins = None
                    for k in range(8):
                        ins = e.matmul(ps[b][:, :], lhsT=slots[0][:, k, c * P:(c + 1) * P], rhs=hview(k), start=(k == 0), stop=(k == 7))
                    return ins
                S.op("pe", mmx, reads=["S0"] + hkeys, writes=[f"ps{b}"])
                S.op("act", lambda e, i=i, b=b, c=c: e.activation(out=xr[:, i, :], in_=ps[b][:, :], func=AF.Identity, scale=V(V_CW3, c), bias=V(V_CB, c)),
                     reads=[f"ps{b}", "vecs"], writes=[f"xr{i}"])
                for k in (2, 1, 0):
                    sh = 3 - k
                    S.op("dve", lambda e, i=i, c=c, k=k, sh=sh, b=b: e.scalar_tensor_tensor(
                        out=xr[:, i, sh:TT], in0=ps[b][:, 0:TT - sh], scalar=V(V_CW0 + k, c), in1=xr[:, i, sh:TT], op0=ALU.mult, op1=ALU.add),
                        reads=[f"ps{b}", f"xr{i}", "vecs"], writes=[f"xr{i}"])
                    S.op("dve", lambda e, i=i, c=c, k=k, sh=sh: e.scalar_tensor_tensor(
                        out=xr[:, i, 0:sh], in0=xh[:, c, 3 - sh:3], scalar=V(V_CW0 + k, c), in1=xr[:, i, 0:sh], op0=ALU.mult, op1=ALU.add),
                        reads=[f"xh{c}", f"xr{i}", "vecs"], writes=[f"xr{i}"])
                S.op("dve", lambda e, c=c, b=b: e.tensor_copy(out=xh[:, c, 0:3], in_=ps[b][:, TT - 3:TT]), reads=[f"ps{b}", f"xh{c}"], writes=[f"xh{c}"])
            i0 = 2 * (blk % 2)
            S.op("dve", lambda e, i0=i0: e.tensor_copy(out=xrb[:, i0:i0 + 2, :], in_=xr[:, i0:i0 + 2, :]),
                 reads=[f"xr{i0}", f"xr{i0 + 1}"], writes=[f"xrb{i0}", f"xrb{i0 + 1}"])
            if last:
                pending.extend(make_stream(slots[0], "S0", w_in[:, 4 * D:5 * D], stA))

        def s34(T, blk):
            for jj_ in range(2):
                i = 2 * (blk % 2) + jj_
                c = 2 * blk + jj_
                bl, j = blk % 2, jj_
                for (wg, dst, hb, dk) in ((wa_g, thr, DV_HBA, "thr"), (wi_g, thi, DV_HBX, "thi")):
                    b = nb()

                    def mmgate(e, j=j, b=b, wg=wg, blk=blk, bl=bl):
                        ins = None
                        for jj in range(2):
                            ins = e.matmul(ps[b][:, :], lhsT=wg[:, 2 * blk + jj, j * P:(j + 1) * P], rhs=xrb[:, 2 * bl + jj, :], start=(jj == 0), stop=(jj == 1))
                        return ins
                    S.op("pe", mmgate, reads=["S2a", "S2b", f"xrb{2 * bl}", f"xrb{2 * bl + 1}"], writes=[f"ps{b}"])
                    S.op("act", lambda e, i=i, b=b, dst=dst, hb=hb, c=c: e.activation(out=dst[:, i, :], in_=ps[b][:, :], func=AF.Tanh,
                                                                                      scale=0.5, bias=DV(hb, c)),
                         reads=[f"ps{b}", "dvhba", "dvhbx"], writes=[f"{dk}{i}"])
            for jj_ in range(2):
                i = 2 * (blk % 2) + jj_
                c = 2 * blk + jj_
                S.op("act", lambda e, i=i, c=c: e.activation(out=thr[:, i, :], in_=thr[:, i, :], func=AF.Exp, scale=DV(DV_M4, c), bias=DV(DV_M4, c)),
                     reads=[f"thr{i}", "dvm4"], writes=[f"thr{i}"])
            i0 = 2 * (blk % 2)
            S.op("dve", lambda e, i0=i0: e.scalar_tensor_tensor(out=t1[:, i0:i0 + 2, :], in0=thi[:, i0:i0 + 2, :], scalar=1.0, in1=xr[:, i0:i0 + 2, :],
                                                                op0=ALU.add, op1=ALU.mult),
                 reads=[f"thi{i0}", f"thi{i0 + 1}", f"xr{i0}", f"xr{i0 + 1}"], writes=[f"t1{i0}", f"t1{i0 + 1}"])
            S.op("pool", lambda e, i0=i0: e.tensor_tensor(out=a2[:, i0:i0 + 2, :], in0=thr[:, i0:i0 + 2, :], in1=thr[:, i0:i0 + 2, :], op=ALU.mult),
                 reads=[f"thr{i0}", f"thr{i0 + 1}"], writes=[f"a2{i0}", f"a2{i0 + 1}"])

        def s56(T, blk, hview, hkeys):
            main = T >= 4
            mt = T - 4
            m0 = mt * TT
            i0 = 2 * (blk % 2)
            c0 = 2 * blk
            ak = [f"a2{i0}", f"a2{i0 + 1}"]
            tk = [f"t1{i0}", f"t1{i0 + 1}"]
            S.op("act", lambda e, i0=i0: e.activation(out=a2[:, i0:i0 + 2, :], in_=a2[:, i0:i0 + 2, :], func=AF.Sqrt, scale=-0.25, bias=0.25),
                 reads=ak, writes=ak)
            if T == 0:
                S.op("dve", lambda e, i0=i0: e.memset(a2[:, i0:i0 + 2, 0:1], 0.5), reads=ak, writes=ak)
            elif T == 4:
                S.op("dve", lambda e, i0=i0: e.tensor_scalar(out=a2[:, i0:i0 + 2, 0:1], in0=a2[:, i0:i0 + 2, 0:1], scalar1=FM, scalar2=FN2, op0=ALU.mult, op1=ALU.add),
                     reads=ak + ["vecs"], writes=ak)
            S.op("pool", lambda e, i0=i0: e.tensor_tensor(out=t1[:, i0:i0 + 2, :], in0=t1[:, i0:i0 + 2, :], in1=a2[:, i0:i0 + 2, :], op=ALU.mult),
                 reads=tk + ak, writes=tk)
            for jj_ in range(2):
                i = i0 + jj_
                c = c0 + jj_
                S.op("dve", lambda e, i=i, c=c: e.tensor_tensor_scan(out=a2[:, i, :], data0=thr[:, i, :], data1=t1[:, i, :], initial=hst[:, c:c + 1],
                                                                     op0=ALU.mult, op1=ALU.add),
                     reads=[f"thr{i}", f"t1{i}", f"hst{c}", f"a2{i}"], writes=[f"a2{i}"])
            S.op("pool", lambda e, i0=i0, c0=c0: e.tensor_copy(out=hst[:, c0:c0 + 2], in_=a2[:, i0:i0 + 2, TT - 1]),
                 reads=ak, writes=[f"hst{c0}", f"hst{c0 + 1}"])
            if main:
                for jj_ in range(2):
                    i = 2 * (blk % 2) + jj_
                    c = 2 * blk + jj_
                    b = nb()

                    def mmy(e, c=c, b=b):
                        ins = None
                        for k in range(8):
                            ins = e.matmul(ps[b][:, :], lhsT=slots[1][:, k, c * P:(c + 1) * P], rhs=hview(k), start=(k == 0), stop=(k == 7))
                        return ins
                    S.op("pe", mmy, reads=["S1"] + hkeys, writes=[f"ps{b}"])
                    S.op("act", lambda e, i=i, b=b: e.activation(out=thi[:, i, :], in_=ps[b][:, :], func=AF.Gelu_apprx_tanh),
                         reads=[f"ps{b}"], writes=[f"thi{i}"])
                S.op("pool", lambda e, i0=i0, c0=c0: e.tensor_tensor(out=prodA[:, c0:c0 + 2, m0:m0 + TT], in0=thi[:, i0:i0 + 2, :], in1=a2[:, i0:i0 + 2, :], op=ALU.mult),
                     reads=[f"thi{i0}", f"thi{i0 + 1}"] + ak, writes=[f"prodA{mt}_{c0}", f"prodA{mt}_{c0 + 1}"])

        norm_part1()
        st8 = [(at(f"stS{i}", [P, D], F32, RX + 24576 + i * 4096), f"stS{i}") for i in range(8)]
        load_cast(slots[0], "S0", w_in[:, 0:D], stages=st8, eng="dve")
        S.op("dve", lambda e: e.tensor_copy(out=wa_g[:], in_=gst[0][:]), reads=["gstA"], writes=["S2a"])
        S.op("dve", lambda e: e.tensor_copy(out=wi_g[:], in_=gst[1][:]), reads=["gstB"], writes=["S2b"])
        norm_load(1)
        pending.extend(make_stream(slots[1], "S1", w_in[:, D:2 * D], stA, extra=["wTs3"]))
        pending.extend(make_stream(slots[3], "S3", w_in[:, 2 * D:3 * D], stA, extra=["gstA", "gstB"]))
        d0, k0 = hT_view(0)
        norm_part2(d0, k0)
        seq = [(T, blk) for T in range(8) for blk in range(4)]
        s12(0, 0, d0, k0)
        s12(0, 1, d0, k0)
        for n, (T, blk) in enumerate(seq):
            dst_fn, dkeys = hT_view(T)
            if (T, blk) == (3, 0):
                S.op("pool", lambda e: e.tensor_copy(out=hTh[:], in_=hT_st[1][:, :, TT - 16:TT]), reads=dkeys, writes=["hTh"])
            if blk == 1 and T + 1 < 8:
                dn_, kn_ = hT_view(T + 1)
                norm_part2(dn_, kn_)
            s34(T, blk)
            if n < 20 or n >= 29:
                tick(1)
            if n == 1:
                S.dma("sp", lambda e: e.dma_start(out=gst[0][:], in_=w_pl.rearrange("(s p) n -> p s n", p=P)), "gstA", writes=["gstA"])
            if n == 3:
                S.op("dve", lambda e: e.tensor_copy(out=wpl[:], in_=gst[0][:]), reads=["gstA"], writes=["S2c"])
            if n == 20:
                flush()
            if blk == 0 and T + 1 < 8:
                norm_part1()
                if T + 2 < 8:
                    norm_load(T + 2)
            if n + 2 < len(seq):
                T2, b2 = seq[n + 2]
                dn, kn = hT_view(T2)
                if b2 == 0 and T2 == 4:
                    S.op("dve", lambda e: e.tensor_scalar(out=xh[:], in0=xh[:], scalar1=FM, scalar2=None, op0=ALU.mult),
                         reads=[f"xh{c}" for c in range(8)] + ["vecs"], writes=[f"xh{c}" for c in range(8)])
                s12(T2, b2, dn, kn, last=(n + 2 == len(seq) - 1))
            if n >= 29:
                tick(1)
            s56(T, blk, dst_fn, dkeys)
            if n >= 29:
                tick(1)
            if (T, blk) == (3, 3):
                S.op("dve", lambda e: e.tensor_scalar(out=hst[:], in0=hst[:], scalar1=FM, scalar2=None, op0=ALU.mult),
                     reads=[f"hst{c}" for c in range(8)] + ["vecs"], writes=[f"hst{c}" for c in range(8)])
        pending.extend(make_stream(slots[1], "S1", w_b, stA))

        S.barrier()

        mB = at("mB", [P, 8, NTOK], BF16, RX + 0)
        usb = [at("usA", [P, 2, 528], F32, RX + 32768), at("usB", [P, 2, 528], F32, RX + 36992)]
        sa = at("sa", [P, 2, 528], F32, RX + 41216)
        sb_ = at("sb", [P, 2, 528], F32, RX + 45440)
        tmp16 = at("tmp16", [P, 2, 16], F32, RX + 49664)
        pmb = at("pmb", [P, 2, TT], BF16, RX + 49792)
        poolB = at("poolB", [P, 8, TT], BF16, RX + 51840)
        thbs = [at("thb0", [P, TT], F32, RX + 60032), at("thb1", [P, TT], F32, RX + 62080)]

        wTb2 = at("wTb2", [P, 2, D], F32, RW + 2 * 16384)
        srow2 = at("srow2", [P, D], F32, RW + 2 * 16384 + 12288)
        junkb2 = at("junkb2", [P, D], BF16, RS + 13824)
        S.dma("sp", lambda e: e.dma_start(out=srow2[:], in_=rows_d[:, 0, :]), "c_srow2", writes=["srow2", "S2a", "S2b"] + [f"xrb{i}" for i in range(4)])

        def ob_init():
            silu_row(srow2, "srow2", wTb2[:, 0, :], "wTb2")

        def ob_dma(b2):
            S.dma("sp", lambda e: e.dma_start(out=wTb2[:], in_=w_adaT[2 * D + b2 * 256:2 * D + (b2 + 1) * 256, :].rearrange("(s p) k -> p s k", p=P)),
                  "wTb2", writes=["wTb2"])

        def ob_acc(b2):
            for s2 in range(2):
                col = DV_ACC * 8 + 16 + b2 * 2 + s2
                S.op("dve", lambda e, s2=s2, col=col: e.scalar_tensor_tensor(
                    out=junkb2[:], in0=wTb2[:, s2, :], scalar=1.0, in1=srow2[:], op0=ALU.mult, op1=ALU.mult,
                    accum_out=dv[:, col:col + 1]), reads=["wTb2", "srow2"], writes=["junkb2", "dvaccB"])

        ob_state = {"next": 0}

        def ob_step():
            b2 = ob_state["next"]
            if b2 > 16:
                return
            if b2 >= 1:
                ob_acc(b2 - 1)
            if b2 < 16:
                ob_dma(b2)
            ob_state["next"] = b2 + 1
        for c in range(8):
            b = nb()

            def mmh(e, c=c, b=b):
                ins = None
                for k in range(8):
                    ins = e.matmul(ps[b][:, 0:16], lhsT=slots[3][:, k, c * P:(c + 1) * P], rhs=hTh[:, k, :], start=(k == 0), stop=(k == 7))
                return ins
            S.op("pe", mmh, reads=["S3", "hTh"], writes=[f"ps{b}"])
            S.op("dve", lambda e, c=c, b=b: e.tensor_scalar(out=uh[:, c, :], in0=ps[b][:, 0:16], scalar1=FM, scalar2=None, op0=ALU.mult),
                 reads=[f"ps{b}", "vecs"], writes=["uh"])

        def uproj(mt_, blk_):
            m0_ = mt_ * TT
            ust = usb[blk_ % 2]
            usk = f"us{blk_ % 2}"
            for j in range(2):
                c = 2 * blk_ + j
                b = nb()

                def mmu(e, c=c, b=b, m0_=m0_):
                    ins = None
                    for k in range(8):
                        ins = e.matmul(ps[b][:, :], lhsT=slots[3][:, k, c * P:(c + 1) * P], rhs=hT_all[:, k, m0_:m0_ + TT], start=(k == 0), stop=(k == 7))
                    return ins
                S.op("pe", mmu, reads=["S3"] + [f"hT{mt_}_{k}" for k in range(8)], writes=[f"ps{b}"])
                S.op("pool", lambda e, j=j, c=c: e.tensor_copy(out=ust[:, j, 0:16], in_=uh[:, c, :]), reads=["uh"], writes=[usk])
                S.op("act", lambda e, j=j, b=b: e.activation(out=ust[:, j, 16:16 + TT], in_=ps[b][:, :], func=AF.Identity),
                     reads=[f"ps{b}"], writes=[usk])
                S.op("pool", lambda e, j=j, c=c: e.tensor_copy(out=uh[:, c, :], in_=ust[:, j, TT:TT + 16]), reads=[usk], writes=["uh"])

        S.op("dve", lambda e: e.tensor_tensor(out=dv[:, DV_ACC * 8:DV_ACC * 8 + 8], in0=V(V_BP), in1=V(V_PS), op=ALU.mult),
             reads=["vecs", "dvacc0", "dvacc1"], writes=["dvbps", "dvacc0", "dvacc1"])
        uproj(0, 0)
        for mt in range(4):
            m0 = mt * TT
            hkeys = [f"hT{mt}_{c}" for c in range(8)]
            for blk in range(4):
                win = 2 ** (blk + 1)
                us = usb[blk % 2]
                usk = f"us{blk % 2}"
                if blk < 3:
                    uproj(mt, blk + 1)
                elif mt < 3:
                    uproj(mt + 1, 0)
                tick(1)
                src, srck = us, usk
                bufs = [(sa, "sa"), (sb_, "sb")]
                for lvl in range(blk + 1):
                    sh = 2 ** lvl
                    lo = 2 ** (lvl + 1) - 1
                    dstt, dstk = bufs[lvl % 2]
                    S.op("dve", lambda e, src=src, dstt=dstt, sh=sh, lo=lo: e.tensor_tensor(
                        out=dstt[:, :, lo:528], in0=src[:, :, lo:528], in1=src[:, :, lo - sh:528 - sh], op=ALU.add),
                        reads=[srck], writes=[dstk])
                    src, srck = dstt, dstk
                for j in range(2):
                    c = 2 * blk + j
                    S.op("dve", lambda e, j=j, src=src, win=win, us=us: e.scalar_tensor_tensor(
                        out=pmb[:, j, :], in0=src[:, j, 16:16 + TT], scalar=1.0 / win, in1=us[:, j, 16:16 + TT], op0=ALU.mult, op1=ALU.subtract),
                        reads=[srck, usk], writes=[f"pmb{j}"])
                    if mt == 0:
                        S.op("dve", lambda e, j=j, src=src, blk=blk: e.tensor_tensor(out=tmp16[:, j, :], in0=src[:, j, 16:32], in1=invc[:, blk, :], op=ALU.mult),
                             reads=[srck, "invc"], writes=["tmp16"])
                        S.op("dve", lambda e, j=j, us=us: e.tensor_tensor(out=pmb[:, j, 0:16], in0=tmp16[:, j, :], in1=us[:, j, 16:32], op=ALU.subtract),
                             reads=["tmp16", usk], writes=[f"pmb{j}"])
                tick(1)
                for j in range(2):
                    c = 2 * blk + j
                    b = nb()

                    def mmp(e, j=j, b=b, blk=blk):
                        ins = None
                        for jj in range(2):
                            ins = e.matmul(ps[b][:, :], lhsT=wpl[:, 2 * blk + jj, j * P:(j + 1) * P], rhs=pmb[:, jj, :], start=(jj == 0), stop=(jj == 1))
                        return ins
                    S.op("pe", mmp, reads=["S2c", "pmb0", "pmb1"], writes=[f"ps{b}"])
                    S.op("act", lambda e, c=c, b=b: e.activation(out=poolB[:, c, :], in_=ps[b][:, :], func=AF.Identity,
                                                                 scale=V(V_PS, c), bias=dv[:, DV_ACC * 8 + c:DV_ACC * 8 + c + 1]),
                         reads=[f"ps{b}", "vecs", "dvbps"], writes=[f"poolB{c}"])
                tick(1)
                if mt == 0 and blk == 1:
                    ob_init()
                if mt < 3 and (mt, blk) >= (0, 1):
                    ob_step()
                tick(1)
            flush()
            if mt == 3:
                while ob_state["next"] <= 16:
                    ob_step()
                pending.extend(make_stream(slots[3], "S3", w_in[:, 3 * D:4 * D], stA))
                pending.extend(make_stream(slots[2], "S2", w_a, stA, extra=["S2a", "S2b", "S2c", "wTb2", "srow2"]))
            for c in range(8):
                tick(2)
                if mt < 3 and c % 2 == 1:
                    ob_step()
                b = nb()

                def mmgb(e, c=c, b=b, m0=m0):
                    ins = None
                    for k in range(8):
                        ins = e.matmul(ps[b][:, :], lhsT=slots[0][:, k, c * P:(c + 1) * P], rhs=hT_all[:, k, m0:m0 + TT], start=(k == 0), stop=(k == 7))
                    return ins
                S.op("pe", mmgb, reads=["S0"] + hkeys, writes=[f"ps{b}"])
                thb = thbs[c % 2]
                thk = f"thb{c % 2}"
                S.op("act", lambda e, b=b, thb=thb: e.activation(out=thb[:], in_=ps[b][:, :], func=AF.Tanh, scale=0.5), reads=[f"ps{b}"], writes=[thk])
                b2 = nb()

                def mmB(e, c=c, b2=b2):
                    ins = None
                    for k in range(8):
                        ins = e.matmul(ps[b2][:, :], lhsT=slots[1][:, k, c * P:(c + 1) * P], rhs=poolB[:, k, :], start=(k == 0), stop=(k == 7))
                    return ins
                S.op("pe", mmB, reads=["S1"] + [f"poolB{k}" for k in range(8)], writes=[f"ps{b2}"])
                S.op("dve", lambda e, c=c, b2=b2, m0=m0, thb=thb: e.scalar_tensor_tensor(out=mB[:, c, m0:m0 + TT], in0=thb[:], scalar=1.0, in1=ps[b2][:, :],
                                                                                       op0=ALU.add, op1=ALU.mult),
                     reads=[thk, f"ps{b2}"], writes=[f"mB{mt}_{c}"])

        flush()
        A0 = DV_ACC * 8

        def bcast_gate(gcol, gck, gdst, gkey):
            for half in range(2):
                b = nb()
                for cc in range(4):
                    c = half * 4 + cc
                    S.op("dve", lambda e, c=c, cc=cc: e.tensor_scalar(out=dgs[cc][:], in0=identf[:], scalar1=DV(gcol, c), scalar2=None, op0=ALU.mult),
                         reads=["identf", gck], writes=[f"dg{cc}", "junkb2"])
                for cc in range(4):
                    S.op("pe", lambda e, b=b, cc=cc: e.matmul(ps[b][:, cc * P:(cc + 1) * P], lhsT=ones[:], rhs=dgs[cc][:], start=True, stop=True),
                         reads=["ones", f"dg{cc}"], writes=[f"ps{b}"])
                S.op("dve", lambda e, b=b, half=half: e.tensor_copy(out=gdst[:, half * 512:(half + 1) * 512], in_=ps[b][:, :]),
                     reads=[f"ps{b}"], writes=[gkey])

        dgs = [dg] + [at(f"dg{i}", [P, P], F32, RS + 13824 + 2048 - 512 * (4 - i)) for i in range(1, 4)]
        S.op("dve", lambda e: e.tensor_tensor(out=DV(DV_T0), in0=dv[:, A0 + 16:A0 + 24], in1=V(V_BGT1), op=ALU.add),
             reads=["dvaccB", "vecs", "dvt0"], writes=["dvt0"])
        S.op("dve", lambda e: e.tensor_scalar(out=DV(DV_T0), in0=DV(DV_T0), scalar1=0.5, scalar2=None, op0=ALU.mult), reads=["dvt0"], writes=["dvt0"])
        S.op("dve", lambda e: e.tensor_tensor(out=DV(DV_SH2), in0=dv[:, A0 + 24:A0 + 32], in1=V(V_BSH2), op=ALU.add),
             reads=["dvaccB", "vecs"], writes=["dvsh2"])
        S.op("dve", lambda e: e.tensor_tensor(out=DV(DV_SC2), in0=dv[:, A0 + 32:A0 + 40], in1=V(V_BSC2), op=ALU.add),
             reads=["dvaccB", "vecs"], writes=["dvsc2"])
        S.op("dve", lambda e: e.scalar_tensor_tensor(out=DV(DV_GM2), in0=DV(DV_SC2), scalar=1.0, in1=V(V_G2), op0=ALU.add, op1=ALU.mult),
             reads=["dvsc2", "vecs"], writes=["dvgm2"])
        S.op("dve", lambda e: e.tensor_tensor(out=DV(DV_T1), in0=dv[:, A0 + 40:A0 + 48], in1=V(V_BGT2), op=ALU.add),
             reads=["dvaccB", "vecs", "dvt1"], writes=["dvt1"])
        bcast_gate(DV_T0, "dvt0", gtb1, "gtb1")
        bcast_gate(DV_T1, "dvt1", gtb2, "gtb2")
        S.barrier()


        tha = [at(f"tha{i}", [P, TT], F32, RX + 32768 + i * 2048) for i in range(2)]
        tt_ = [at(f"tt{i}", [P, TT], F32, RX + 36864 + i * 2048) for i in range(2)]
        stC = [(at("stC0", [P, D], F32, RX + 40960), "stC0"), (at("stC1", [P, D], F32, RX + 45056), "stC1")]
        pending.extend(make_stream(slots[0], "S0", w_o, stC, gtb1, "gtb1"))
        pending.extend(make_stream(slots[1], "S1", w_up[:, 0:D], stC))
        xt2 = at("xt2", [P, 4, D], F32, RX + 49152)

        def load_xt2(mt_):
            r0_ = NTOK + mt_ * TT
            S.dma("sp", lambda e: e.dma_start(out=xt2[:], in_=xin[r0_:r0_ + TT, :].rearrange("(s p) d -> p s d", p=P)), "xt2", writes=["xt2"])

        for mt in range(4):
            m0 = mt * TT
            hkeys = [f"hT{mt}_{c}" for c in range(8)]
            if mt == 3:
                load_xt2(0)
            for c in range(8):
                i2 = c % 2
                b = nb()

                def mmga(e, c=c, b=b, m0=m0):
                    ins = None
                    for k in range(8):
                        ins = e.matmul(ps[b][:, :], lhsT=slots[3][:, k, c * P:(c + 1) * P], rhs=hT_all[:, k, m0:m0 + TT], start=(k == 0), stop=(k == 7))
                    return ins
                S.op("pe", mmga, reads=["S3"] + hkeys, writes=[f"ps{b}"])
                S.op("act", lambda e, b=b, i2=i2: e.activation(out=tha[i2][:], in_=ps[b][:, :], func=AF.Tanh, scale=0.5), reads=[f"ps{b}"], writes=[f"tha{i2}"])
                b2 = nb()

                def mmA(e, c=c, b2=b2, m0=m0):
                    ins = None
                    for k in range(8):
                        ins = e.matmul(ps[b2][:, :], lhsT=slots[2][:, k, c * P:(c + 1) * P], rhs=prodA[:, k, m0:m0 + TT], start=(k == 0), stop=(k == 7))
                    return ins
                S.op("pe", mmA, reads=["S2"] + [f"prodA{mt}_{k}" for k in range(8)], writes=[f"ps{b2}"])
                S.op("dve", lambda e, i2=i2, b2=b2: e.scalar_tensor_tensor(out=tt_[i2][:], in0=tha[i2][:], scalar=1.0, in1=ps[b2][:, :], op0=ALU.add, op1=ALU.mult),
                     reads=[f"tha{i2}", f"ps{b2}"], writes=[f"tt{i2}"])
                S.op("pool", lambda e, c=c, i2=i2, m0=m0: e.tensor_tensor(out=mB[:, c, m0:m0 + TT], in0=tt_[i2][:], in1=mB[:, c, m0:m0 + TT], op=ALU.add),
                     reads=[f"tt{i2}", f"mB{mt}_{c}"], writes=[f"mB{mt}_{c}"])
                tick()
        flush()

        S.barrier()

        x1 = at("x1", [P, 16, D], F32, RHP)
        h2T = at("h2T", [P, 8, NTOK], BF16, RX + 0)
        stg = [at(f"stg{i}", [P, D], F32, RX + 32768 + i * 4096) for i in range(2)]
        xn2 = at("xn2", [P, 4, D], BF16, RX + 40960)
        STG = [(stg[0], "stg0"), (stg[1], "stg1")]

        pending += make_stream(slots[3], "S3", w_dn[0:D, :], STG, gtb2, "gtb2")
        pending += make_stream(slots[2], "S2", w_up[:, D:2 * D], STG)

        def norm2_part1(mt):
            for s in range(4):
                S.op("act", lambda e, s=s: e.activation(out=xn2[:, s, :], in_=x1[:, mt * 4 + s, :], func=AF.Square, accum_out=stat[:, 16 + s:17 + s]),
                     reads=[f"x1_{mt * 4 + s}"], writes=[f"xn2_{s}", f"ss2_{s}"])
            S.op("act", lambda e: e.activation(out=stat[:, 20:24], in_=stat[:, 16:20], func=AF.Sqrt, scale=1.0 / D, bias=misc[:, 0:1]),
                 reads=["ss2_0", "ss2_1", "ss2_2", "ss2_3", "misc"], writes=["std2"])
            S.op("dve", lambda e: e.reciprocal(out=stat[:, 24:28], in_=stat[:, 20:24]), reads=["std2"], writes=["rstd2"])
            for s in range(4):
                S.op("act", lambda e, s=s: e.activation(out=xn2[:, s, :], in_=x1[:, mt * 4 + s, :], func=AF.Identity, scale=stat[:, 24 + s:25 + s]),
                     reads=[f"x1_{mt * 4 + s}", "rstd2"], writes=[f"xn2_{s}"])

        def norm2_part2(mt):
            m0 = mt * TT
            for c in range(8):
                b = nb()

                def tr2(e, c=c, b=b):
                    ins = None
                    for s in range(4):
                        ins = e.transpose(out=psb[b][:, s * P:(s + 1) * P], in_=xn2[:, s, c * P:(c + 1) * P], identity=ident[:])
                    return ins
                S.op("pe", tr2, reads=["xn2_0", "xn2_1", "xn2_2", "xn2_3", "ident"], writes=[f"ps{b}"])
                S.op("act", lambda e, c=c, b=b: e.activation(out=h2T[:, c, m0:m0 + TT], in_=psb[b][:, 0:TT], func=AF.Identity, scale=DV(DV_GM2, c), bias=DV(DV_SH2, c)),
                     reads=[f"ps{b}", "dvgm2", "dvsh2"], writes=[f"h2T{mt}_{c}", f"mB{mt}_{c}"])

        for mt in range(4):
            m0 = mt * TT
            r0 = NTOK + m0
            if mt > 0:
                load_xt2(mt)
            for s in range(4):
                for half in range(2):
                    b = nb()

                    def mmo(e, s=s, half=half, b=b, m0=m0):
                        ins = None
                        for k in range(8):
                            ins = e.matmul(ps[b][:, :], lhsT=mB[:, k, m0 + s * P:m0 + (s + 1) * P], rhs=slots[0][:, k, half * 512:(half + 1) * 512],
                                           start=(k == 0), stop=(k == 7))
                        return ins
                    S.op("pe", mmo, reads=["S0"] + [f"mB{mt}_{k}" for k in range(8)], writes=[f"ps{b}"])
                    S.op("dve", lambda e, s=s, half=half, b=b, mt=mt: e.tensor_tensor(
                        out=x1[:, mt * 4 + s, half * 512:(half + 1) * 512], in0=ps[b][:, :], in1=xt2[:, s, half * 512:(half + 1) * 512], op=ALU.add),
                        reads=[f"ps{b}", "xt2"], writes=[f"x1_{mt * 4 + s}"])
                    tick()
            if mt > 0:
                norm2_part2(mt - 1)
            norm2_part1(mt)
        norm2_part2(3)
        flush()

        S.barrier()

        ff = [at(f"ff{i}", [P, 8, TT], BF16, RX + 32768 + i * 8192) for i in range(2)]
        rr = [at(f"rr{i}", [P, TT], F32, RX + 49152 + i * 2048) for i in range(2)]
        stgE = [at("stgE0", [P, D], F32, RX + 53248), at("stgE1", [P, D], F32, RX + 57344)]
        junk2 = at("junk2", [P, D], BF16, RX + 61440)
        STGE = [(stgE[0], "stgE0"), (stgE[1], "stgE1")]
        up_slot = {0: 1, 1: 2, 2: 1, 3: 2}
        dn_slot = {0: 3, 1: 0, 2: 3, 3: 0}

        def queue_quarter(q, up=True):
            if up:
                pending.extend(make_stream(slots[up_slot[q]], f"S{up_slot[q]}", w_up[:, q * D:(q + 1) * D], [(gtb1, "gtb1")]))
            pending.extend(make_stream(slots[dn_slot[q]], f"S{dn_slot[q]}", w_dn[q * D:(q + 1) * D, :], STGE, gtb2, "gtb2"))

        fgb = stgE[0]

        def final_norm_tile(mt):
            for s4 in range(4):
                s16 = mt * 4 + s4
                S.op("act", lambda e, s16=s16, s4=s4: e.activation(out=junk2[:], in_=x1[:, s16, :], func=AF.Square, accum_out=stat[:, 32 + s4:33 + s4]),
                     reads=[f"x1_{s16}"], writes=["junk2", f"ss3_{s4}"])
            S.op("act", lambda e: e.activation(out=stat[:, 40:44], in_=stat[:, 32:36], func=AF.Sqrt, scale=1.0 / D, bias=misc[:, 0:1]),
                 reads=["ss3_0", "ss3_1", "ss3_2", "ss3_3", "misc"], writes=["std3"])
            S.op("dve", lambda e: e.reciprocal(out=stat[:, 44:48], in_=stat[:, 40:44]), reads=["std3"], writes=["rstd3"])
            for s4 in range(4):
                s16 = mt * 4 + s4
                S.op("dve", lambda e, s16=s16, s4=s4: e.scalar_tensor_tensor(out=x1[:, s16, :], in0=x1[:, s16, :], scalar=stat[:, 44 + s4:45 + s4], in1=fgb[:],
                                                                             op0=ALU.mult, op1=ALU.mult),
                     reads=[f"x1_{s16}", "rstd3", "fgb"], writes=[f"x1_{s16}"])
                S.dma("sp", lambda e, s16=s16: e.dma_start(out=yout[s16 * P:(s16 + 1) * P, :], in_=x1[:, s16, :]), "yo", reads=[f"x1_{s16}"], writes=[f"y_{s16}"])

        def final_norm_sub(mt, s4):
            s16 = mt * 4 + s4
            S.op("act", lambda e: e.activation(out=junk2[:], in_=x1[:, s16, :], func=AF.Square, accum_out=stat[:, 32 + s4:33 + s4]),
                 reads=[f"x1_{s16}"], writes=["junk2", f"ss3_{s4}"])
            S.op("act", lambda e: e.activation(out=stat[:, 40 + s4:41 + s4], in_=stat[:, 32 + s4:33 + s4], func=AF.Sqrt, scale=1.0 / D, bias=misc[:, 0:1]),
                 reads=[f"ss3_{s4}", "misc"], writes=[f"std3_{s4}", "std3"])
            S.op("dve", lambda e: e.reciprocal(out=stat[:, 44 + s4:45 + s4], in_=stat[:, 40 + s4:41 + s4]), reads=[f"std3_{s4}"], writes=[f"rstd3_{s4}", "rstd3"])
            S.op("dve", lambda e: e.scalar_tensor_tensor(out=x1[:, s16, :], in0=x1[:, s16, :], scalar=stat[:, 44 + s4:45 + s4], in1=fgb[:],
                                                         op0=ALU.mult, op1=ALU.mult),
                 reads=[f"x1_{s16}", f"rstd3_{s4}", "fgb"], writes=[f"x1_{s16}"])
            S.dma("sp", lambda e: e.dma_start(out=yout[s16 * P:(s16 + 1) * P, :], in_=x1[:, s16, :]), "yo", reads=[f"x1_{s16}"], writes=[f"y_{s16}"])

        queue_quarter(1, up=False)
        for q in range(4):
            if 1 <= q and q + 1 < 4:
                queue_quarter(q + 1)
            if q == 3:
                S.dma("sp", lambda e: e.dma_start(out=fgb[:], in_=rows_d[:, 1, :]), "st_stgE0", reads=[], writes=["stgE0", "fgb"])
            us_, ds_ = up_slot[q], dn_slot[q]

            def ffn_up(mt, us_=us_):
                m0 = mt * TT
                fb = ff[mt % 2]
                fk = f"ff{mt % 2}"
                for j in range(8):
                    b = nb()
                    i2 = j % 2

                    def mmup(e, j=j, b=b, m0=m0, us_=us_):
                        ins = None
                        for k in range(8):
                            ins = e.matmul(ps[b][:, :], lhsT=slots[us_][:, k, j * P:(j + 1) * P], rhs=h2T[:, k, m0:m0 + TT], start=(k == 0), stop=(k == 7))
                        return ins
                    S.op("pe", mmup, reads=[f"S{us_}"] + [f"h2T{mt}_{k}" for k in range(8)], writes=[f"ps{b}"])
                    S.op("act", lambda e, b=b, i2=i2: e.activation(out=rr[i2][:], in_=ps[b][:, :], func=AF.Relu), reads=[f"ps{b}"], writes=[f"rr{i2}"])
                    S.op("act", lambda e, j=j, i2=i2, fb=fb: e.activation(out=fb[:, j, :], in_=rr[i2][:], func=AF.Square), reads=[f"rr{i2}"], writes=[f"{fk}_{j}"])
                    if j % 2 == 1:
                        tick()

            def ffn_dn(mt, ds_=ds_, q=q):
                fb = ff[mt % 2]
                fk = f"ff{mt % 2}"
                for s in range(4):
                    for half in range(2):
                        b = nb()

                        def mmdn(e, s=s, half=half, b=b, fb=fb, ds_=ds_):
                            ins = None
                            for j in range(8):
                                ins = e.matmul(ps[b][:, :], lhsT=fb[:, j, s * P:(s + 1) * P], rhs=slots[ds_][:, j, half * 512:(half + 1) * 512],
                                               start=(j == 0), stop=(j == 7))
                            return ins
                        S.op("pe", mmdn, reads=[f"S{ds_}"] + [f"{fk}_{j}" for j in range(8)], writes=[f"ps{b}"])
                        S.op("dve", lambda e, s=s, half=half, b=b, mt=mt: e.tensor_tensor(
                            out=x1[:, mt * 4 + s, half * 512:(half + 1) * 512], in0=ps[b][:, :], in1=x1[:, mt * 4 + s, half * 512:(half + 1) * 512], op=ALU.add),
                            reads=[f"ps{b}", f"x1_{mt * 4 + s}"], writes=[f"x1_{mt * 4 + s}"])
                        if half == 1 and s % 2 == 1:
                            tick()
                        if q == 3 and mt == 3 and half == 1:
                            final_norm_sub(mt, s)
                if q == 3 and mt < 3:
                    final_norm_tile(mt)

            for kind, mt in (("up", 0), ("up", 1), ("dn", 0), ("up", 2), ("dn", 1), ("up", 3), ("dn", 2), ("dn", 3)):
                if kind == "up":
                    ffn_up(mt)
                else:
                    ffn_dn(mt)
            flush()

        S.barrier()
        S.emit(nc, st)
    return nc


def _cols(v):
    return np.ascontiguousarray(np.asarray(v, dtype=np.float32).reshape(8, P).T)


_NC_CACHE = {}


def kernel(x, c, norm_mix_g, norm_mlp_g, w_ada, b_ada, w_in, conv_w, conv_b, w_rg_a, b_rg_a, w_rg_x, b_rg_x,
           a_param, w_branch_a, w_pool, b_pool, pool_scale, w_branch_b, w_out, w_up, w_down, final_g):
    f32 = np.float32
    x = np.asarray(x, f32)
    c = np.asarray(c, f32)
    w_ada0 = np.asarray(w_ada, f32)[0]
    b_ada0 = np.asarray(b_ada, f32)[0]
    B, SEQ, _ = x.shape
    half = SEQ // 2
    w_adaT = np.ascontiguousarray(w_ada0.T)
    shared = {
        "ident": np.eye(P, dtype=f32),
        "w_in": np.ascontiguousarray(np.asarray(w_in, f32)[0]),
        "w_ra": np.ascontiguousarray(np.asarray(w_rg_a, f32)[0].reshape(D, 256)),
        "w_rx": np.ascontiguousarray(np.asarray(w_rg_x, f32)[0].reshape(D, 256)),
        "w_pl": np.ascontiguousarray(np.asarray(w_pool, f32)[0].reshape(D, 256)),
        "w_a": np.ascontiguousarray(np.asarray(w_branch_a, f32)[0]),
        "w_b": np.ascontiguousarray(np.asarray(w_branch_b, f32)[0]),
        "w_o": np.ascontiguousarray(np.asarray(w_out, f32)[0]),
        "w_up": np.ascontiguousarray(np.asarray(w_up, f32)[0]),
        "w_dn": np.ascontiguousarray(np.asarray(w_down, f32)[0]),
        "w_adaT": w_adaT,
    }
    cw = np.asarray(conv_w, f32)[0]
    in_maps = []
    for r in range(8):
        b, hf = r // 2, r % 2
        if hf == 0:
            xin = np.concatenate([np.zeros((half, D), f32), x[b, 0:half]], axis=0)
        else:
            xin = x[b]
        flag = np.zeros((P, 8), f32)
        flag[:, 0] = float(hf)
        flag[:, 1] = 1.0 - float(hf)
        flag[:, 2] = 0.5 * (1.0 - float(hf))
        vec_list = [_cols(np.asarray(norm_mix_g, f32)[0]), _cols(np.asarray(norm_mlp_g, f32)[0]),
                    _cols(b_ada0[0:D]), _cols(b_ada0[D:2 * D]), _cols(b_ada0[3 * D:4 * D]), _cols(b_ada0[4 * D:5 * D]),
                    _cols(cw[0]), _cols(cw[1]), _cols(cw[2]), _cols(cw[3]), _cols(np.asarray(conv_b, f32)[0]),
                    _cols(np.asarray(b_rg_a, f32)[0]), _cols(np.asarray(b_rg_x, f32)[0]), _cols(np.asarray(a_param, f32)[0]),
                    _cols(np.asarray(b_pool, f32)[0]), _cols(np.asarray(pool_scale, f32)[0]), _cols(c[b]), flag,
                    _cols(b_ada0[2 * D:3 * D]), _cols(b_ada0[5 * D:6 * D])]
        vecs = np.ascontiguousarray(np.concatenate(vec_list, axis=1))
        rows = np.empty((P, 2, D), f32)
        rows[:, 0, :] = c[b][None, :]
        rows[:, 1, :] = np.asarray(final_g, f32)[None, :]
        invc = np.empty((P, 4, 16), f32)
        for gi, win in enumerate((2, 4, 8, 16)):
            if hf == 0:
                cntv = np.minimum(np.arange(16) + 1, win).astype(f32)
            else:
                cntv = np.full(16, win, f32)
            invc[:, gi, :] = (1.0 / cntv)[None, :]
        m = {"xin": np.ascontiguousarray(xin), "vecs": vecs, "rows": rows, "invc": invc}
        m.update(shared)
        in_maps.append(m)
    if "nc" not in _NC_CACHE:
        _NC_CACHE["nc"] = build_nc()
    nc = _NC_CACHE["nc"]
    res = run_bass_kernel_spmd(nc, in_maps, core_ids=list(range(8)))
    out = np.empty((B, SEQ, D), f32)
    for r in range(8):
        b, hf = r // 2, r % 2
        out[b, hf * half:(hf + 1) * half] = res.results[r]["yout"]
    return out
```
